# Optimizing a Trainium2 kernel written in Bass

```python
import math
import jax, jax.numpy as jnp
from jax import lax
import numpy as np


D_MODEL = 1024
BATCH = 2
SEQ = 8192
DEPTH = 2

N_META = 16
MIX_WIDTH = D_MODEL
DIFF_WIDTH = MIX_WIDTH // 2
FOX_WIDTH = MIX_WIDTH - DIFF_WIDTH
DIFF_QK_DIM = 64
DIFF_V_DIM = 2 * DIFF_QK_DIM
DIFF_HEADS = DIFF_WIDTH // DIFF_V_DIM
FOX_HEAD_DIM = 64
FOX_HEADS = FOX_WIDTH // FOX_HEAD_DIM
ROPE_THETA = 10000.0
Q_BLOCK = 128
NORM_EPS = 1e-6
SPLITS = (DIFF_HEADS * 2 * DIFF_QK_DIM, DIFF_HEADS * 2 * DIFF_QK_DIM, DIFF_WIDTH, DIFF_WIDTH,
          FOX_WIDTH, FOX_WIDTH, FOX_WIDTH, FOX_WIDTH, FOX_HEADS)
PROJ_WIDTH = sum(SPLITS)

kernel_name = 'hymba_diff_fox_hybrid'


def rms_norm(x, g):
    xf = x.astype(jnp.float32)
    y = xf * lax.rsqrt(jnp.mean(xf * xf, axis=-1, keepdims=True) + NORM_EPS)
    return (y * g.astype(jnp.float32)).astype(x.dtype)


def rope(t, pos):
    half = t.shape[-1] // 2
    inv = ROPE_THETA ** (-jnp.arange(half, dtype=jnp.float32) / half)
    ang = pos.astype(jnp.float32)[:, None] * inv[None, :]
    cos, sin = jnp.cos(ang), jnp.sin(ang)
    tf = t.astype(jnp.float32)
    t1, t2 = tf[..., :half], tf[..., half:]
    return jnp.concatenate([t1 * cos - t2 * sin, t2 * cos + t1 * sin], axis=-1).astype(t.dtype)


def sweep_query_blocks(block_fn, q_parts, pos):
    meta_out = block_fn(tuple(q[:, :, :N_META] for q in q_parts), pos[:N_META])

    def to_blocks(q):
        b, h, _, d = q.shape
        return jnp.moveaxis(q[:, :, N_META:].reshape(b, h, -1, Q_BLOCK, d), 2, 0)

    blocks = tuple(to_blocks(q) for q in q_parts)
    pos_blocks = pos[N_META:].reshape(-1, Q_BLOCK)
    outs = lax.map(lambda a: block_fn(a[0], a[1]), (blocks, pos_blocks))
    nb, b, h, t, dv = outs.shape
    real = jnp.moveaxis(outs, 0, 2).reshape(b, h, nb * t, dv)
    return jnp.concatenate([meta_out, real], axis=2)


def hybrid_layer(h, pos, norm_g, w_in, b_forget, lam_q1, lam_k1, lam_q2, lam_k2,
                 subln_g, w_out, lambda_init):
    bsz, length, _ = h.shape
    u = rms_norm(h, norm_g)
    proj = jnp.einsum('bld,dp->blp', u, w_in)
    split_points = np.cumsum(SPLITS)[:-1].tolist()
    dq, dk, dv, dz, fq, fk, fv, fz, f_logit = jnp.split(proj, split_points, axis=-1)

    def diff_qk(t):
        t = t.reshape(bsz, length, DIFF_HEADS, 2, DIFF_QK_DIM).transpose(3, 0, 2, 1, 4)
        return rope(t[0], pos), rope(t[1], pos)

    q1, q2 = diff_qk(dq)
    k1, k2 = diff_qk(dk)
    v_a = dv.reshape(bsz, length, DIFF_HEADS, DIFF_V_DIM).transpose(0, 2, 1, 3)
    lam = (jnp.exp(jnp.sum(lam_q1.astype(jnp.float32) * lam_k1.astype(jnp.float32)))
           - jnp.exp(jnp.sum(lam_q2.astype(jnp.float32) * lam_k2.astype(jnp.float32)))
           + lambda_init)
    d_scale = DIFF_QK_DIM ** -0.5

    def diff_block(qs, qpos):
        qa, qc = qs
        mask = pos[None, :] <= qpos[:, None]

        def probs(q, k):
            s = jnp.einsum('bhqd,bhkd->bhqk', q, k).astype(jnp.float32) * d_scale
            return jax.nn.softmax(jnp.where(mask, s, -jnp.inf), axis=-1)

        p = probs(qa, k1) - lam * probs(qc, k2)
        return jnp.einsum('bhqk,bhkd->bhqd', p.astype(v_a.dtype), v_a)

    o_a = sweep_query_blocks(diff_block, (q1, q2), pos)
    o_a = rms_norm(o_a, subln_g.reshape(DIFF_HEADS, 1, DIFF_V_DIM)) * (1.0 - lambda_init)
    o_a = o_a.transpose(0, 2, 1, 3).reshape(bsz, length, DIFF_WIDTH)

    def fox_heads(t):
        return t.reshape(bsz, length, FOX_HEADS, FOX_HEAD_DIM).transpose(0, 2, 1, 3)

    q_f, k_f, v_f = fox_heads(fq), fox_heads(fk), fox_heads(fv)
    log_f = jax.nn.log_sigmoid(f_logit.astype(jnp.float32) + b_forget.astype(jnp.float32))
    cum = jnp.cumsum(log_f, axis=1).transpose(0, 2, 1)
    f_scale = FOX_HEAD_DIM ** -0.5

    def fox_block(qs, qpos):
        q, c_q = qs
        mask = pos[None, :] <= qpos[:, None]
        s = (jnp.einsum('bhqd,bhkd->bhqk', q, k_f).astype(jnp.float32) * f_scale
             + c_q - cum[:, :, None, :])
        p = jax.nn.softmax(jnp.where(mask, s, -jnp.inf), axis=-1)
        return jnp.einsum('bhqk,bhkd->bhqd', p.astype(v_f.dtype), v_f)

    o_b = sweep_query_blocks(fox_block, (q_f, cum[..., None]), pos)
    o_b = o_b.transpose(0, 2, 1, 3).reshape(bsz, length, FOX_WIDTH)

    mixed = jnp.concatenate([o_a * jax.nn.silu(dz), o_b * jax.nn.silu(fz)], axis=-1)
    return h + jnp.einsum('blm,md->bld', mixed, w_out)


def setup_inputs(seed: int = 0) -> dict:
    key = jax.random.key(seed)
    ks = jax.random.split(key, 14)
    f32 = jnp.float32
    x = jax.random.normal(ks[0], (BATCH, SEQ, D_MODEL), f32)
    meta_tokens = jax.random.normal(ks[1], (N_META, D_MODEL), f32)
    norm_g = 1.0 + 0.02 * jax.random.normal(ks[2], (DEPTH, D_MODEL), f32)
    w_in = jax.random.normal(ks[3], (DEPTH, D_MODEL, PROJ_WIDTH), f32) * D_MODEL ** -0.5
    b_forget = (jnp.linspace(1.0, 6.0, FOX_HEADS, dtype=f32)[None, :]
                + 0.1 * jax.random.normal(ks[4], (DEPTH, FOX_HEADS), f32))
    lam_q1 = 0.1 * jax.random.normal(ks[5], (DEPTH, DIFF_QK_DIM), f32)
    lam_k1 = 0.1 * jax.random.normal(ks[6], (DEPTH, DIFF_QK_DIM), f32)
    lam_q2 = 0.1 * jax.random.normal(ks[7], (DEPTH, DIFF_QK_DIM), f32)
    lam_k2 = 0.1 * jax.random.normal(ks[8], (DEPTH, DIFF_QK_DIM), f32)
    subln_g = 1.0 + 0.02 * jax.random.normal(ks[9], (DEPTH, DIFF_WIDTH), f32)
    w_out = jax.random.normal(ks[10], (DEPTH, MIX_WIDTH, D_MODEL), f32) * MIX_WIDTH ** -0.5
    final_g = 1.0 + 0.02 * jax.random.normal(ks[11], (D_MODEL,), f32)
    return {'x': x, 'meta_tokens': meta_tokens, 'norm_g': norm_g, 'w_in': w_in,
            'b_forget': b_forget, 'lam_q1': lam_q1, 'lam_k1': lam_k1, 'lam_q2': lam_q2,
            'lam_k2': lam_k2, 'subln_g': subln_g, 'w_out': w_out, 'final_g': final_g}


def reference(x, meta_tokens, norm_g, w_in, b_forget, lam_q1, lam_k1, lam_q2, lam_k2,
              subln_g, w_out, final_g):
    bsz, _, dm = x.shape
    meta = jnp.broadcast_to(meta_tokens.astype(x.dtype)[None], (bsz, N_META, dm))
    h = jnp.concatenate([meta, x], axis=1)
    pos = jnp.arange(h.shape[1], dtype=jnp.int32)
    for layer in range(DEPTH):
        lambda_init = 0.8 - 0.6 * math.exp(-0.3 * layer)
        h = hybrid_layer(h, pos, norm_g[layer], w_in[layer], b_forget[layer],
                         lam_q1[layer], lam_k1[layer], lam_q2[layer], lam_k2[layer],
                         subln_g[layer], w_out[layer], lambda_init)
    return rms_norm(h, final_g)[:, N_META:]
```

```python
import math
from contextlib import ExitStack

import numpy as np
import ml_dtypes

import concourse.bass as bass
import concourse.mybir as mybir
from concourse.bass_utils import run_bass_kernel_spmd

F32 = mybir.dt.float32
BF16 = mybir.dt.bfloat16
AF = mybir.ActivationFunctionType
ALU = mybir.AluOpType

D = 1024
SEQ = 8192
NMETA = 16
L = SEQ + NMETA
NCH = 17
NT = 65
NCOL = 1282
EPS = 1e-6
NPB = np.dtype(ml_dtypes.bfloat16)

C_DQ, C_DQS, C_DK, C_DKS = 0, 128, 256, 384
C_V = 512
C_DZ, C_FZ = 768, 896
C_FQA, C_FQB, C_FKA, C_FKB = 1024, 1088, 1152, 1216
C_FLA, C_FLB = 1280, 1281


def tile_pos(t):
    return (0, 16) if t == 0 else (16 + 128 * (t - 1), 128)


def chunk_tiles(c):
    return [0] if c == 0 else list(range(4 * (c - 1) + 1, 4 * c + 1))


def chunk_pos(c):
    return (0, 16) if c == 0 else (16 + 512 * (c - 1), 512)


class Buf:
    def __init__(self, name):
        self.name = name
        self.w = None
        self.r = {}


class Sched:
    def __init__(self, nc, es):
        self.nc = nc
        self.es = es
        self.eng = {"pe": nc.tensor, "act": nc.scalar, "dve": nc.vector, "pool": nc.gpsimd, "sp": nc.sync}
        self.sem = {k: es.enter_context(nc.semaphore("s_" + k)) for k in ("pe", "act", "dve", "pool")}
        self.cnt = {k: 0 for k in self.sem}
        self.seen = {k: {} for k in self.eng}
        self.dsems = []
        self.nsem = 0

    def _deps(self, e, reads, writes):
        deps = {}

        def add(tok, raw=False):
            if tok is None:
                return
            sem, val, owner = tok
            if owner == e and not (raw and e != "pe"):
                return
            if deps.get(sem, 0) < val:
                deps[sem] = val

        for b in reads:
            add(b.w, raw=True)
        for b in writes:
            add(b.w)
            for t in b.r.values():
                add(t)
        for sem, val in deps.items():
            if self.seen[e].get(sem, 0) < val:
                self.eng[e].wait_ge(sem, val)
                self.seen[e][sem] = val

    def op(self, e, fn, reads=(), writes=()):
        self._deps(e, reads, writes)
        ins = fn(self.eng[e])
        self.cnt[e] += 1
        ins.then_inc(self.sem[e], 1)
        tok = (self.sem[e], self.cnt[e], e)
        for b in reads:
            b.r[e] = tok
        for b in writes:
            b.w = tok
            b.r = {}

    def new_slot(self, name):
        sem = self.es.enter_context(self.nc.semaphore("d_" + name))
        slot = {"sem": sem, "total": 0, "name": name}
        self.dsems.append(slot)
        return slot

    def dma(self, q, slot, out, in_, reads=(), writes=(), same_batch=False):
        self._deps(q, reads, writes)
        if not same_batch and slot["total"] > 0:
            if self.seen[q].get(slot["sem"], 0) < slot["total"]:
                self.eng[q].wait_ge(slot["sem"], slot["total"])
                self.seen[q][slot["sem"]] = slot["total"]
        ins = self.eng[q].dma_start(out=out, in_=in_)
        slot["total"] += 16
        ins.then_inc(slot["sem"], 16)
        tok = (slot["sem"], slot["total"], "dma:" + slot["name"])
        if not same_batch:
            slot["bw"], slot["br"] = [], []
        for b in writes:
            b.r = {}
        slot.setdefault("bw", []).extend(writes)
        slot.setdefault("br", []).extend(reads)
        for b in slot["br"]:
            b.r["dma:" + slot["name"]] = tok
        for b in slot["bw"]:
            b.w = tok

    def barrier(self):
        for e in self.eng:
            for k in self.sem:
                if k != e and self.cnt[k] > 0 and self.seen[e].get(self.sem[k], 0) < self.cnt[k]:
                    self.eng[e].wait_ge(self.sem[k], self.cnt[k])
                    self.seen[e][self.sem[k]] = self.cnt[k]
            for slot in self.dsems:
                if slot["total"] > 0 and self.seen[e].get(slot["sem"], 0) < slot["total"]:
                    self.eng[e].wait_ge(slot["sem"], slot["total"])
                    self.seen[e][slot["sem"]] = slot["total"]

    def finish(self, q="sp"):
        for slot in self.dsems:
            if slot["total"] > 0:
                self.eng[q].wait_ge(slot["sem"], slot["total"])


def _sb(nc, es, name, shape, dt):
    return es.enter_context(nc.sbuf_tensor(name, shape, dt))


def _ps(nc, es, name, shape, dt):
    return es.enter_context(nc.psum_tensor(name, shape, dt))


def build_layer(has_prev, lambda_init, nchunks=NCH, dbg=False, fused=False):
    nc = bass.Bass("TRN2", target_bir_lowering=False)
    hin = nc.dram_tensor("hin", [L, D], F32, kind="ExternalInput").ap()
    cosT = nc.dram_tensor("cosT", [128, L], F32, kind="ExternalInput").ap()
    sinT = nc.dram_tensor("sinT", [128, L], F32, kind="ExternalInput").ap()
    identD = nc.dram_tensor("ident", [128, 128], BF16, kind="ExternalInput").ap()
    triD = nc.dram_tensor("tri", [128, 128], BF16, kind="ExternalInput").ap()
    cfgs = []
    if fused:
        mixall = [nc.dram_tensor("mixall%d" % l, [D, L], BF16).ap() for l in range(2)]
        h1scr = nc.dram_tensor("h1scr", [L, D], F32).ap()
        utscr = nc.dram_tensor("utscr", [128, 8, L], BF16).ap()
        woutp0 = nc.dram_tensor("woutp0", [D, D], F32, kind="ExternalInput").ap()
        woutp1 = nc.dram_tensor("woutp1", [D, D], F32, kind="ExternalInput").ap()
        fg = nc.dram_tensor("fg", [1, D], F32, kind="ExternalInput").ap()
        outD = nc.dram_tensor("out", [SEQ, D], F32, kind="ExternalOutput").ap()
        for l in range(2):
            winL = nc.dram_tensor("win%d" % l, [4 * D, NCOL], F32, kind="ExternalInput").ap()
            normgL = nc.dram_tensor("normg%d" % l, [128, 8], F32, kind="ExternalInput").ap()
            bfgL = nc.dram_tensor("bfg%d" % l, [4, 2], F32, kind="ExternalInput").ap()
            lam4L = nc.dram_tensor("lam4%d" % l, [1, 256], F32, kind="ExternalInput").ap()
            sublngL = nc.dram_tensor("sublng%d" % l, [4 * 128, 1], F32, kind="ExternalInput").ap()
            for g in range(4):
                cfg = {"hin": hin, "win": winL[g * D:(g + 1) * D, :], "normg": normgL, "bfg": bfgL[g:g + 1, :],
                       "lam4": lam4L, "sublng": sublngL[g * 128:(g + 1) * 128, :],
                       "has_prev": l > 0, "li": 0.8 - 0.6 * math.exp(-0.3 * l),
                       "mixo": mixall[l][g * 256:(g + 1) * 256, :], "layer": l, "g": g}
                cfg["utscr"] = utscr
                cfg["ut_store"] = (g == 0)
                cfg["ut_load"] = (g > 0)
                if l > 0 and g == 0:
                    cfg["mixp"] = mixall[0]
                    cfg["woutp"] = woutp0
                    cfg["h1out"] = h1scr
                elif l > 0:
                    cfg["hin"] = h1scr
                    cfg["has_prev"] = False
                cfgs.append(cfg)
    else:
        cfg = {
            "hin": hin,
            "win": nc.dram_tensor("win", [D, NCOL], F32, kind="ExternalInput").ap(),
            "normg": nc.dram_tensor("normg", [128, 8], F32, kind="ExternalInput").ap(),
            "bfg": nc.dram_tensor("bfg", [1, 2], F32, kind="ExternalInput").ap(),
            "lam4": nc.dram_tensor("lam4", [1, 256], F32, kind="ExternalInput").ap(),
            "sublng": nc.dram_tensor("sublng", [128, 1], F32, kind="ExternalInput").ap(),
            "has_prev": has_prev, "li": lambda_init, "layer": 0, "g": 0,
        }
        if has_prev:
            cfg["mixp"] = nc.dram_tensor("mixp", [D, L], BF16, kind="ExternalInput").ap()
            cfg["woutp"] = nc.dram_tensor("woutp", [D, D], F32, kind="ExternalInput").ap()
        cfg["mixo"] = nc.dram_tensor("mixo", [256, L], BF16, kind="ExternalOutput").ap()
        cfgs.append(cfg)
    has_prev = has_prev or fused
    cur = {}

    with ExitStack() as es:
        S = Sched(nc, es)
        es2 = es.enter_context(ExitStack())
        sb = lambda name, shape, dt: _sb(nc, es2, name, shape, dt)

        W = sb("W", [128, 8, NCOL], BF16)
        bW = Buf("W")
        if has_prev:
            WO = sb("WO", [128, 8, D], BF16)
            bWO = Buf("WO")
        KdT = sb("KdT", [128, L], BF16)
        KA = sb("KA", [70, L], BF16)
        KB = sb("KB", [70, L], BF16)
        Vd = sb("Vd", [128, NT, 128], BF16)
        Vf = sb("Vf", [128, NT, 192], BF16)
        bKd = [Buf(f"Kd{c}") for c in range(NCH)]
        bKA = [Buf(f"KA{c}") for c in range(NCH)]
        bKB = [Buf(f"KB{c}") for c in range(NCH)]
        bVd = [Buf(f"Vd{c}") for c in range(NCH)]
        bVf = [Buf(f"Vf{c}") for c in range(NCH)]

        qsets = []
        for qi in range(2):
            qsets.append((sb(f"QdT{qi}", [128, 512], BF16), Buf(f"Qd{qi}"),
                          sb(f"QA{qi}", [70, 512], BF16), Buf(f"QA{qi}"),
                          sb(f"QB{qi}", [70, 512], BF16), Buf(f"QB{qi}"),
                          sb(f"ZdT{qi}", [128, 512], BF16), Buf(f"Zd{qi}"),
                          sb(f"ZfT{qi}", [128, 512], BF16), Buf(f"Zf{qi}")))

        xt = [sb(f"xt{i}", [128, D], F32) for i in range(2)]
        bxt = [Buf(f"xt{i}") for i in range(2)]
        sxt = [S.new_slot(f"xt{i}") for i in range(2)]
        sh1 = [S.new_slot(f"h1o{i}") for i in range(2)]
        sutl = S.new_slot("utl")
        suts = S.new_slot("uts")
        ub = [sb(f"ub{i}", [128, D], BF16) for i in range(2)]
        bub = [Buf(f"ub{i}") for i in range(2)]
        uT = sb("uT", [128, 8, 512], BF16); buT = Buf("uT")
        stat = sb("stat", [128, 8], F32); bstat = Buf("stat")
        if has_prev:
            mp = [sb(f"mp{i}", [128, 8, 512], BF16) for i in range(1)]
            bmp = [Buf(f"mp{i}") for i in range(1)]
            smp = [S.new_slot(f"mp{i}") for i in range(1)]
        rope = [sb(f"rope{i}", [128, 2, 512], F32) for i in range(1)]
        brope = [Buf(f"rope{i}") for i in range(1)]
        srope = [S.new_slot(f"rope{i}") for i in range(1)]

        ident = sb("identS", [128, 128], BF16)
        tri = sb("triS", [128, 128], BF16)
        onesb = sb("onesb", [128, 128], BF16)
        onesf = sb("onesf", [128, 128], F32)
        gT = sb("gT", [128, 8], F32)
        cst = sb("cst", [128, 16], F32)
        lamv = sb("lamv", [128, 256], F32)
        bconst = Buf("const")
        sconst = S.new_slot("const")
        bf2 = sb("bf2", [2, 1], F32)

        rows = sb("rows", [2, 2, 512], F32); brows = Buf("rows")
        stg = sb("stg", [2, 6, 512], BF16); bstg = Buf("stg")
        onesrow = sb("onesrow", [2, 512], F32)
        carry = sb("carry", [2, 1], F32)
        saug = S.new_slot("aug")

        NP = 6
        Pb = [sb(f"P{i}", [128, 512], BF16) for i in range(NP)]
        bP = [Buf(f"P{i}") for i in range(NP)]
        ep = [sb(f"ep{i}", [128, 512], F32) for i in range(6)]
        bep = [Buf(f"ep{i}") for i in range(6)]
        t1, bt1, t2, bt2 = ep[4], bep[4], ep[5], bep[5]
        mixd = [sb(f"mixd{i}", [128, 512], BF16) for i in range(2)]
        bmixd = [Buf(f"mixd{i}") for i in range(2)]
        smixd = [S.new_slot(f"mixd{i}") for i in range(2)]
        mixf = [sb(f"mixf{i}", [128, 512], BF16) for i in range(2)]
        bmixf = [Buf(f"mixf{i}") for i in range(2)]
        smixf = [S.new_slot(f"mixf{i}") for i in range(2)]

        bank = [_ps(nc, es, f"bank{i}", [128, 512], F32) for i in range(8)]
        bbank = [Buf(f"bank{i}") for i in range(8)]

        acc = [sb(f"acc{i}", [128, 512], F32) for i in range(2)]
        bacc = [Buf(f"acc{i}") for i in range(2)]
        ones32 = sb("ones32", [128, 128], F32)

        def weights_gen(cfg):
            for fc in range(8):
                S.dma("sp", sxt[0], xt[0][:, :], cfg["win"][fc * 128:(fc + 1) * 128, 0:D], writes=[bxt[0]])
                S.dma("sp", sxt[1], xt[1][:, 0:NCOL - D], cfg["win"][fc * 128:(fc + 1) * 128, D:NCOL], writes=[bxt[1]])
                yield True
                S.op("dve", lambda e: e.tensor_copy(out=W[:, fc, 0:D], in_=xt[0][:, :]), reads=[bxt[0]], writes=[bW])
                S.op("pool", lambda e: e.tensor_copy(out=W[:, fc, D:NCOL], in_=xt[1][:, 0:NCOL - D]), reads=[bxt[1]],
                     writes=[bW])
                yield True

        def setup():
            first = not state.get("setup_done")
            state["setup_done"] = True
            S.dma("sp", sconst, gT[:, :], cur["normg"][:, :], writes=[bconst])
            if first:
                S.dma("sp", sconst, ident[:, :], identD[:, :], writes=[bconst], same_batch=True)
                S.dma("sp", sconst, tri[:, :], triD[:, :], writes=[bconst], same_batch=True)
            S.dma("sp", sconst, lamv[:, :], cur["lam4"][0:1, :].partition_broadcast(128), writes=[bconst], same_batch=True)
            S.dma("sp", sconst, cst[:, 4:5], cur["sublng"][:, :], writes=[bconst], same_batch=True)
            S.dma("sp", sconst, bf2[0:1, 0:1], cur["bfg"][0:1, 0:1], writes=[bconst], same_batch=True)
            S.dma("sp", sconst, bf2[1:2, 0:1], cur["bfg"][0:1, 1:2], writes=[bconst], same_batch=True)

            S.op("pool", lambda e: e.memset(carry[:, :], 0.0), writes=[brows])
            if first:
                S.op("pool", lambda e: e.memset(onesb[:, :], 1.0), writes=[bconst])
                S.op("pool", lambda e: e.memset(onesf[:, :], 1.0 / 128.0), writes=[bconst])
                S.op("pool", lambda e: e.memset(ones32[:, :], 1.0), writes=[bconst])
                S.op("pool", lambda e: e.memset(cst[:, 0:1], -0.5), writes=[bconst])
                S.op("pool", lambda e: e.memset(cst[:, 1:2], EPS), writes=[bconst])
                S.op("pool", lambda e: e.memset(onesrow[:, :], 1.0), writes=[bconst])
                S.op("pool", lambda e: e.memset(Vf[:, :, :], 1.0), writes=bVf)
                S.op("pool", lambda e: e.memset(KA[64:70, :], 1.0), writes=bKA)
                S.op("pool", lambda e: e.memset(KB[64:70, :], 1.0), writes=bKB)
                for qs_ in qsets:
                    S.op("pool", lambda e: e.memset(qs_[2][64:70, :], 1.0), writes=[qs_[3]])
                    S.op("pool", lambda e: e.memset(qs_[4][64:70, :], 1.0), writes=[qs_[5]])

            S.op("dve", lambda e: e.tensor_tensor(out=ep[2][:, 0:64], in0=lamv[:, 0:64], in1=lamv[:, 64:128], op=ALU.mult),
                 reads=[bconst], writes=[bep[2]])
            S.op("dve", lambda e: e.tensor_tensor(out=ep[2][:, 64:128], in0=lamv[:, 128:192], in1=lamv[:, 192:256],
                                                  op=ALU.mult), reads=[bconst], writes=[bep[2]])
            S.op("dve", lambda e: e.reduce_sum(out=cst[:, 5:6], in_=ep[2][:, 0:64], axis=mybir.AxisListType.X),
                 reads=[bep[2]], writes=[bconst])
            S.op("dve", lambda e: e.reduce_sum(out=cst[:, 6:7], in_=ep[2][:, 64:128], axis=mybir.AxisListType.X),
                 reads=[bep[2]], writes=[bconst])
            S.op("act", lambda e: e.activation(out=cst[:, 7:9], in_=cst[:, 5:7], func=AF.Exp), reads=[bconst], writes=[bconst])
            S.op("dve", lambda e: e.tensor_tensor(out=cst[:, 9:10], in0=cst[:, 8:9], in1=cst[:, 7:8], op=ALU.subtract),
                 reads=[bconst], writes=[bconst])
            S.op("dve", lambda e: e.tensor_scalar(out=cst[:, 3:4], in0=cst[:, 9:10], scalar1=-float(cur["li"]),
                                                  scalar2=None, op0=ALU.add), reads=[bconst], writes=[bconst])
            S.op("dve", lambda e: e.tensor_scalar(out=cst[:, 2:3], in0=cst[:, 4:5], scalar1=float(1.0 - cur["li"]),
                                                  scalar2=None, op0=ALU.mult), reads=[bconst], writes=[bconst])
            S.op("dve", lambda e: e.tensor_scalar(out=bf2[:, :], in0=bf2[:, :], scalar1=-1.0, scalar2=None, op0=ALU.mult),
                 reads=[bconst], writes=[bconst])

            if not state.get("w_loaded"):
                for _ in weights_gen(cur):
                    pass
            state["w_loaded"] = False
            if cur["has_prev"]:
                for fc in range(8):
                    s = fc % 2
                    S.dma("sp", sxt[s], xt[s][:, :], cur["woutp"][fc * 128:(fc + 1) * 128, :], writes=[bxt[s]])
                    S.op("dve", lambda e: e.tensor_copy(out=WO[:, fc, :], in_=xt[s][:, :]), reads=[bxt[s]], writes=[bWO])

        state = {"xt": 0, "pb": 0, "sbank": 0, "p": 0, "mix": 0}

        def proj_bank():
            i = (3, 5)[state["pb"] % 2]
            state["pb"] += 1
            return bank[i], bbank[i]

        def s_bank(pool=(0, 1)):
            i = pool[state["sbank"] % len(pool)]
            state["sbank"] += 1
            return bank[i], bbank[i]

        def project(c):
            QdT, bQd, QA, bQA, QB, bQB, ZdT, bZd, ZfT, bZf = qsets[c % 2]
            p0, w = chunk_pos(c)
            tiles = chunk_tiles(c)
            rs = 0
            S.dma("sp", srope[rs], rope[rs][:, 0, 0:w], cosT[:, p0:p0 + w], writes=[brope[rs]])
            S.dma("sp", srope[rs], rope[rs][:, 1, 0:w], sinT[:, p0:p0 + w], writes=[brope[rs]], same_batch=True)
            if cur["has_prev"]:
                ms = 0
                S.dma("sp", smp[ms], mp[ms][:, :, 0:w],
                      cur["mixp"].rearrange("(c p) t -> p c t", p=128)[:, :, p0:p0 + w], writes=[bmp[ms]])
            tinfo = []
            if cur.get("ut_load"):
                S.dma("sp", sutl, uT[:, :, 0:w], cur["utscr"][:, :, p0:p0 + w], writes=[buT])
                yield True
            else:
                for j, t in enumerate(tiles):
                    tp, n = tile_pos(t)
                    xs = state["xt"] % 2
                    state["xt"] += 1
                    tinfo.append((j, t, tp, n, xs))
            def load(k):
                j, t, tp, n, xs = tinfo[k]
                S.dma("sp", sxt[xs], xt[xs][0:n, :], cur["hin"][tp:tp + n, :], writes=[bxt[xs]])
            if tinfo:
                load(0)
                yield True
            for k, (j, t, tp, n, xs) in enumerate(tinfo):
                X, bX = xt[xs], bxt[xs]
                U, bU = ub[xs], bub[xs]
                if k + 1 < len(tinfo):
                    load(k + 1)
                if cur["has_prev"]:
                    for half in range(2):
                        pb, bpb = proj_bank()
                        for fc in range(8):
                            S.op("pe", lambda e: e.matmul(
                                pb[0:n, :], lhsT=mp[ms][:, fc, 128 * j:128 * j + n],
                                rhs=WO[:, fc, half * 512:(half + 1) * 512], start=(fc == 0), stop=(fc == 7)),
                                reads=[bmp[ms], bWO], writes=[bpb])
                        yield False
                        S.op("dve", lambda e: e.tensor_tensor(
                            out=X[0:n, half * 512:(half + 1) * 512], in0=pb[0:n, :],
                            in1=X[0:n, half * 512:(half + 1) * 512], op=ALU.add), reads=[bpb, bX], writes=[bX])
                        yield True
                if cur.get("h1out") is not None:
                    S.dma("pool", sh1[xs], cur["h1out"][tp:tp + n, :], X[0:n, :], reads=[bX])
                for hf in range(2):
                    S.op("dve", lambda e: e.tensor_tensor(out=ep[2 + hf][0:n, :], in0=X[0:n, 512 * hf:512 * hf + 512],
                                                          in1=X[0:n, 512 * hf:512 * hf + 512], op=ALU.mult),
                         reads=[bX], writes=[bep[2 + hf]])
                    S.op("dve", lambda e: e.reduce_sum(out=stat[0:n, 4 + hf:5 + hf], in_=ep[2 + hf][0:n, :],
                                                       axis=mybir.AxisListType.X), reads=[bep[2 + hf]], writes=[bstat])
                S.op("dve", lambda e: e.tensor_tensor(out=stat[0:n, 0:1], in0=stat[0:n, 4:5], in1=stat[0:n, 5:6], op=ALU.add),
                     reads=[bstat], writes=[bstat])
                S.op("dve", lambda e: e.tensor_scalar(out=stat[0:n, 1:2], in0=stat[0:n, 0:1], scalar1=1.0 / D,
                                                      scalar2=EPS, op0=ALU.mult, op1=ALU.add),
                     reads=[bstat], writes=[bstat])
                yield True
                S.op("act", lambda e: e.activation(out=stat[0:n, 3:4], in_=stat[0:n, 1:2], func=AF.Ln),
                     reads=[bstat], writes=[bstat])
                S.op("act", lambda e: e.activation(out=stat[0:n, 2:3], in_=stat[0:n, 3:4], func=AF.Exp, scale=-0.5),
                     reads=[bstat], writes=[bstat])
                S.op("act", lambda e: e.activation(out=U[0:n, :], in_=X[0:n, :], func=AF.Copy, scale=stat[0:n, 2:3]),
                     reads=[bX, bstat], writes=[bU])
                yield True
                tb, btb = proj_bank()
                tbv = tb[:, :].bitcast(BF16)
                for fc in range(8):
                    S.op("pe", lambda e: e.transpose(out=tbv[:, fc * 128:fc * 128 + n],
                                                     in_=U[0:n, fc * 128:(fc + 1) * 128], identity=ident[0:n, 0:n]),
                         reads=[bU, bconst], writes=[btb])
                yield False
                for fc in range(8):
                    S.op("dve", lambda e: e.tensor_scalar(out=uT[:, fc, 128 * j:128 * j + n],
                                                          in0=tbv[:, fc * 128:fc * 128 + n],
                                                          scalar1=gT[:, fc:fc + 1], scalar2=None, op0=ALU.mult),
                         reads=[btb, bconst], writes=[buT])
                yield True
            if cur.get("ut_store"):
                S.dma("pool", suts, cur["utscr"][:, :, p0:p0 + w], uT[:, :, 0:w], reads=[buT])

            def fm(col, M):
                pb, bpb = proj_bank()
                for fc in range(8):
                    S.op("pe", lambda e: e.matmul(pb[0:M, 0:w], lhsT=W[:, fc, col:col + M], rhs=uT[:, fc, 0:w],
                                                  start=(fc == 0), stop=(fc == 7)),
                         reads=[bW, buT], writes=[bpb])
                return pb, bpb

            for (cn, cs, dst, bdst, dcol) in ((C_DQ, C_DQS, QdT, bQd, 0), (C_DK, C_DKS, KdT, bKd[c], p0)):
                pa, bpa = fm(cn, 128)
                pbk, bpbk = fm(cs, 128)
                yield False
                S.op("dve", lambda e: e.tensor_tensor(out=t1[:, 0:w], in0=pa[:, 0:w], in1=rope[rs][:, 0, 0:w],
                                                      op=ALU.mult), reads=[bpa, brope[rs]], writes=[bt1])
                S.op("dve", lambda e: e.tensor_tensor(out=t2[:, 0:w], in0=pbk[:, 0:w], in1=rope[rs][:, 1, 0:w],
                                                      op=ALU.mult), reads=[bpbk, brope[rs]], writes=[bt2])
                S.op("dve", lambda e: e.tensor_tensor(out=dst[:, dcol:dcol + w], in0=t1[:, 0:w], in1=t2[:, 0:w],
                                                      op=ALU.add), reads=[bt1, bt2], writes=[bdst])
                yield True
            for (col, dA, bdA, dB, bdB, dcol) in ((C_FQA, QA, bQA, QB, bQB, 0), (C_FKA, KA, bKA[c], KB, bKB[c], p0)):
                pa, bpa = fm(col, 128)
                yield False
                S.op("dve", lambda e: e.tensor_copy(out=dA[0:64, dcol:dcol + w], in_=pa[0:64, 0:w]),
                     reads=[bpa], writes=[bdA])
                S.op("dve", lambda e: e.tensor_copy(out=dB[0:64, dcol:dcol + w], in_=pa[64:128, 0:w]),
                     reads=[bpa], writes=[bdB])
                yield True
            for (col, dst, bdst) in ((C_DZ, ZdT, bZd), (C_FZ, ZfT, bZf)):
                pa, bpa = fm(col, 128)
                yield False
                S.op("act", lambda e: e.activation(out=dst[:, 0:w], in_=pa[:, 0:w], func=AF.Silu),
                     reads=[bpa], writes=[bdst])
                yield True
            pa, bpa = fm(C_FLA, 2)
            yield False
            ra, rb = rows[:, 0, 0:w], rows[:, 1, 0:w]
            S.op("act", lambda e: e.activation(out=ra, in_=pa[0:2, 0:w], func=AF.Exp, scale=-1.0, bias=bf2[0:2, 0:1]),
                 reads=[bpa, bconst], writes=[brows])
            S.op("act", lambda e: e.activation(out=ra, in_=ra, func=AF.Ln, bias=1.0, scale=1.0),
                 reads=[brows], writes=[brows])
            yield True
            S.op("dve", lambda e: e.tensor_tensor_scan(out=rb, data0=onesrow[:, 0:w], data1=ra, initial=carry[:, 0:1],
                                                       op0=ALU.mult, op1=ALU.subtract),
                 reads=[brows, bconst], writes=[brows])
            S.op("dve", lambda e: e.tensor_copy(out=carry[:, 0:1], in_=rows[:, 1, w - 1:w]), reads=[brows], writes=[brows])
            S.op("dve", lambda e: e.tensor_scalar(out=ra, in0=rb, scalar1=8.0, scalar2=None, op0=ALU.mult),
                 reads=[brows], writes=[brows])
            S.op("dve", lambda e: e.tensor_copy(out=stg[:, 0, 0:w], in_=ra), reads=[brows], writes=[bstg])
            S.op("dve", lambda e: e.tensor_tensor(out=rb, in0=ra, in1=stg[:, 0, 0:w], op=ALU.subtract),
                 reads=[brows, bstg], writes=[brows])
            S.op("dve", lambda e: e.tensor_copy(out=stg[:, 1, 0:w], in_=rb), reads=[brows], writes=[bstg])
            S.op("dve", lambda e: e.tensor_tensor(out=ra, in0=rb, in1=stg[:, 1, 0:w], op=ALU.subtract),
                 reads=[brows, bstg], writes=[brows])
            S.op("dve", lambda e: e.tensor_copy(out=stg[:, 2, 0:w], in_=ra), reads=[brows], writes=[bstg])
            S.op("dve", lambda e: e.tensor_scalar(out=stg[:, 3:6, 0:w], in0=stg[:, 0:3, 0:w], scalar1=-1.0,
                                                  scalar2=None, op0=ALU.mult), reads=[bstg], writes=[bstg])
            yield True
            first = True
            for hh in range(2):
                Qt, bQt = (QA, bQA) if hh == 0 else (QB, bQB)
                Kt, bKt = (KA, bKA[c]) if hh == 0 else (KB, bKB[c])
                for i in range(3):
                    S.dma("sp", saug, Qt[64 + i:65 + i, 0:w], stg[hh:hh + 1, i, 0:w], reads=[bstg], writes=[bQt],
                          same_batch=not first)
                    first = False
                    S.dma("sp", saug, Kt[67 + i:68 + i, p0:p0 + w], stg[hh:hh + 1, 3 + i, 0:w], reads=[bstg], writes=[bKt],
                          same_batch=True)
            yield True
            for j, t in enumerate(tiles):
                tp, n = tile_pos(t)
                pb, bpb = proj_bank()
                for fc in range(8):
                    S.op("pe", lambda e: e.matmul(pb[0:n, 0:256], lhsT=uT[:, fc, 128 * j:128 * j + n],
                                                  rhs=W[:, fc, C_V:C_V + 256], start=(fc == 0), stop=(fc == 7)),
                         reads=[bW, buT], writes=[bpb])
                yield False
                S.op("dve", lambda e: e.tensor_copy(out=Vd[0:n, t, :], in_=pb[0:n, 0:128]), reads=[bpb], writes=[bVd[c]])
                S.op("dve", lambda e: e.tensor_copy(out=Vf[0:n, t, 0:64], in_=pb[0:n, 128:192]), reads=[bpb],
                     writes=[bVf[c]])
                S.op("dve", lambda e: e.tensor_copy(out=Vf[0:n, t, 128:192], in_=pb[0:n, 192:256]), reads=[bpb],
                     writes=[bVf[c]])
                yield True

        def attention(c, nxt=None):
            QdT, bQd, QA, bQA, QB, bQB, ZdT, bZd, ZfT, bZf = qsets[c % 2]
            p0, w = chunk_pos(c)
            last_t = chunk_tiles(c)[-1]
            kts = list(range(0, last_t + 1))
            first_diag = chunk_tiles(c)[0]

            def kinfo(kt):
                kp, kn = tile_pos(kt)
                if kt >= first_diag:
                    i = kt - first_diag
                    return kp, kn, 128 * i, True
                return kp, kn, 0, False

            def kchunk(kt):
                return 0 if kt == 0 else (kt - 1) // 4 + 1

            O1, bO1 = bank[2], bbank[2]
            L1, bL1 = bank[3], bbank[3]
            O2, bO2 = bank[4], bbank[4]
            L2, bL2 = bank[5], bbank[5]
            OA, bOA = bank[6], bbank[6]
            OB, bOB = bank[7], bbank[7]

            steps = []
            for kt in kts:
                steps.append(("d1", kt))
                steps.append(("d2", kt))
            for kt in kts:
                steps.append(("fA", kt))
            for kt in kts:
                steps.append(("fB", kt))
            pend = {}

            def qk(si):
                u, kt = steps[si]
                kp, kn, qlo, diag = kinfo(kt)
                kc = kchunk(kt)
                if c == 0:
                    spool = (0, 1)
                elif u in ("d1", "d2"):
                    spool = (0, 1, 6, 7)
                elif u == "fA":
                    spool = (0, 1, 7)
                else:
                    spool = (0, 1, 2, 4)
                sbk, bsbk = s_bank(spool)
                pi = state["p"] % NP
                state["p"] += 1
                P, bPi = Pb[pi], bP[pi]
                if u == "d1":
                    lhsT, rhs, rd = KdT[0:64, kp:kp + kn], QdT[0:64, qlo:w], [bKd[kc], bQd]
                elif u == "d2":
                    lhsT, rhs, rd = KdT[64:128, kp:kp + kn], QdT[64:128, qlo:w], [bKd[kc], bQd]
                elif u == "fA":
                    lhsT, rhs, rd = KA[0:70, kp:kp + kn], QA[0:70, qlo:w], [bKA[kc], bQA]
                else:
                    lhsT, rhs, rd = KB[0:70, kp:kp + kn], QB[0:70, qlo:w], [bKB[kc], bQB]
                S.op("pe", lambda e: e.matmul(sbk[0:kn, qlo:w], lhsT=lhsT, rhs=rhs, start=True, stop=True),
                     reads=rd, writes=[bsbk])
                pend[si] = (P, bPi, sbk, bsbk)

            def qk_post(si):
                u, kt = steps[si]
                kp, kn, qlo, diag = kinfo(kt)
                P, bPi, sbk, bsbk = pend[si]
                S.op("act", lambda e: e.activation(out=P[0:kn, qlo:w], in_=sbk[0:kn, qlo:w], func=AF.Exp, scale=0.125),
                     reads=[bsbk], writes=[bPi])
                if diag:
                    dw = min(128, w - qlo)
                    S.op("pool", lambda e: e.tensor_tensor(out=P[0:kn, qlo:qlo + dw], in0=P[0:kn, qlo:qlo + dw],
                                                           in1=tri[0:kn, 0:dw], op=ALU.mult),
                         reads=[bPi, bconst], writes=[bPi])

            def pv(si):
                u, kt = steps[si]
                kp, kn, qlo, diag = kinfo(kt)
                kc = kchunk(kt)
                P, bPi = pend.pop(si)[0:2]
                st, sp_ = (kt == kts[0]), (kt == kts[-1])
                rhs = P[0:kn, qlo:w]
                if u in ("d1", "d2"):
                    O, bO, Lb, bLb = (O1, bO1, L1, bL1) if u == "d1" else (O2, bO2, L2, bL2)
                    S.op("pe", lambda e: e.matmul(O[:, qlo:w], lhsT=Vd[0:kn, kt, :], rhs=rhs, start=st, stop=sp_),
                         reads=[bVd[kc], bPi], writes=[bO])
                    ai = 0 if u == "d1" else 1
                    S.op("dve", lambda e: e.tensor_tensor(out=acc[ai][0:kn, qlo:w], in0=acc[ai][0:kn, qlo:w],
                                                          in1=P[0:kn, qlo:w], op=ALU.add),
                         reads=[bPi, bacc[ai]], writes=[bacc[ai]])
                elif u == "fA":
                    S.op("pe", lambda e: e.matmul(OA[:, qlo:w], lhsT=Vf[0:kn, kt, 0:128], rhs=rhs, start=st, stop=sp_),
                         reads=[bVf[kc], bPi], writes=[bOA])
                else:
                    S.op("pe", lambda e: e.matmul(OB[:, qlo:w], lhsT=Vf[0:kn, kt, 64:192], rhs=rhs, start=st, stop=sp_),
                         reads=[bVf[kc], bPi], writes=[bOB])

            for ai in range(2):
                S.op("pool", lambda e: e.memset(acc[ai][:, 0:w], 0.0), writes=[bacc[ai]])
            LA = 2 if c >= 1 else 1
            nd = 2 * len(kts)
            nfa = nd + len(kts)
            groups = []
            i = 0
            while i < len(steps):
                if steps[i][0] == "d1":
                    groups.append([i, i + 1])
                    i += 2
                else:
                    groups.append([i])
                    i += 1
            per = max(1, -(-70 // len(groups)))
            pst = {"clean": True}

            def pull(k):
                if nxt is None:
                    return
                for _ in range(k):
                    r = next(nxt, None)
                    if r is None:
                        pst["clean"] = True
                        return
                    pst["clean"] = r

            def make_clean():
                while not pst["clean"]:
                    pull(1)

            for gi in range(len(groups) + LA):
                if gi > 0:
                    pull(per)
                if gi < len(groups):
                    for si in groups[gi]:
                        qk(si)
                    for si in groups[gi]:
                        qk_post(si)
                if gi - LA >= 0:
                    for si in groups[gi - LA]:
                        pv(si)
                        done = si + 1
                        if done in (nd, nfa, len(steps)):
                            make_clean()
                        if done == nd:
                            epi_diff(c, p0, w, O1, bO1, L1, bL1, O2, bO2, L2, bL2)
                        elif done == nfa:
                            epi_fox_a(c, p0, w, OA, bOA)
                        elif done == len(steps):
                            epi_fox_b(c, p0, w, OB, bOB)

        def epi_diff(c, p0, w, O1, bO1, L1, bL1, O2, bO2, L2, bL2):
            QdT, bQd, QA, bQA, QB, bQB, ZdT, bZd, ZfT, bZf = qsets[c % 2]
            r1, r2, a, b, o, sq = (ep[i] for i in range(6))
            br1, br2, ba, bb, bo, bsq = (bep[i] for i in range(6))
            S.op("pe", lambda e: e.matmul(L1[:, 0:w], lhsT=ones32[:, :], rhs=acc[0][:, 0:w], start=True, stop=True),
                 reads=[bacc[0], bconst], writes=[bL1])
            S.op("pe", lambda e: e.matmul(L2[:, 0:w], lhsT=ones32[:, :], rhs=acc[1][:, 0:w], start=True, stop=True),
                 reads=[bacc[1], bconst], writes=[bL2])
            S.op("dve", lambda e: e.reciprocal(out=r1[:, 0:w], in_=L1[:, 0:w]), reads=[bL1], writes=[br1])
            S.op("dve", lambda e: e.reciprocal(out=r2[:, 0:w], in_=L2[:, 0:w]), reads=[bL2], writes=[br2])
            S.op("dve", lambda e: e.tensor_tensor(out=a[:, 0:w], in0=O1[:, 0:w], in1=r1[:, 0:w], op=ALU.mult),
                 reads=[bO1, br1], writes=[ba])
            S.op("dve", lambda e: e.tensor_tensor(out=b[:, 0:w], in0=O2[:, 0:w], in1=r2[:, 0:w], op=ALU.mult),
                 reads=[bO2, br2], writes=[bb])
            S.op("dve", lambda e: e.scalar_tensor_tensor(out=o[:, 0:w], in0=b[:, 0:w], scalar=cst[:, 3:4], in1=a[:, 0:w],
                                                         op0=ALU.mult, op1=ALU.add), reads=[ba, bb, bconst], writes=[bo])
            S.op("pool", lambda e: e.tensor_tensor(out=sq[:, 0:w], in0=o[:, 0:w], in1=o[:, 0:w], op=ALU.mult),
                 reads=[bo], writes=[bsq])
            mb, bmb = s_bank()
            S.op("pe", lambda e: e.matmul(mb[:, 0:w], lhsT=onesf[:, :], rhs=sq[:, 0:w], start=True, stop=True),
                 reads=[bsq, bconst], writes=[bmb])
            S.op("dve", lambda e: e.tensor_scalar(out=r1[:, 0:w], in0=mb[:, 0:w], scalar1=EPS, scalar2=None, op0=ALU.add),
                 reads=[bmb], writes=[br1])
            S.op("act", lambda e: e.activation(out=r1[:, 0:w], in_=r1[:, 0:w], func=AF.Ln), reads=[br1], writes=[br1])
            S.op("act", lambda e: e.activation(out=r2[:, 0:w], in_=r1[:, 0:w], func=AF.Exp, scale=-0.5),
                 reads=[br1], writes=[br2])
            S.op("dve", lambda e: e.tensor_tensor(out=a[:, 0:w], in0=o[:, 0:w], in1=r2[:, 0:w], op=ALU.mult),
                 reads=[bo, br2], writes=[ba])
            mi = state["mix"] % 2
            S.op("dve", lambda e: e.scalar_tensor_tensor(out=mixd[mi][:, 0:w], in0=a[:, 0:w], scalar=cst[:, 2:3],
                                                         in1=ZdT[:, 0:w], op0=ALU.mult, op1=ALU.mult),
                 reads=[ba, bconst, bZd, bmixd[mi]], writes=[bmixd[mi]])
            S.dma("pool", smixd[mi], cur["mixo"][0:128, p0:p0 + w], mixd[mi][:, 0:w], reads=[bmixd[mi]])

        def epi_fox_a(c, p0, w, OA, bOA):
            QdT, bQd, QA, bQA, QB, bQB, ZdT, bZd, ZfT, bZf = qsets[c % 2]
            mi = state["mix"] % 2
            rl, t = ep[0], ep[1]
            brl, bt = bep[0], bep[1]
            S.op("dve", lambda e: e.reciprocal(out=rl[0:64, 0:w], in_=OA[64:128, 0:w]), reads=[bOA], writes=[brl])
            S.op("dve", lambda e: e.tensor_tensor(out=t[0:64, 0:w], in0=OA[0:64, 0:w], in1=rl[0:64, 0:w], op=ALU.mult),
                 reads=[bOA, brl], writes=[bt])
            S.op("pool", lambda e: e.tensor_tensor(out=mixf[mi][0:64, 0:w], in0=t[0:64, 0:w], in1=ZfT[0:64, 0:w],
                                                   op=ALU.mult), reads=[bt, bZf, bmixf[mi]], writes=[bmixf[mi]])

        def epi_fox_b(c, p0, w, OB, bOB):
            QdT, bQd, QA, bQA, QB, bQB, ZdT, bZd, ZfT, bZf = qsets[c % 2]
            mi = state["mix"] % 2
            rl, t = ep[2], ep[3]
            brl, bt = bep[2], bep[3]
            S.op("dve", lambda e: e.reciprocal(out=rl[64:128, 0:w], in_=OB[0:64, 0:w]), reads=[bOB], writes=[brl])
            S.op("dve", lambda e: e.tensor_tensor(out=t[64:128, 0:w], in0=OB[64:128, 0:w], in1=rl[64:128, 0:w],
                                                  op=ALU.mult), reads=[bOB, brl], writes=[bt])
            S.op("pool", lambda e: e.tensor_tensor(out=mixf[mi][64:128, 0:w], in0=t[64:128, 0:w], in1=ZfT[64:128, 0:w],
                                                   op=ALU.mult), reads=[bt, bZf, bmixf[mi]], writes=[bmixf[mi]])
            S.dma("pool", smixf[mi], cur["mixo"][128:256, p0:p0 + w], mixf[mi][:, 0:w], reads=[bmixf[mi]])
            state["mix"] += 1

        for ci, cfg in enumerate(cfgs):
            if fused and ci in (1, 4, 5):
                S.barrier()
            cur.clear()
            cur.update(cfg)
            setup()
            for _ in project(0):
                pass
            for c in range(nchunks):
                if c + 1 < nchunks:
                    nxt = project(c + 1)
                elif fused and ci + 1 < len(cfgs):
                    nxt = weights_gen(cfgs[ci + 1])
                    state["w_loaded"] = True
                else:
                    nxt = None
                attention(c, nxt)
                if nxt is not None:
                    for _ in nxt:
                        pass
        if fused:
            S.barrier()
            es2.close()
            emit_final(nc, es, S, bank, bbank, h1scr, [(mixall[1], woutp1)], fg, outD)
        S.finish("sp")
    return nc


def emit_final(nc, es, S, bank, bbank, hin, pairs, fg, out):
    NPR = len(pairs)
    sb = lambda name, shape, dt: _sb(nc, es, name, shape, dt)
    WO = [sb(f"fWO{i}", [128, 8, D], BF16) for i in range(NPR)]
    bWO = Buf("fWO")
    wst = [sb(f"fwst{i}", [128, D], F32) for i in range(2)]
    bwst = [Buf(f"fwst{i}") for i in range(2)]
    swst = [S.new_slot(f"fwst{i}") for i in range(2)]
    fgt = sb("fgt", [128, D], F32); bconst = Buf("fconst"); sconst = S.new_slot("fconst")
    cst = sb("fcst", [128, 2], F32)
    xt = [sb(f"fxt{i}", [128, D], F32) for i in range(2)]
    bxt = [Buf(f"fxt{i}") for i in range(2)]
    sxt = [S.new_slot(f"fxt{i}") for i in range(2)]
    mt = [sb(f"fmt{i}", [128, 2, 8, 128], BF16) for i in range(2)]
    bmt = [Buf(f"fmt{i}") for i in range(2)]
    smt = [S.new_slot(f"fmt{i}") for i in range(2)]
    ot = [sb(f"fot{i}", [128, D], F32) for i in range(2)]
    bot = [Buf(f"fot{i}") for i in range(2)]
    sot = [S.new_slot(f"fot{i}") for i in range(2)]
    junk = sb("fjunk", [128, D], F32); bjunk = Buf("fjunk")
    stat = sb("fstat", [128, 4], F32); bstat = Buf("fstat")
    S.dma("sp", sconst, fgt[:, :], fg[0:1, :].partition_broadcast(128), writes=[bconst])
    S.op("pool", lambda e: e.memset(cst[:, 0:1], -0.5), writes=[bconst])
    wi = 0
    for li, wsrc in enumerate([p[1] for p in pairs]):
        for fc in range(8):
            s = wi % 2; wi += 1
            S.dma("sp", swst[s], wst[s][:, :], wsrc[fc * 128:(fc + 1) * 128, :], writes=[bwst[s]])
            S.op("dve", lambda e: e.tensor_copy(out=WO[li][:, fc, :], in_=wst[s][:, :]), reads=[bwst[s]], writes=[bWO])
    pbi = 0
    for t in range(SEQ // 128):
        s = t % 2
        X = xt[s]
        p0 = NMETA + 128 * t
        S.dma("sp", sxt[s], X[:, :], hin[p0:p0 + 128, :], writes=[bxt[s]])
        for li in range(NPR):
            S.dma("sp", smt[s], mt[s][:, li, :, :], pairs[li][0].rearrange("(c p) t -> p c t", p=128)[:, :, p0:p0 + 128],
                  writes=[bmt[s]], same_batch=(li > 0))
        for half in range(2):
            pb, bpb = bank[pbi % 4], bbank[pbi % 4]
            pbi += 1
            k = 0
            for li in range(NPR):
                for fc in range(8):
                    S.op("pe", lambda e: e.matmul(pb[:, :], lhsT=mt[s][:, li, fc, :], rhs=WO[li][:, fc, half * 512:(half + 1) * 512],
                                                  start=(k == 0), stop=(k == 8 * NPR - 1)), reads=[bmt[s], bWO], writes=[bpb])
                    k += 1
            S.op("dve", lambda e: e.tensor_tensor(out=X[:, half * 512:(half + 1) * 512], in0=pb[:, :],
                                                  in1=X[:, half * 512:(half + 1) * 512], op=ALU.add),
                 reads=[bpb, bxt[s]], writes=[bxt[s]])
        S.op("dve", lambda e: e.tensor_tensor(out=junk[:, :], in0=X[:, :], in1=X[:, :], op=ALU.mult), reads=[bxt[s]], writes=[bjunk])
        S.op("dve", lambda e: e.reduce_sum(out=stat[:, 0:1], in_=junk[:, :], axis=mybir.AxisListType.X), reads=[bjunk], writes=[bstat])
        S.op("dve", lambda e: e.tensor_scalar(out=stat[:, 1:2], in0=stat[:, 0:1], scalar1=1.0 / D, scalar2=EPS,
                                              op0=ALU.mult, op1=ALU.add), reads=[bstat], writes=[bstat])
        S.op("act", lambda e: e.activation(out=stat[:, 3:4], in_=stat[:, 1:2], func=AF.Ln), reads=[bstat], writes=[bstat])
        S.op("act", lambda e: e.activation(out=stat[:, 2:3], in_=stat[:, 3:4], func=AF.Exp, scale=-0.5),
             reads=[bstat], writes=[bstat])
        S.op("dve", lambda e: e.scalar_tensor_tensor(out=ot[s][:, :], in0=X[:, :], scalar=stat[:, 2:3], in1=fgt[:, :],
                                                     op0=ALU.mult, op1=ALU.mult),
             reads=[bxt[s], bstat, bconst, bot[s]], writes=[bot[s]])
        S.dma("pool", sot[s], out[t * 128:(t + 1) * 128, :], ot[s][:, :], reads=[bot[s]])


def build_final():
    NTOK = 2048
    nc = bass.Bass("TRN2", target_bir_lowering=False)
    xq = nc.dram_tensor("xq", [NTOK, D], F32, kind="ExternalInput").ap()
    m1 = nc.dram_tensor("m1", [D, NTOK], BF16, kind="ExternalInput").ap()
    m2 = nc.dram_tensor("m2", [D, NTOK], BF16, kind="ExternalInput").ap()
    wo0 = nc.dram_tensor("wo0", [D, D], F32, kind="ExternalInput").ap()
    wo1 = nc.dram_tensor("wo1", [D, D], F32, kind="ExternalInput").ap()
    fg = nc.dram_tensor("fg", [1, D], F32, kind="ExternalInput").ap()
    out = nc.dram_tensor("out", [NTOK, D], F32, kind="ExternalOutput").ap()
    with ExitStack() as es:
        S = Sched(nc, es)
        sb = lambda name, shape, dt: _sb(nc, es, name, shape, dt)
        WO = [sb(f"WO{i}", [128, 8, D], BF16) for i in range(2)]
        bWO = Buf("WO")
        wst = [sb(f"wst{i}", [128, D], F32) for i in range(2)]
        bwst = [Buf(f"wst{i}") for i in range(2)]
        swst = [S.new_slot(f"wst{i}") for i in range(2)]
        fgt = sb("fgt", [128, D], F32); bconst = Buf("const"); sconst = S.new_slot("const")
        cst = sb("cst", [128, 2], F32)
        xt = [sb(f"xt{i}", [128, D], F32) for i in range(2)]
        bxt = [Buf(f"xt{i}") for i in range(2)]
        sxt = [S.new_slot(f"xt{i}") for i in range(2)]
        mt = [sb(f"mt{i}", [128, 2, 8, 128], BF16) for i in range(2)]
        bmt = [Buf(f"mt{i}") for i in range(2)]
        smt = [S.new_slot(f"mt{i}") for i in range(2)]
        ot = [sb(f"ot{i}", [128, D], F32) for i in range(2)]
        bot = [Buf(f"ot{i}") for i in range(2)]
        sot = [S.new_slot(f"ot{i}") for i in range(2)]
        junk = sb("junk", [128, D], F32); bjunk = Buf("junk")
        stat = sb("stat", [128, 4], F32); bstat = Buf("stat")
        bank = [_ps(nc, es, f"bank{i}", [128, 512], F32) for i in range(4)]
        bbank = [Buf(f"bank{i}") for i in range(4)]

        S.dma("sp", sconst, fgt[:, :], fg[0:1, :].partition_broadcast(128), writes=[bconst])
        S.op("pool", lambda e: e.memset(cst[:, 0:1], -0.5), writes=[bconst])
        wi = 0
        for li, wsrc in enumerate((wo0, wo1)):
            for fc in range(8):
                s = wi % 2; wi += 1
                S.dma("sp", swst[s], wst[s][:, :], wsrc[fc * 128:(fc + 1) * 128, :], writes=[bwst[s]])
                S.op("dve", lambda e, s=s, fc=fc, li=li: e.tensor_copy(out=WO[li][:, fc, :], in_=wst[s][:, :]),
                     reads=[bwst[s]], writes=[bWO])
        pbi = 0
        for t in range(NTOK // 128):
            s = t % 2
            X = xt[s]
            S.dma("sp", sxt[s], X[:, :], xq[t * 128:(t + 1) * 128, :], writes=[bxt[s]])
            S.dma("sp", smt[s], mt[s][:, 0, :, :], m1.rearrange("(c p) t -> p c t", p=128)[:, :, t * 128:(t + 1) * 128],
                  writes=[bmt[s]])
            S.dma("sp", smt[s], mt[s][:, 1, :, :], m2.rearrange("(c p) t -> p c t", p=128)[:, :, t * 128:(t + 1) * 128],
                  writes=[bmt[s]], same_batch=True)
            for half in range(2):
                pb, bpb = bank[pbi % 4], bbank[pbi % 4]
                pbi += 1
                k = 0
                for li in range(2):
                    for fc in range(8):
                        S.op("pe", lambda e, li=li, fc=fc, k=k: e.matmul(pb[:, :], lhsT=mt[s][:, li, fc, :],
                                                                          rhs=WO[li][:, fc, half * 512:(half + 1) * 512],
                                                                          start=(k == 0), stop=(k == 15)),
                             reads=[bmt[s], bWO], writes=[bpb])
                        k += 1
                S.op("dve", lambda e: e.tensor_tensor(out=X[:, half * 512:(half + 1) * 512], in0=pb[:, :],
                                                      in1=X[:, half * 512:(half + 1) * 512], op=ALU.add),
                     reads=[bpb, bxt[s]], writes=[bxt[s]])
            S.op("dve", lambda e: e.tensor_tensor(out=junk[:, :], in0=X[:, :], in1=X[:, :], op=ALU.mult),
                 reads=[bxt[s]], writes=[bjunk])
            S.op("dve", lambda e: e.reduce_sum(out=stat[:, 0:1], in_=junk[:, :], axis=mybir.AxisListType.X),
                 reads=[bjunk], writes=[bstat])
            S.op("dve", lambda e: e.tensor_scalar(out=stat[:, 1:2], in0=stat[:, 0:1], scalar1=1.0 / D, scalar2=EPS,
                                                  op0=ALU.mult, op1=ALU.add), reads=[bstat], writes=[bstat])
            S.op("act", lambda e: e.activation(out=stat[:, 3:4], in_=stat[:, 1:2], func=AF.Ln), reads=[bstat], writes=[bstat])
            S.op("act", lambda e: e.activation(out=stat[:, 2:3], in_=stat[:, 3:4], func=AF.Exp, scale=-0.5),
                 reads=[bstat], writes=[bstat])
            S.op("dve", lambda e: e.scalar_tensor_tensor(out=ot[s][:, :], in0=X[:, :], scalar=stat[:, 2:3], in1=fgt[:, :],
                                                         op0=ALU.mult, op1=ALU.mult),
                 reads=[bxt[s], bstat, bconst, bot[s]], writes=[bot[s]])
            S.dma("pool", sot[s], out[t * 128:(t + 1) * 128, :], ot[s][:, :], reads=[bot[s]])
        S.finish("sp")
    return nc


def _core_cols(g):
    dq = [g * 128 + r for r in range(128)]
    dqs = [g * 128 + (r // 64) * 64 + ((r % 64) + 32) % 64 for r in range(128)]
    dk = [512 + x for x in dq]
    dks = [512 + x for x in dqs]
    dv = [1024 + g * 128 + r for r in range(128)]
    a, b = 2 * g, 2 * g + 1
    fva = [3072 + a * 64 + r for r in range(64)]
    fvb = [3072 + b * 64 + r for r in range(64)]
    dz = [1536 + g * 128 + r for r in range(128)]
    fz = [3584 + a * 64 + r for r in range(128)]
    fqa = [2048 + a * 64 + r for r in range(64)]
    fqb = [2048 + b * 64 + r for r in range(64)]
    fka = [2560 + a * 64 + r for r in range(64)]
    fkb = [2560 + b * 64 + r for r in range(64)]
    fl = [4096 + a, 4096 + b]
    cols = dq + dqs + dk + dks + dv + fva + fvb + dz + fz + fqa + fqb + fka + fkb + fl
    assert len(cols) == NCOL
    return np.array(cols)


def _wout_perm():
    rows = []
    for g in range(4):
        rows += list(range(g * 128, (g + 1) * 128))
        rows += list(range(512 + g * 128, 512 + (g + 1) * 128))
    return np.array(rows)


def _rope_tables():
    half = 32
    inv = (10000.0 ** (-np.arange(half, dtype=np.float32) / half)).astype(np.float32)
    pos = np.arange(L, dtype=np.float32)
    ang = (pos[:, None] * inv[None, :]).astype(np.float32)
    cos = np.cos(ang).astype(np.float32).T
    sin = np.sin(ang).astype(np.float32).T
    cosT = np.concatenate([cos, cos, cos, cos], axis=0)
    sinT = np.concatenate([-sin, sin, -sin, sin], axis=0)
    return np.ascontiguousarray(cosT), np.ascontiguousarray(sinT)


_PROGS = {}
FUSED = True


def _prog(key, fn):
    if key not in _PROGS:
        _PROGS[key] = fn()
    return _PROGS[key]


def kernel(x, meta_tokens, norm_g, w_in, b_forget, lam_q1, lam_k1, lam_q2, lam_k2, subln_g, w_out, final_g):
    x = np.asarray(x, np.float32)
    f = lambda a: np.asarray(a, np.float32)
    meta_tokens, norm_g, w_in, b_forget = f(meta_tokens), f(norm_g), f(w_in), f(b_forget)
    lam_q1, lam_k1, lam_q2, lam_k2, subln_g, w_out, final_g = map(f, (lam_q1, lam_k1, lam_q2, lam_k2, subln_g, w_out, final_g))
    cores = list(range(8))
    cosT, sinT = _rope_tables()
    ident = np.eye(128, dtype=np.float32).astype(NPB)
    tri = np.triu(np.ones((128, 128), np.float32)).astype(NPB)
    perm = _wout_perm()
    hin = [np.ascontiguousarray(np.concatenate([meta_tokens, x[b]], axis=0)) for b in range(2)]

    def layer_maps(layer, mix_prev):
        maps = []
        for core in cores:
            b, g = core // 4, core % 4
            m = {
                "hin": hin[b],
                "win": np.ascontiguousarray(w_in[layer][:, _core_cols(g)]),
                "normg": np.ascontiguousarray(norm_g[layer].reshape(8, 128).T),
                "bfg": np.ascontiguousarray(b_forget[layer][2 * g:2 * g + 2].reshape(1, 2)),
                "lam4": np.concatenate([lam_q1[layer], lam_k1[layer], lam_q2[layer], lam_k2[layer]]).reshape(1, 256),
                "sublng": np.ascontiguousarray(subln_g[layer][g * 128:(g + 1) * 128].reshape(128, 1)),
                "cosT": cosT, "sinT": sinT, "ident": ident, "tri": tri,
            }
            if mix_prev is not None:
                m["mixp"] = mix_prev[b]
                m["woutp"] = np.ascontiguousarray(w_out[layer - 1][perm])
            maps.append(m)
        return maps

    if FUSED:
        nc = _prog("fused", lambda: build_layer(True, 0.0, fused=True))
        shared = {"cosT": cosT, "sinT": sinT, "ident": ident, "tri": tri,
                  "woutp0": np.ascontiguousarray(w_out[0][perm]), "woutp1": np.ascontiguousarray(w_out[1][perm]),
                  "fg": np.ascontiguousarray(final_g.reshape(1, D))}
        for layer in range(2):
            sfx = str(layer)
            shared["win" + sfx] = np.ascontiguousarray(
                np.concatenate([w_in[layer][:, _core_cols(g)] for g in range(4)], axis=0))
            shared["normg" + sfx] = np.ascontiguousarray(norm_g[layer].reshape(8, 128).T)
            shared["bfg" + sfx] = np.ascontiguousarray(b_forget[layer].reshape(4, 2))
            shared["lam4" + sfx] = np.concatenate([lam_q1[layer], lam_k1[layer], lam_q2[layer], lam_k2[layer]]).reshape(1, 256)
            shared["sublng" + sfx] = np.ascontiguousarray(subln_g[layer].reshape(512, 1))
        maps = [dict(shared, hin=hin[core // 4]) for core in cores]
        res = run_bass_kernel_spmd(nc, maps, core_ids=cores)
        return np.stack([np.asarray(res.results[0]["out"]), np.asarray(res.results[4]["out"])], axis=0).astype(np.float32)

    mixes = []
    mix_prev = None
    for layer in range(2):
        li = 0.8 - 0.6 * math.exp(-0.3 * layer)
        nc = _prog(("layer", layer), lambda: build_layer(layer > 0, li))
        res = run_bass_kernel_spmd(nc, layer_maps(layer, mix_prev), core_ids=cores)
        outs = [np.asarray(r["mixo"]) for r in res.results]
        mix_prev = [np.ascontiguousarray(np.concatenate(outs[4 * b:4 * b + 4], axis=0)) for b in range(2)]
        mixes.append(mix_prev)

    ncf = _prog("final", build_final)
    maps = []
    for core in cores:
        b, q = core // 4, core % 4
        t0 = NMETA + 2048 * q
        maps.append({
            "xq": np.ascontiguousarray(x[b, 2048 * q:2048 * (q + 1)]),
            "m1": np.ascontiguousarray(mixes[0][b][:, t0:t0 + 2048]),
            "m2": np.ascontiguousarray(mixes[1][b][:, t0:t0 + 2048]),
            "wo0": np.ascontiguousarray(w_out[0][perm]),
            "wo1": np.ascontiguousarray(w_out[1][perm]),
            "fg": final_g.reshape(1, D),
        })
    res = run_bass_kernel_spmd(ncf, maps, core_ids=cores)
    out = np.empty((2, SEQ, D), np.float32)
    for core in cores:
        b, q = core // 4, core % 4
        out[b, 2048 * q:2048 * (q + 1)] = np.asarray(res.results[core]["out"])
    return out
```

```python
import math
from contextlib import ExitStack

import numpy as np
import ml_dtypes

import concourse.bass as bass
import concourse.mybir as mybir
from concourse.bass_utils import run_bass_kernel_spmd

F32 = mybir.dt.float32
BF16 = mybir.dt.bfloat16
AF = mybir.ActivationFunctionType
ALU = mybir.AluOpType

D = 1024
SEQ = 8192
NMETA = 16
L = SEQ + NMETA
NCH = 17
NT = 65
NCOL = 1282
EPS = 1e-6
NPB = np.dtype(ml_dtypes.bfloat16)

C_DQ, C_DQS, C_DK, C_DKS = 0, 128, 256, 384
C_V = 512
C_DZ, C_FZ = 768, 896
C_FQA, C_FQB, C_FKA, C_FKB = 1024, 1088, 1152, 1216
C_FLA, C_FLB = 1280, 1281


def tile_pos(t):
    return (0, 16) if t == 0 else (16 + 128 * (t - 1), 128)


def chunk_tiles(c):
    return [0] if c == 0 else list(range(4 * (c - 1) + 1, 4 * c + 1))


def chunk_pos(c):
    return (0, 16) if c == 0 else (16 + 512 * (c - 1), 512)


class Buf:
    def __init__(self, name):
        self.name = name
        self.w = None
        self.r = {}


class Sched:
    def __init__(self, nc, es):
        self.nc = nc
        self.es = es
        self.eng = {"pe": nc.tensor, "act": nc.scalar, "dve": nc.vector, "pool": nc.gpsimd, "sp": nc.sync}
        self.sem = {k: es.enter_context(nc.semaphore("s_" + k)) for k in ("pe", "act", "dve", "pool")}
        self.cnt = {k: 0 for k in self.sem}
        self.seen = {k: {} for k in self.eng}
        self.dsems = []
        self.nsem = 0

    def _deps(self, e, reads, writes):
        deps = {}

        def add(tok, raw=False):
            if tok is None:
                return
            sem, val, owner = tok
            if owner == e and not (raw and e != "pe"):
                return
            if deps.get(sem, 0) < val:
                deps[sem] = val

        for b in reads:
            add(b.w, raw=True)
        for b in writes:
            add(b.w)
            for t in b.r.values():
                add(t)
        for sem, val in deps.items():
            if self.seen[e].get(sem, 0) < val:
                self.eng[e].wait_ge(sem, val)
                self.seen[e][sem] = val

    def op(self, e, fn, reads=(), writes=()):
        self._deps(e, reads, writes)
        ins = fn(self.eng[e])
        self.cnt[e] += 1
        ins.then_inc(self.sem[e], 1)
        tok = (self.sem[e], self.cnt[e], e)
        for b in reads:
            b.r[e] = tok
        for b in writes:
            b.w = tok
            b.r = {}

    def new_slot(self, name):
        sem = self.es.enter_context(self.nc.semaphore("d_" + name))
        slot = {"sem": sem, "total": 0, "name": name}
        self.dsems.append(slot)
        return slot

    def dma(self, q, slot, out, in_, reads=(), writes=(), same_batch=False):
        self._deps(q, reads, writes)
        if not same_batch and slot["total"] > 0:
            if self.seen[q].get(slot["sem"], 0) < slot["total"]:
                self.eng[q].wait_ge(slot["sem"], slot["total"])
                self.seen[q][slot["sem"]] = slot["total"]
        ins = self.eng[q].dma_start(out=out, in_=in_)
        slot["total"] += 16
        ins.then_inc(slot["sem"], 16)
        tok = (slot["sem"], slot["total"], "dma:" + slot["name"])
        if not same_batch:
            slot["bw"], slot["br"] = [], []
        for b in writes:
            b.r = {}
        slot.setdefault("bw", []).extend(writes)
        slot.setdefault("br", []).extend(reads)
        for b in slot["br"]:
            b.r["dma:" + slot["name"]] = tok
        for b in slot["bw"]:
            b.w = tok

    def barrier(self):
        for e in self.eng:
            for k in self.sem:
                if k != e and self.cnt[k] > 0 and self.seen[e].get(self.sem[k], 0) < self.cnt[k]:
                    self.eng[e].wait_ge(self.sem[k], self.cnt[k])
                    self.seen[e][self.sem[k]] = self.cnt[k]
            for slot in self.dsems:
                if slot["total"] > 0 and self.seen[e].get(slot["sem"], 0) < slot["total"]:
                    self.eng[e].wait_ge(slot["sem"], slot["total"])
                    self.seen[e][slot["sem"]] = slot["total"]

    def finish(self, q="sp"):
        for slot in self.dsems:
            if slot["total"] > 0:
                self.eng[q].wait_ge(slot["sem"], slot["total"])


def _sb(nc, es, name, shape, dt):
    return es.enter_context(nc.sbuf_tensor(name, shape, dt))


def _ps(nc, es, name, shape, dt):
    return es.enter_context(nc.psum_tensor(name, shape, dt))


def build_layer(has_prev, lambda_init, nchunks=NCH, dbg=False, fused=False):
    nc = bass.Bass("TRN2", target_bir_lowering=False)
    hin = nc.dram_tensor("hin", [L, D], F32, kind="ExternalInput").ap()
    cosT = nc.dram_tensor("cosT", [128, L], F32, kind="ExternalInput").ap()
    sinT = nc.dram_tensor("sinT", [128, L], F32, kind="ExternalInput").ap()
    identD = nc.dram_tensor("ident", [128, 128], BF16, kind="ExternalInput").ap()
    triD = nc.dram_tensor("tri", [128, 128], BF16, kind="ExternalInput").ap()
    cfgs = []
    if fused:
        mixall = [nc.dram_tensor("mixall%d" % l, [D, L], BF16).ap() for l in range(2)]
        h1scr = nc.dram_tensor("h1scr", [L, D], F32).ap()
        utscr = nc.dram_tensor("utscr", [128, 8, L], BF16).ap()
        woutp0 = nc.dram_tensor("woutp0", [D, D], F32, kind="ExternalInput").ap()
        woutp1 = nc.dram_tensor("woutp1", [D, D], F32, kind="ExternalInput").ap()
        fg = nc.dram_tensor("fg", [1, D], F32, kind="ExternalInput").ap()
        outD = nc.dram_tensor("out", [SEQ, D], F32, kind="ExternalOutput").ap()
        for l in range(2):
            winL = nc.dram_tensor("win%d" % l, [4 * D, NCOL], F32, kind="ExternalInput").ap()
            normgL = nc.dram_tensor("normg%d" % l, [128, 8], F32, kind="ExternalInput").ap()
            bfgL = nc.dram_tensor("bfg%d" % l, [4, 2], F32, kind="ExternalInput").ap()
            lam4L = nc.dram_tensor("lam4%d" % l, [1, 256], F32, kind="ExternalInput").ap()
            sublngL = nc.dram_tensor("sublng%d" % l, [4 * 128, 1], F32, kind="ExternalInput").ap()
            for g in range(4):
                cfg = {"hin": hin, "win": winL[g * D:(g + 1) * D, :], "normg": normgL, "bfg": bfgL[g:g + 1, :],
                       "lam4": lam4L, "sublng": sublngL[g * 128:(g + 1) * 128, :],
                       "has_prev": l > 0, "li": 0.8 - 0.6 * math.exp(-0.3 * l),
                       "mixo": mixall[l][g * 256:(g + 1) * 256, :], "layer": l, "g": g}
                cfg["utscr"] = utscr
                cfg["ut_store"] = (g == 0)
                cfg["ut_load"] = (g > 0)
                if l > 0 and g == 0:
                    cfg["mixp"] = mixall[0]
                    cfg["woutp"] = woutp0
                    cfg["h1out"] = h1scr
                elif l > 0:
                    cfg["hin"] = h1scr
                    cfg["has_prev"] = False
                cfgs.append(cfg)
    else:
        cfg = {
            "hin": hin,
            "win": nc.dram_tensor("win", [D, NCOL], F32, kind="ExternalInput").ap(),
            "normg": nc.dram_tensor("normg", [128, 8], F32, kind="ExternalInput").ap(),
            "bfg": nc.dram_tensor("bfg", [1, 2], F32, kind="ExternalInput").ap(),
            "lam4": nc.dram_tensor("lam4", [1, 256], F32, kind="ExternalInput").ap(),
            "sublng": nc.dram_tensor("sublng", [128, 1], F32, kind="ExternalInput").ap(),
            "has_prev": has_prev, "li": lambda_init, "layer": 0, "g": 0,
        }
        if has_prev:
            cfg["mixp"] = nc.dram_tensor("mixp", [D, L], BF16, kind="ExternalInput").ap()
            cfg["woutp"] = nc.dram_tensor("woutp", [D, D], F32, kind="ExternalInput").ap()
        cfg["mixo"] = nc.dram_tensor("mixo", [256, L], BF16, kind="ExternalOutput").ap()
        cfgs.append(cfg)
    has_prev = has_prev or fused
    cur = {}

    with ExitStack() as es:
        S = Sched(nc, es)
        es2 = es.enter_context(ExitStack())
        sb = lambda name, shape, dt: _sb(nc, es2, name, shape, dt)

        W = sb("W", [128, 8, NCOL], BF16)
        bW = Buf("W")
        if has_prev:
            WO = sb("WO", [128, 8, D], BF16)
            bWO = Buf("WO")
        KdT = sb("KdT", [128, L], BF16)
        KA = sb("KA", [70, L], BF16)
        KB = sb("KB", [70, L], BF16)
        Vd = sb("Vd", [128, NT, 128], BF16)
        Vf = sb("Vf", [128, NT, 192], BF16)
        bKd = [Buf(f"Kd{c}") for c in range(NCH)]
        bKA = [Buf(f"KA{c}") for c in range(NCH)]
        bKB = [Buf(f"KB{c}") for c in range(NCH)]
        bVd = [Buf(f"Vd{c}") for c in range(NCH)]
        bVf = [Buf(f"Vf{c}") for c in range(NCH)]

        qsets = []
        for qi in range(2):
            qsets.append((sb(f"QdT{qi}", [128, 512], BF16), Buf(f"Qd{qi}"),
                          sb(f"QA{qi}", [70, 512], BF16), Buf(f"QA{qi}"),
                          sb(f"QB{qi}", [70, 512], BF16), Buf(f"QB{qi}"),
                          sb(f"ZdT{qi}", [128, 512], BF16), Buf(f"Zd{qi}"),
                          sb(f"ZfT{qi}", [128, 512], BF16), Buf(f"Zf{qi}")))

        xt = [sb(f"xt{i}", [128, D], F32) for i in range(2)]
        bxt = [Buf(f"xt{i}") for i in range(2)]
        sxt = [S.new_slot(f"xt{i}") for i in range(2)]
        sh1 = [S.new_slot(f"h1o{i}") for i in range(2)]
        sutl = S.new_slot("utl")
        suts = S.new_slot("uts")
        ub = [sb(f"ub{i}", [128, D], BF16) for i in range(2)]
        bub = [Buf(f"ub{i}") for i in range(2)]
        uT = sb("uT", [128, 8, 512], BF16); buT = Buf("uT")
        stat = sb("stat", [128, 8], F32); bstat = Buf("stat")
        if has_prev:
            mp = [sb(f"mp{i}", [128, 8, 512], BF16) for i in range(1)]
            bmp = [Buf(f"mp{i}") for i in range(1)]
            smp = [S.new_slot(f"mp{i}") for i in range(1)]
        rope = [sb(f"rope{i}", [128, 2, 512], F32) for i in range(1)]
        brope = [Buf(f"rope{i}") for i in range(1)]
        srope = [S.new_slot(f"rope{i}") for i in range(1)]

        ident = sb("identS", [128, 128], BF16)
        tri = sb("triS", [128, 128], BF16)
        onesb = sb("onesb", [128, 128], BF16)
        onesf = sb("onesf", [128, 128], F32)
        gT = sb("gT", [128, 8], F32)
        cst = sb("cst", [128, 16], F32)
        lamv = sb("lamv", [128, 256], F32)
        bconst = Buf("const")
        sconst = S.new_slot("const")
        bf2 = sb("bf2", [2, 1], F32)

        rows = sb("rows", [2, 2, 512], F32); brows = Buf("rows")
        stg = sb("stg", [2, 6, 512], BF16); bstg = Buf("stg")
        onesrow = sb("onesrow", [2, 512], F32)
        carry = sb("carry", [2, 1], F32)
        saug = S.new_slot("aug")

        NP = 6
        Pb = [sb(f"P{i}", [128, 512], BF16) for i in range(NP)]
        bP = [Buf(f"P{i}") for i in range(NP)]
        ep = [sb(f"ep{i}", [128, 512], F32) for i in range(6)]
        bep = [Buf(f"ep{i}") for i in range(6)]
        t1, bt1, t2, bt2 = ep[4], bep[4], ep[5], bep[5]
        mixd = [sb(f"mixd{i}", [128, 512], BF16) for i in range(2)]
        bmixd = [Buf(f"mixd{i}") for i in range(2)]
        smixd = [S.new_slot(f"mixd{i}") for i in range(2)]
        mixf = [sb(f"mixf{i}", [128, 512], BF16) for i in range(2)]
        bmixf = [Buf(f"mixf{i}") for i in range(2)]
        smixf = [S.new_slot(f"mixf{i}") for i in range(2)]

        bank = [_ps(nc, es, f"bank{i}", [128, 512], F32) for i in range(8)]
        bbank = [Buf(f"bank{i}") for i in range(8)]

        acc = [sb(f"acc{i}", [128, 512], F32) for i in range(2)]
        bacc = [Buf(f"acc{i}") for i in range(2)]
        ones32 = sb("ones32", [128, 128], F32)

        def setup():
            first = not state.get("setup_done")
            state["setup_done"] = True
            S.dma("sp", sconst, gT[:, :], cur["normg"][:, :], writes=[bconst])
            if first:
                S.dma("sp", sconst, ident[:, :], identD[:, :], writes=[bconst], same_batch=True)
                S.dma("sp", sconst, tri[:, :], triD[:, :], writes=[bconst], same_batch=True)
            S.dma("sp", sconst, lamv[:, :], cur["lam4"][0:1, :].partition_broadcast(128), writes=[bconst], same_batch=True)
            S.dma("sp", sconst, cst[:, 4:5], cur["sublng"][:, :], writes=[bconst], same_batch=True)
            S.dma("sp", sconst, bf2[0:1, 0:1], cur["bfg"][0:1, 0:1], writes=[bconst], same_batch=True)
            S.dma("sp", sconst, bf2[1:2, 0:1], cur["bfg"][0:1, 1:2], writes=[bconst], same_batch=True)

            S.op("pool", lambda e: e.memset(carry[:, :], 0.0), writes=[brows])
            if first:
                S.op("pool", lambda e: e.memset(onesb[:, :], 1.0), writes=[bconst])
                S.op("pool", lambda e: e.memset(onesf[:, :], 1.0 / 128.0), writes=[bconst])
                S.op("pool", lambda e: e.memset(ones32[:, :], 1.0), writes=[bconst])
                S.op("pool", lambda e: e.memset(cst[:, 0:1], -0.5), writes=[bconst])
                S.op("pool", lambda e: e.memset(cst[:, 1:2], EPS), writes=[bconst])
                S.op("pool", lambda e: e.memset(onesrow[:, :], 1.0), writes=[bconst])
                S.op("pool", lambda e: e.memset(Vf[:, :, :], 1.0), writes=bVf)
                S.op("pool", lambda e: e.memset(KA[64:70, :], 1.0), writes=bKA)
                S.op("pool", lambda e: e.memset(KB[64:70, :], 1.0), writes=bKB)
                for qs_ in qsets:
                    S.op("pool", lambda e: e.memset(qs_[2][64:70, :], 1.0), writes=[qs_[3]])
                    S.op("pool", lambda e: e.memset(qs_[4][64:70, :], 1.0), writes=[qs_[5]])

            S.op("dve", lambda e: e.tensor_tensor(out=ep[2][:, 0:64], in0=lamv[:, 0:64], in1=lamv[:, 64:128], op=ALU.mult),
                 reads=[bconst], writes=[bep[2]])
            S.op("dve", lambda e: e.tensor_tensor(out=ep[2][:, 64:128], in0=lamv[:, 128:192], in1=lamv[:, 192:256],
                                                  op=ALU.mult), reads=[bconst], writes=[bep[2]])
            S.op("dve", lambda e: e.reduce_sum(out=cst[:, 5:6], in_=ep[2][:, 0:64], axis=mybir.AxisListType.X),
                 reads=[bep[2]], writes=[bconst])
            S.op("dve", lambda e: e.reduce_sum(out=cst[:, 6:7], in_=ep[2][:, 64:128], axis=mybir.AxisListType.X),
                 reads=[bep[2]], writes=[bconst])
            S.op("act", lambda e: e.activation(out=cst[:, 7:9], in_=cst[:, 5:7], func=AF.Exp), reads=[bconst], writes=[bconst])
            S.op("dve", lambda e: e.tensor_tensor(out=cst[:, 9:10], in0=cst[:, 8:9], in1=cst[:, 7:8], op=ALU.subtract),
                 reads=[bconst], writes=[bconst])
            S.op("dve", lambda e: e.tensor_scalar(out=cst[:, 3:4], in0=cst[:, 9:10], scalar1=-float(cur["li"]),
                                                  scalar2=None, op0=ALU.add), reads=[bconst], writes=[bconst])
            S.op("dve", lambda e: e.tensor_scalar(out=cst[:, 2:3], in0=cst[:, 4:5], scalar1=float(1.0 - cur["li"]),
                                                  scalar2=None, op0=ALU.mult), reads=[bconst], writes=[bconst])
            S.op("dve", lambda e: e.tensor_scalar(out=bf2[:, :], in0=bf2[:, :], scalar1=-1.0, scalar2=None, op0=ALU.mult),
                 reads=[bconst], writes=[bconst])

            for fc in range(8):
                S.dma("sp", sxt[0], xt[0][:, :], cur["win"][fc * 128:(fc + 1) * 128, 0:D], writes=[bxt[0]])
                S.dma("sp", sxt[1], xt[1][:, 0:NCOL - D], cur["win"][fc * 128:(fc + 1) * 128, D:NCOL], writes=[bxt[1]])
                S.op("dve", lambda e: e.tensor_copy(out=W[:, fc, 0:D], in_=xt[0][:, :]), reads=[bxt[0]], writes=[bW])
                S.op("dve", lambda e: e.tensor_scalar(out=W[:, fc, C_DZ:C_DZ + 256], in0=xt[0][:, C_DZ:C_DZ + 256],
                                                      scalar1=0.5, scalar2=None, op0=ALU.mult), reads=[bxt[0]], writes=[bW])
                S.op("pool", lambda e: e.tensor_copy(out=W[:, fc, D:NCOL], in_=xt[1][:, 0:NCOL - D]), reads=[bxt[1]],
                     writes=[bW])
            if cur["has_prev"]:
                for fc in range(8):
                    s = fc % 2
                    S.dma("sp", sxt[s], xt[s][:, :], cur["woutp"][fc * 128:(fc + 1) * 128, :], writes=[bxt[s]])
                    S.op("dve", lambda e: e.tensor_copy(out=WO[:, fc, :], in_=xt[s][:, :]), reads=[bxt[s]], writes=[bWO])

        state = {"xt": 0, "pb": 0, "sbank": 0, "p": 0, "mix": 0}

        def proj_bank():
            pool = (3, 5, 0, 1, 6, 7, 2, 4) if state.get("drain") else (3, 5)
            i = pool[state["pb"] % len(pool)]
            state["pb"] += 1
            return bank[i], bbank[i]

        def s_bank(pool=(0, 1)):
            i = pool[state["sbank"] % len(pool)]
            state["sbank"] += 1
            return bank[i], bbank[i]

        def project(c):
            QdT, bQd, QA, bQA, QB, bQB, ZdT, bZd, ZfT, bZf = qsets[c % 2]
            p0, w = chunk_pos(c)
            tiles = chunk_tiles(c)
            rs = 0
            S.dma("sp", srope[rs], rope[rs][:, 0, 0:w], cosT[:, p0:p0 + w], writes=[brope[rs]])
            S.dma("sp", srope[rs], rope[rs][:, 1, 0:w], sinT[:, p0:p0 + w], writes=[brope[rs]], same_batch=True)
            if cur["has_prev"]:
                ms = 0
                S.dma("sp", smp[ms], mp[ms][:, :, 0:w],
                      cur["mixp"].rearrange("(c p) t -> p c t", p=128)[:, :, p0:p0 + w], writes=[bmp[ms]])
            tinfo = []
            if cur.get("ut_load"):
                S.dma("sp", sutl, uT[:, :, 0:w], cur["utscr"][:, :, p0:p0 + w], writes=[buT])
                yield True
            else:
                for j, t in enumerate(tiles):
                    tp, n = tile_pos(t)
                    xs = state["xt"] % 2
                    state["xt"] += 1
                    tinfo.append((j, t, tp, n, xs))
            def load(k):
                j, t, tp, n, xs = tinfo[k]
                S.dma("sp", sxt[xs], xt[xs][0:n, :], cur["hin"][tp:tp + n, :], writes=[bxt[xs]])
            if tinfo:
                load(0)
                yield True
            for k, (j, t, tp, n, xs) in enumerate(tinfo):
                X, bX = xt[xs], bxt[xs]
                U, bU = ub[xs], bub[xs]
                if k + 1 < len(tinfo):
                    load(k + 1)
                if cur["has_prev"]:
                    for half in range(2):
                        pb, bpb = proj_bank()
                        for fc in range(8):
                            S.op("pe", lambda e: e.matmul(
                                pb[0:n, :], lhsT=mp[ms][:, fc, 128 * j:128 * j + n],
                                rhs=WO[:, fc, half * 512:(half + 1) * 512], start=(fc == 0), stop=(fc == 7)),
                                reads=[bmp[ms], bWO], writes=[bpb])
                        yield False
                        S.op("dve", lambda e: e.tensor_tensor(
                            out=X[0:n, half * 512:(half + 1) * 512], in0=pb[0:n, :],
                            in1=X[0:n, half * 512:(half + 1) * 512], op=ALU.add), reads=[bpb, bX], writes=[bX])
                        yield True
                if cur.get("h1out") is not None:
                    S.dma("pool", sh1[xs], cur["h1out"][tp:tp + n, :], X[0:n, :], reads=[bX])
                for hf in range(2):
                    S.op("dve", lambda e: e.tensor_tensor(out=ep[2 + hf][0:n, :], in0=X[0:n, 512 * hf:512 * hf + 512],
                                                          in1=X[0:n, 512 * hf:512 * hf + 512], op=ALU.mult),
                         reads=[bX], writes=[bep[2 + hf]])
                    S.op("dve", lambda e: e.reduce_sum(out=stat[0:n, 4 + hf:5 + hf], in_=ep[2 + hf][0:n, :],
                                                       axis=mybir.AxisListType.X), reads=[bep[2 + hf]], writes=[bstat])
                S.op("dve", lambda e: e.tensor_tensor(out=stat[0:n, 0:1], in0=stat[0:n, 4:5], in1=stat[0:n, 5:6], op=ALU.add),
                     reads=[bstat], writes=[bstat])
                S.op("dve", lambda e: e.tensor_scalar(out=stat[0:n, 1:2], in0=stat[0:n, 0:1], scalar1=1.0 / D,
                                                      scalar2=EPS, op0=ALU.mult, op1=ALU.add),
                     reads=[bstat], writes=[bstat])
                yield True
                S.op("act", lambda e: e.activation(out=stat[0:n, 3:4], in_=stat[0:n, 1:2], func=AF.Ln),
                     reads=[bstat], writes=[bstat])
                S.op("act", lambda e: e.activation(out=stat[0:n, 2:3], in_=stat[0:n, 3:4], func=AF.Exp, scale=-0.5),
                     reads=[bstat], writes=[bstat])
                S.op("act", lambda e: e.activation(out=U[0:n, :], in_=X[0:n, :], func=AF.Copy, scale=stat[0:n, 2:3]),
                     reads=[bX, bstat], writes=[bU])
                yield True
                tb, btb = proj_bank()
                tbv = tb[:, :].bitcast(BF16)
                for fc in range(8):
                    S.op("pe", lambda e: e.transpose(out=tbv[:, fc * 128:fc * 128 + n],
                                                     in_=U[0:n, fc * 128:(fc + 1) * 128], identity=ident[0:n, 0:n]),
                         reads=[bU, bconst], writes=[btb])
                yield False
                for fc in range(8):
                    S.op("dve", lambda e: e.tensor_scalar(out=uT[:, fc, 128 * j:128 * j + n],
                                                          in0=tbv[:, fc * 128:fc * 128 + n],
                                                          scalar1=gT[:, fc:fc + 1], scalar2=None, op0=ALU.mult),
                         reads=[btb, bconst], writes=[buT])
                yield True
            if cur.get("ut_store"):
                S.dma("pool", suts, cur["utscr"][:, :, p0:p0 + w], uT[:, :, 0:w], reads=[buT])

            def fm(col, M):
                pb, bpb = proj_bank()
                for fc in range(8):
                    S.op("pe", lambda e: e.matmul(pb[0:M, 0:w], lhsT=W[:, fc, col:col + M], rhs=uT[:, fc, 0:w],
                                                  start=(fc == 0), stop=(fc == 7)),
                         reads=[bW, buT], writes=[bpb])
                return pb, bpb

            for (cn, cs, dst, bdst, dcol) in ((C_DQ, C_DQS, QdT, bQd, 0), (C_DK, C_DKS, KdT, bKd[c], p0)):
                pa, bpa = fm(cn, 128)
                pbk, bpbk = fm(cs, 128)
                yield False
                S.op("dve", lambda e: e.tensor_tensor(out=t1[:, 0:w], in0=pa[:, 0:w], in1=rope[rs][:, 0, 0:w],
                                                      op=ALU.mult), reads=[bpa, brope[rs]], writes=[bt1])
                S.op("dve", lambda e: e.tensor_tensor(out=t2[:, 0:w], in0=pbk[:, 0:w], in1=rope[rs][:, 1, 0:w],
                                                      op=ALU.mult), reads=[bpbk, brope[rs]], writes=[bt2])
                S.op("dve", lambda e: e.tensor_tensor(out=dst[:, dcol:dcol + w], in0=t1[:, 0:w], in1=t2[:, 0:w],
                                                      op=ALU.add), reads=[bt1, bt2], writes=[bdst])
                yield True
            for (col, dA, bdA, dB, bdB, dcol) in ((C_FQA, QA, bQA, QB, bQB, 0), (C_FKA, KA, bKA[c], KB, bKB[c], p0)):
                pa, bpa = fm(col, 128)
                yield False
                S.op("dve", lambda e: e.tensor_copy(out=dA[0:64, dcol:dcol + w], in_=pa[0:64, 0:w]),
                     reads=[bpa], writes=[bdA])
                S.op("dve", lambda e: e.tensor_copy(out=dB[0:64, dcol:dcol + w], in_=pa[64:128, 0:w]),
                     reads=[bpa], writes=[bdB])
                yield True
            for (col, dst, bdst) in ((C_DZ, ZdT, bZd), (C_FZ, ZfT, bZf)):
                pa, bpa = fm(col, 128)
                yield False
                S.op("act", lambda e: e.activation(out=t1[:, 0:w], in_=pa[:, 0:w], func=AF.Tanh),
                     reads=[bpa], writes=[bt1])
                yield False
                S.op("dve", lambda e: e.scalar_tensor_tensor(out=dst[:, 0:w], in0=t1[:, 0:w], scalar=1.0, in1=pa[:, 0:w],
                                                             op0=ALU.add, op1=ALU.mult),
                     reads=[bt1, bpa], writes=[bdst])
                yield True
            pa, bpa = fm(C_FLA, 2)
            yield False
            ra, rb = rows[:, 0, 0:w], rows[:, 1, 0:w]
            S.op("act", lambda e: e.activation(out=ra, in_=pa[0:2, 0:w], func=AF.Exp, scale=-1.0, bias=bf2[0:2, 0:1]),
                 reads=[bpa, bconst], writes=[brows])
            S.op("act", lambda e: e.activation(out=ra, in_=ra, func=AF.Ln, bias=1.0, scale=1.0),
                 reads=[brows], writes=[brows])
            yield True
            S.op("dve", lambda e: e.tensor_tensor_scan(out=rb, data0=onesrow[:, 0:w], data1=ra, initial=carry[:, 0:1],
                                                       op0=ALU.mult, op1=ALU.subtract),
                 reads=[brows, bconst], writes=[brows])
            S.op("dve", lambda e: e.tensor_copy(out=carry[:, 0:1], in_=rows[:, 1, w - 1:w]), reads=[brows], writes=[brows])
            S.op("dve", lambda e: e.tensor_scalar(out=ra, in0=rb, scalar1=8.0, scalar2=None, op0=ALU.mult),
                 reads=[brows], writes=[brows])
            S.op("dve", lambda e: e.tensor_copy(out=stg[:, 0, 0:w], in_=ra), reads=[brows], writes=[bstg])
            S.op("dve", lambda e: e.tensor_tensor(out=rb, in0=ra, in1=stg[:, 0, 0:w], op=ALU.subtract),
                 reads=[brows, bstg], writes=[brows])
            S.op("dve", lambda e: e.tensor_copy(out=stg[:, 1, 0:w], in_=rb), reads=[brows], writes=[bstg])
            S.op("dve", lambda e: e.tensor_tensor(out=ra, in0=rb, in1=stg[:, 1, 0:w], op=ALU.subtract),
                 reads=[brows, bstg], writes=[brows])
            S.op("dve", lambda e: e.tensor_copy(out=stg[:, 2, 0:w], in_=ra), reads=[brows], writes=[bstg])
            S.op("dve", lambda e: e.tensor_scalar(out=stg[:, 3:6, 0:w], in0=stg[:, 0:3, 0:w], scalar1=-1.0,
                                                  scalar2=None, op0=ALU.mult), reads=[bstg], writes=[bstg])
            yield True
            first = True
            for hh in range(2):
                Qt, bQt = (QA, bQA) if hh == 0 else (QB, bQB)
                Kt, bKt = (KA, bKA[c]) if hh == 0 else (KB, bKB[c])
                for i in range(3):
                    S.dma("sp", saug, Qt[64 + i:65 + i, 0:w], stg[hh:hh + 1, i, 0:w], reads=[bstg], writes=[bQt],
                          same_batch=not first)
                    first = False
                    S.dma("sp", saug, Kt[67 + i:68 + i, p0:p0 + w], stg[hh:hh + 1, 3 + i, 0:w], reads=[bstg], writes=[bKt],
                          same_batch=True)
            yield True
            for j, t in enumerate(tiles):
                tp, n = tile_pos(t)
                pb, bpb = proj_bank()
                for fc in range(8):
                    S.op("pe", lambda e: e.matmul(pb[0:n, 0:256], lhsT=uT[:, fc, 128 * j:128 * j + n],
                                                  rhs=W[:, fc, C_V:C_V + 256], start=(fc == 0), stop=(fc == 7)),
                         reads=[bW, buT], writes=[bpb])
                yield False
                S.op("dve", lambda e: e.tensor_copy(out=Vd[0:n, t, :], in_=pb[0:n, 0:128]), reads=[bpb], writes=[bVd[c]])
                S.op("dve", lambda e: e.tensor_copy(out=Vf[0:n, t, 0:64], in_=pb[0:n, 128:192]), reads=[bpb],
                     writes=[bVf[c]])
                S.op("dve", lambda e: e.tensor_copy(out=Vf[0:n, t, 128:192], in_=pb[0:n, 192:256]), reads=[bpb],
                     writes=[bVf[c]])
                yield True

        def attention(c, nxt=None):
            QdT, bQd, QA, bQA, QB, bQB, ZdT, bZd, ZfT, bZf = qsets[c % 2]
            p0, w = chunk_pos(c)
            last_t = chunk_tiles(c)[-1]
            kts = list(range(0, last_t + 1))
            first_diag = chunk_tiles(c)[0]

            def kinfo(kt):
                kp, kn = tile_pos(kt)
                if kt >= first_diag:
                    i = kt - first_diag
                    return kp, kn, 128 * i, True
                return kp, kn, 0, False

            def kchunk(kt):
                return 0 if kt == 0 else (kt - 1) // 4 + 1

            O1, bO1 = bank[2], bbank[2]
            L1, bL1 = bank[3], bbank[3]
            O2, bO2 = bank[4], bbank[4]
            L2, bL2 = bank[5], bbank[5]
            OA, bOA = bank[6], bbank[6]
            OB, bOB = bank[7], bbank[7]

            steps = []
            for kt in kts:
                steps.append(("d1", kt))
                steps.append(("d2", kt))
            for kt in kts:
                steps.append(("fA", kt))
            for kt in kts:
                steps.append(("fB", kt))
            pend = {}

            def qk(si):
                u, kt = steps[si]
                kp, kn, qlo, diag = kinfo(kt)
                kc = kchunk(kt)
                if c == 0:
                    spool = (0, 1)
                elif u in ("d1", "d2"):
                    spool = (0, 1, 6, 7)
                elif u == "fA":
                    spool = (0, 1, 7)
                else:
                    spool = (0, 1, 2, 4)
                sbk, bsbk = s_bank(spool)
                pi = state["p"] % NP
                state["p"] += 1
                P, bPi = Pb[pi], bP[pi]
                if u == "d1":
                    lhsT, rhs, rd = KdT[0:64, kp:kp + kn], QdT[0:64, qlo:w], [bKd[kc], bQd]
                elif u == "d2":
                    lhsT, rhs, rd = KdT[64:128, kp:kp + kn], QdT[64:128, qlo:w], [bKd[kc], bQd]
                elif u == "fA":
                    lhsT, rhs, rd = KA[0:70, kp:kp + kn], QA[0:70, qlo:w], [bKA[kc], bQA]
                else:
                    lhsT, rhs, rd = KB[0:70, kp:kp + kn], QB[0:70, qlo:w], [bKB[kc], bQB]
                S.op("pe", lambda e: e.matmul(sbk[0:kn, qlo:w], lhsT=lhsT, rhs=rhs, start=True, stop=True),
                     reads=rd, writes=[bsbk])
                pend[si] = (P, bPi, sbk, bsbk)

            def qk_post(si):
                u, kt = steps[si]
                kp, kn, qlo, diag = kinfo(kt)
                P, bPi, sbk, bsbk = pend[si]
                S.op("act", lambda e: e.activation(out=P[0:kn, qlo:w], in_=sbk[0:kn, qlo:w], func=AF.Exp, scale=0.125),
                     reads=[bsbk], writes=[bPi])
                if diag:
                    dw = min(128, w - qlo)
                    S.op("pool", lambda e: e.tensor_tensor(out=P[0:kn, qlo:qlo + dw], in0=P[0:kn, qlo:qlo + dw],
                                                           in1=tri[0:kn, 0:dw], op=ALU.mult),
                         reads=[bPi, bconst], writes=[bPi])

            def pv(si):
                u, kt = steps[si]
                kp, kn, qlo, diag = kinfo(kt)
                kc = kchunk(kt)
                P, bPi = pend.pop(si)[0:2]
                st, sp_ = (kt == kts[0]), (kt == kts[-1])
                rhs = P[0:kn, qlo:w]
                if u in ("d1", "d2"):
                    O, bO, Lb, bLb = (O1, bO1, L1, bL1) if u == "d1" else (O2, bO2, L2, bL2)
                    S.op("pe", lambda e: e.matmul(O[:, qlo:w], lhsT=Vd[0:kn, kt, :], rhs=rhs, start=st, stop=sp_),
                         reads=[bVd[kc], bPi], writes=[bO])
                    ai = 0 if u == "d1" else 1
                    S.op("dve", lambda e: e.tensor_tensor(out=acc[ai][0:kn, qlo:w], in0=acc[ai][0:kn, qlo:w],
                                                          in1=P[0:kn, qlo:w], op=ALU.add),
                         reads=[bPi, bacc[ai]], writes=[bacc[ai]])
                elif u == "fA":
                    S.op("pe", lambda e: e.matmul(OA[:, qlo:w], lhsT=Vf[0:kn, kt, 0:128], rhs=rhs, start=st, stop=sp_),
                         reads=[bVf[kc], bPi], writes=[bOA])
                else:
                    S.op("pe", lambda e: e.matmul(OB[:, qlo:w], lhsT=Vf[0:kn, kt, 64:192], rhs=rhs, start=st, stop=sp_),
                         reads=[bVf[kc], bPi], writes=[bOB])

            for ai in range(2):
                S.op("pool", lambda e: e.memset(acc[ai][:, 0:w], 0.0), writes=[bacc[ai]])
            LA = 2 if c >= 1 else 1
            nd = 2 * len(kts)
            nfa = nd + len(kts)
            groups = []
            i = 0
            while i < len(steps):
                if steps[i][0] == "d1":
                    groups.append([i, i + 1])
                    i += 2
                else:
                    groups.append([i])
                    i += 1
            per = max(1, -(-70 // len(groups)))
            pst = {"clean": True}

            def pull(k):
                if nxt is None:
                    return
                for _ in range(k):
                    r = next(nxt, None)
                    if r is None:
                        pst["clean"] = True
                        return
                    pst["clean"] = r

            def make_clean():
                while not pst["clean"]:
                    pull(1)

            for gi in range(len(groups) + LA):
                if gi > 0:
                    pull(per)
                if gi < len(groups):
                    for si in groups[gi]:
                        qk(si)
                    for si in groups[gi]:
                        qk_post(si)
                if gi - LA >= 0:
                    for si in groups[gi - LA]:
                        pv(si)
                        done = si + 1
                        if done in (nd, nfa, len(steps)):
                            make_clean()
                        if done == nd:
                            epi_diff(c, p0, w, O1, bO1, L1, bL1, O2, bO2, L2, bL2)
                        elif done == nfa:
                            epi_fox_a(c, p0, w, OA, bOA)
                        elif done == len(steps):
                            epi_fox_b(c, p0, w, OB, bOB)

        def epi_diff(c, p0, w, O1, bO1, L1, bL1, O2, bO2, L2, bL2):
            QdT, bQd, QA, bQA, QB, bQB, ZdT, bZd, ZfT, bZf = qsets[c % 2]
            r1, r2, a, b, o, sq = (ep[i] for i in range(6))
            br1, br2, ba, bb, bo, bsq = (bep[i] for i in range(6))
            S.op("pe", lambda e: e.matmul(L1[:, 0:w], lhsT=ones32[:, :], rhs=acc[0][:, 0:w], start=True, stop=True),
                 reads=[bacc[0], bconst], writes=[bL1])
            S.op("pe", lambda e: e.matmul(L2[:, 0:w], lhsT=ones32[:, :], rhs=acc[1][:, 0:w], start=True, stop=True),
                 reads=[bacc[1], bconst], writes=[bL2])
            S.op("dve", lambda e: e.reciprocal(out=r1[:, 0:w], in_=L1[:, 0:w]), reads=[bL1], writes=[br1])
            S.op("dve", lambda e: e.reciprocal(out=r2[:, 0:w], in_=L2[:, 0:w]), reads=[bL2], writes=[br2])
            S.op("dve", lambda e: e.tensor_tensor(out=a[:, 0:w], in0=O1[:, 0:w], in1=r1[:, 0:w], op=ALU.mult),
                 reads=[bO1, br1], writes=[ba])
            S.op("dve", lambda e: e.tensor_tensor(out=b[:, 0:w], in0=O2[:, 0:w], in1=r2[:, 0:w], op=ALU.mult),
                 reads=[bO2, br2], writes=[bb])
            S.op("dve", lambda e: e.scalar_tensor_tensor(out=o[:, 0:w], in0=b[:, 0:w], scalar=cst[:, 3:4], in1=a[:, 0:w],
                                                         op0=ALU.mult, op1=ALU.add), reads=[ba, bb, bconst], writes=[bo])
            S.op("pool", lambda e: e.tensor_tensor(out=sq[:, 0:w], in0=o[:, 0:w], in1=o[:, 0:w], op=ALU.mult),
                 reads=[bo], writes=[bsq])
            mb, bmb = s_bank()
            S.op("pe", lambda e: e.matmul(mb[:, 0:w], lhsT=onesf[:, :], rhs=sq[:, 0:w], start=True, stop=True),
                 reads=[bsq, bconst], writes=[bmb])
            S.op("dve", lambda e: e.tensor_scalar(out=r1[:, 0:w], in0=mb[:, 0:w], scalar1=EPS, scalar2=None, op0=ALU.add),
                 reads=[bmb], writes=[br1])
            S.op("act", lambda e: e.activation(out=r1[:, 0:w], in_=r1[:, 0:w], func=AF.Ln), reads=[br1], writes=[br1])
            S.op("act", lambda e: e.activation(out=r2[:, 0:w], in_=r1[:, 0:w], func=AF.Exp, scale=-0.5),
                 reads=[br1], writes=[br2])
            S.op("dve", lambda e: e.tensor_tensor(out=a[:, 0:w], in0=o[:, 0:w], in1=r2[:, 0:w], op=ALU.mult),
                 reads=[bo, br2], writes=[ba])
            mi = state["mix"] % 2
            S.op("dve", lambda e: e.scalar_tensor_tensor(out=mixd[mi][:, 0:w], in0=a[:, 0:w], scalar=cst[:, 2:3],
                                                         in1=ZdT[:, 0:w], op0=ALU.mult, op1=ALU.mult),
                 reads=[ba, bconst, bZd, bmixd[mi]], writes=[bmixd[mi]])
            S.dma("pool", smixd[mi], cur["mixo"][0:128, p0:p0 + w], mixd[mi][:, 0:w], reads=[bmixd[mi]])

        def epi_fox_a(c, p0, w, OA, bOA):
            QdT, bQd, QA, bQA, QB, bQB, ZdT, bZd, ZfT, bZf = qsets[c % 2]
            mi = state["mix"] % 2
            rl, t = ep[0], ep[1]
            brl, bt = bep[0], bep[1]
            S.op("dve", lambda e: e.reciprocal(out=rl[0:64, 0:w], in_=OA[64:128, 0:w]), reads=[bOA], writes=[brl])
            S.op("dve", lambda e: e.tensor_tensor(out=t[0:64, 0:w], in0=OA[0:64, 0:w], in1=rl[0:64, 0:w], op=ALU.mult),
                 reads=[bOA, brl], writes=[bt])
            S.op("pool", lambda e: e.tensor_tensor(out=mixf[mi][0:64, 0:w], in0=t[0:64, 0:w], in1=ZfT[0:64, 0:w],
                                                   op=ALU.mult), reads=[bt, bZf, bmixf[mi]], writes=[bmixf[mi]])

        def epi_fox_b(c, p0, w, OB, bOB):
            QdT, bQd, QA, bQA, QB, bQB, ZdT, bZd, ZfT, bZf = qsets[c % 2]
            mi = state["mix"] % 2
            rl, t = ep[2], ep[3]
            brl, bt = bep[2], bep[3]
            S.op("dve", lambda e: e.reciprocal(out=rl[64:128, 0:w], in_=OB[0:64, 0:w]), reads=[bOB], writes=[brl])
            S.op("dve", lambda e: e.tensor_tensor(out=t[64:128, 0:w], in0=OB[64:128, 0:w], in1=rl[64:128, 0:w],
                                                  op=ALU.mult), reads=[bOB, brl], writes=[bt])
            S.op("pool", lambda e: e.tensor_tensor(out=mixf[mi][64:128, 0:w], in0=t[64:128, 0:w], in1=ZfT[64:128, 0:w],
                                                   op=ALU.mult), reads=[bt, bZf, bmixf[mi]], writes=[bmixf[mi]])
            S.dma("pool", smixf[mi], cur["mixo"][128:256, p0:p0 + w], mixf[mi][:, 0:w], reads=[bmixf[mi]])
            state["mix"] += 1

        for ci, cfg in enumerate(cfgs):
            if fused and ci in (1, 4, 5):
                S.barrier()
            cur.clear()
            cur.update(cfg)
            setup()
            state["drain"] = True
            for _ in project(0):
                pass
            state["drain"] = False
            for c in range(nchunks):
                nxt = project(c + 1) if c + 1 < nchunks else None
                attention(c, None if c < SERIAL_CUT else nxt)
                if nxt is not None:
                    state["drain"] = True
                    for _ in nxt:
                        pass
                    state["drain"] = False
        if fused:
            S.barrier()
            es2.close()
            emit_final(nc, es, S, bank, bbank, h1scr, [(mixall[1], woutp1)], fg, outD)
        S.finish("sp")
    return nc


def emit_final(nc, es, S, bank, bbank, hin, pairs, fg, out):
    NPR = len(pairs)
    sb = lambda name, shape, dt: _sb(nc, es, name, shape, dt)
    WO = [sb(f"fWO{i}", [128, 8, D], BF16) for i in range(NPR)]
    bWO = Buf("fWO")
    wst = [sb(f"fwst{i}", [128, D], F32) for i in range(2)]
    bwst = [Buf(f"fwst{i}") for i in range(2)]
    swst = [S.new_slot(f"fwst{i}") for i in range(2)]
    fgt = sb("fgt", [128, D], F32); bconst = Buf("fconst"); sconst = S.new_slot("fconst")
    cst = sb("fcst", [128, 2], F32)
    xt = [sb(f"fxt{i}", [128, D], F32) for i in range(2)]
    bxt = [Buf(f"fxt{i}") for i in range(2)]
    sxt = [S.new_slot(f"fxt{i}") for i in range(2)]
    mt = [sb(f"fmt{i}", [128, 2, 8, 128], BF16) for i in range(2)]
    bmt = [Buf(f"fmt{i}") for i in range(2)]
    smt = [S.new_slot(f"fmt{i}") for i in range(2)]
    ot = [sb(f"fot{i}", [128, D], F32) for i in range(2)]
    bot = [Buf(f"fot{i}") for i in range(2)]
    sot = [S.new_slot(f"fot{i}") for i in range(2)]
    junk = sb("fjunk", [128, D], F32); bjunk = Buf("fjunk")
    stat = sb("fstat", [128, 4], F32); bstat = Buf("fstat")
    S.dma("sp", sconst, fgt[:, :], fg[0:1, :].partition_broadcast(128), writes=[bconst])
    S.op("pool", lambda e: e.memset(cst[:, 0:1], -0.5), writes=[bconst])
    wi = 0
    for li, wsrc in enumerate([p[1] for p in pairs]):
        for fc in range(8):
            s = wi % 2; wi += 1
            S.dma("sp", swst[s], wst[s][:, :], wsrc[fc * 128:(fc + 1) * 128, :], writes=[bwst[s]])
            S.op("dve", lambda e: e.tensor_copy(out=WO[li][:, fc, :], in_=wst[s][:, :]), reads=[bwst[s]], writes=[bWO])
    pbi = 0
    for t in range(SEQ // 128):
        s = t % 2
        X = xt[s]
        p0 = NMETA + 128 * t
        S.dma("sp", sxt[s], X[:, :], hin[p0:p0 + 128, :], writes=[bxt[s]])
        for li in range(NPR):
            S.dma("sp", smt[s], mt[s][:, li, :, :], pairs[li][0].rearrange("(c p) t -> p c t", p=128)[:, :, p0:p0 + 128],
                  writes=[bmt[s]], same_batch=(li > 0))
        for half in range(2):
            pb, bpb = bank[pbi % 4], bbank[pbi % 4]
            pbi += 1
            k = 0
            for li in range(NPR):
                for fc in range(8):
                    S.op("pe", lambda e: e.matmul(pb[:, :], lhsT=mt[s][:, li, fc, :], rhs=WO[li][:, fc, half * 512:(half + 1) * 512],
                                                  start=(k == 0), stop=(k == 8 * NPR - 1)), reads=[bmt[s], bWO], writes=[bpb])
                    k += 1
            S.op("dve", lambda e: e.tensor_tensor(out=X[:, half * 512:(half + 1) * 512], in0=pb[:, :],
                                                  in1=X[:, half * 512:(half + 1) * 512], op=ALU.add),
                 reads=[bpb, bxt[s]], writes=[bxt[s]])
        S.op("dve", lambda e: e.tensor_tensor(out=junk[:, :], in0=X[:, :], in1=X[:, :], op=ALU.mult), reads=[bxt[s]], writes=[bjunk])
        S.op("dve", lambda e: e.reduce_sum(out=stat[:, 0:1], in_=junk[:, :], axis=mybir.AxisListType.X), reads=[bjunk], writes=[bstat])
        S.op("dve", lambda e: e.tensor_scalar(out=stat[:, 1:2], in0=stat[:, 0:1], scalar1=1.0 / D, scalar2=EPS,
                                              op0=ALU.mult, op1=ALU.add), reads=[bstat], writes=[bstat])
        S.op("act", lambda e: e.activation(out=stat[:, 3:4], in_=stat[:, 1:2], func=AF.Ln), reads=[bstat], writes=[bstat])
        S.op("act", lambda e: e.activation(out=stat[:, 2:3], in_=stat[:, 3:4], func=AF.Exp, scale=-0.5),
             reads=[bstat], writes=[bstat])
        S.op("dve", lambda e: e.scalar_tensor_tensor(out=ot[s][:, :], in0=X[:, :], scalar=stat[:, 2:3], in1=fgt[:, :],
                                                     op0=ALU.mult, op1=ALU.mult),
             reads=[bxt[s], bstat, bconst, bot[s]], writes=[bot[s]])
        S.dma("pool", sot[s], out[t * 128:(t + 1) * 128, :], ot[s][:, :], reads=[bot[s]])


def build_final():
    NTOK = 2048
    nc = bass.Bass("TRN2", target_bir_lowering=False)
    xq = nc.dram_tensor("xq", [NTOK, D], F32, kind="ExternalInput").ap()
    m1 = nc.dram_tensor("m1", [D, NTOK], BF16, kind="ExternalInput").ap()
    m2 = nc.dram_tensor("m2", [D, NTOK], BF16, kind="ExternalInput").ap()
    wo0 = nc.dram_tensor("wo0", [D, D], F32, kind="ExternalInput").ap()
    wo1 = nc.dram_tensor("wo1", [D, D], F32, kind="ExternalInput").ap()
    fg = nc.dram_tensor("fg", [1, D], F32, kind="ExternalInput").ap()
    out = nc.dram_tensor("out", [NTOK, D], F32, kind="ExternalOutput").ap()
    with ExitStack() as es:
        S = Sched(nc, es)
        sb = lambda name, shape, dt: _sb(nc, es, name, shape, dt)
        WO = [sb(f"WO{i}", [128, 8, D], BF16) for i in range(2)]
        bWO = Buf("WO")
        wst = [sb(f"wst{i}", [128, D], F32) for i in range(2)]
        bwst = [Buf(f"wst{i}") for i in range(2)]
        swst = [S.new_slot(f"wst{i}") for i in range(2)]
        fgt = sb("fgt", [128, D], F32); bconst = Buf("const"); sconst = S.new_slot("const")
        cst = sb("cst", [128, 2], F32)
        xt = [sb(f"xt{i}", [128, D], F32) for i in range(2)]
        bxt = [Buf(f"xt{i}") for i in range(2)]
        sxt = [S.new_slot(f"xt{i}") for i in range(2)]
        mt = [sb(f"mt{i}", [128, 2, 8, 128], BF16) for i in range(2)]
        bmt = [Buf(f"mt{i}") for i in range(2)]
        smt = [S.new_slot(f"mt{i}") for i in range(2)]
        ot = [sb(f"ot{i}", [128, D], F32) for i in range(2)]
        bot = [Buf(f"ot{i}") for i in range(2)]
        sot = [S.new_slot(f"ot{i}") for i in range(2)]
        junk = sb("junk", [128, D], F32); bjunk = Buf("junk")
        stat = sb("stat", [128, 4], F32); bstat = Buf("stat")
        bank = [_ps(nc, es, f"bank{i}", [128, 512], F32) for i in range(4)]
        bbank = [Buf(f"bank{i}") for i in range(4)]

        S.dma("sp", sconst, fgt[:, :], fg[0:1, :].partition_broadcast(128), writes=[bconst])
        S.op("pool", lambda e: e.memset(cst[:, 0:1], -0.5), writes=[bconst])
        wi = 0
        for li, wsrc in enumerate((wo0, wo1)):
            for fc in range(8):
                s = wi % 2; wi += 1
                S.dma("sp", swst[s], wst[s][:, :], wsrc[fc * 128:(fc + 1) * 128, :], writes=[bwst[s]])
                S.op("dve", lambda e, s=s, fc=fc, li=li: e.tensor_copy(out=WO[li][:, fc, :], in_=wst[s][:, :]),
                     reads=[bwst[s]], writes=[bWO])
        pbi = 0
        for t in range(NTOK // 128):
            s = t % 2
            X = xt[s]
            S.dma("sp", sxt[s], X[:, :], xq[t * 128:(t + 1) * 128, :], writes=[bxt[s]])
            S.dma("sp", smt[s], mt[s][:, 0, :, :], m1.rearrange("(c p) t -> p c t", p=128)[:, :, t * 128:(t + 1) * 128],
                  writes=[bmt[s]])
            S.dma("sp", smt[s], mt[s][:, 1, :, :], m2.rearrange("(c p) t -> p c t", p=128)[:, :, t * 128:(t + 1) * 128],
                  writes=[bmt[s]], same_batch=True)
            for half in range(2):
                pb, bpb = bank[pbi % 4], bbank[pbi % 4]
                pbi += 1
                k = 0
                for li in range(2):
                    for fc in range(8):
                        S.op("pe", lambda e, li=li, fc=fc, k=k: e.matmul(pb[:, :], lhsT=mt[s][:, li, fc, :],
                                                                          rhs=WO[li][:, fc, half * 512:(half + 1) * 512],
                                                                          start=(k == 0), stop=(k == 15)),
                             reads=[bmt[s], bWO], writes=[bpb])
                        k += 1
                S.op("dve", lambda e: e.tensor_tensor(out=X[:, half * 512:(half + 1) * 512], in0=pb[:, :],
                                                      in1=X[:, half * 512:(half + 1) * 512], op=ALU.add),
                     reads=[bpb, bxt[s]], writes=[bxt[s]])
            S.op("dve", lambda e: e.tensor_tensor(out=junk[:, :], in0=X[:, :], in1=X[:, :], op=ALU.mult),
                 reads=[bxt[s]], writes=[bjunk])
            S.op("dve", lambda e: e.reduce_sum(out=stat[:, 0:1], in_=junk[:, :], axis=mybir.AxisListType.X),
                 reads=[bjunk], writes=[bstat])
            S.op("dve", lambda e: e.tensor_scalar(out=stat[:, 1:2], in0=stat[:, 0:1], scalar1=1.0 / D, scalar2=EPS,
                                                  op0=ALU.mult, op1=ALU.add), reads=[bstat], writes=[bstat])
            S.op("act", lambda e: e.activation(out=stat[:, 3:4], in_=stat[:, 1:2], func=AF.Ln), reads=[bstat], writes=[bstat])
            S.op("act", lambda e: e.activation(out=stat[:, 2:3], in_=stat[:, 3:4], func=AF.Exp, scale=-0.5),
                 reads=[bstat], writes=[bstat])
            S.op("dve", lambda e: e.scalar_tensor_tensor(out=ot[s][:, :], in0=X[:, :], scalar=stat[:, 2:3], in1=fgt[:, :],
                                                         op0=ALU.mult, op1=ALU.mult),
                 reads=[bxt[s], bstat, bconst, bot[s]], writes=[bot[s]])
            S.dma("pool", sot[s], out[t * 128:(t + 1) * 128, :], ot[s][:, :], reads=[bot[s]])
        S.finish("sp")
    return nc


def _core_cols(g):
    dq = [g * 128 + r for r in range(128)]
    dqs = [g * 128 + (r // 64) * 64 + ((r % 64) + 32) % 64 for r in range(128)]
    dk = [512 + x for x in dq]
    dks = [512 + x for x in dqs]
    dv = [1024 + g * 128 + r for r in range(128)]
    a, b = 2 * g, 2 * g + 1
    fva = [3072 + a * 64 + r for r in range(64)]
    fvb = [3072 + b * 64 + r for r in range(64)]
    dz = [1536 + g * 128 + r for r in range(128)]
    fz = [3584 + a * 64 + r for r in range(128)]
    fqa = [2048 + a * 64 + r for r in range(64)]
    fqb = [2048 + b * 64 + r for r in range(64)]
    fka = [2560 + a * 64 + r for r in range(64)]
    fkb = [2560 + b * 64 + r for r in range(64)]
    fl = [4096 + a, 4096 + b]
    cols = dq + dqs + dk + dks + dv + fva + fvb + dz + fz + fqa + fqb + fka + fkb + fl
    assert len(cols) == NCOL
    return np.array(cols)


def _wout_perm():
    rows = []
    for g in range(4):
        rows += list(range(g * 128, (g + 1) * 128))
        rows += list(range(512 + g * 128, 512 + (g + 1) * 128))
    return np.array(rows)


def _rope_tables():
    half = 32
    inv = (10000.0 ** (-np.arange(half, dtype=np.float32) / half)).astype(np.float32)
    pos = np.arange(L, dtype=np.float32)
    ang = (pos[:, None] * inv[None, :]).astype(np.float32)
    cos = np.cos(ang).astype(np.float32).T
    sin = np.sin(ang).astype(np.float32).T
    cosT = np.concatenate([cos, cos, cos, cos], axis=0)
    sinT = np.concatenate([-sin, sin, -sin, sin], axis=0)
    return np.ascontiguousarray(cosT), np.ascontiguousarray(sinT)


_PROGS = {}
SERIAL_CUT = 0
FUSED = True


def _prog(key, fn):
    if key not in _PROGS:
        _PROGS[key] = fn()
    return _PROGS[key]


def kernel(x, meta_tokens, norm_g, w_in, b_forget, lam_q1, lam_k1, lam_q2, lam_k2, subln_g, w_out, final_g):
    x = np.asarray(x, np.float32)
    f = lambda a: np.asarray(a, np.float32)
    meta_tokens, norm_g, w_in, b_forget = f(meta_tokens), f(norm_g), f(w_in), f(b_forget)
    lam_q1, lam_k1, lam_q2, lam_k2, subln_g, w_out, final_g = map(f, (lam_q1, lam_k1, lam_q2, lam_k2, subln_g, w_out, final_g))
    cores = list(range(8))
    cosT, sinT = _rope_tables()
    ident = np.eye(128, dtype=np.float32).astype(NPB)
    tri = np.triu(np.ones((128, 128), np.float32)).astype(NPB)
    perm = _wout_perm()
    hin = [np.ascontiguousarray(np.concatenate([meta_tokens, x[b]], axis=0)) for b in range(2)]

    def layer_maps(layer, mix_prev):
        maps = []
        for core in cores:
            b, g = core // 4, core % 4
            m = {
                "hin": hin[b],
                "win": np.ascontiguousarray(w_in[layer][:, _core_cols(g)]),
                "normg": np.ascontiguousarray(norm_g[layer].reshape(8, 128).T),
                "bfg": np.ascontiguousarray(b_forget[layer][2 * g:2 * g + 2].reshape(1, 2)),
                "lam4": np.concatenate([lam_q1[layer], lam_k1[layer], lam_q2[layer], lam_k2[layer]]).reshape(1, 256),
                "sublng": np.ascontiguousarray(subln_g[layer][g * 128:(g + 1) * 128].reshape(128, 1)),
                "cosT": cosT, "sinT": sinT, "ident": ident, "tri": tri,
            }
            if mix_prev is not None:
                m["mixp"] = mix_prev[b]
                m["woutp"] = np.ascontiguousarray(w_out[layer - 1][perm])
            maps.append(m)
        return maps

    if FUSED:
        nc = _prog("fused", lambda: build_layer(True, 0.0, fused=True))
        shared = {"cosT": cosT, "sinT": sinT, "ident": ident, "tri": tri,
                  "woutp0": np.ascontiguousarray(w_out[0][perm]), "woutp1": np.ascontiguousarray(w_out[1][perm]),
                  "fg": np.ascontiguousarray(final_g.reshape(1, D))}
        for layer in range(2):
            sfx = str(layer)
            shared["win" + sfx] = np.ascontiguousarray(
                np.concatenate([w_in[layer][:, _core_cols(g)] for g in range(4)], axis=0))
            shared["normg" + sfx] = np.ascontiguousarray(norm_g[layer].reshape(8, 128).T)
            shared["bfg" + sfx] = np.ascontiguousarray(b_forget[layer].reshape(4, 2))
            shared["lam4" + sfx] = np.concatenate([lam_q1[layer], lam_k1[layer], lam_q2[layer], lam_k2[layer]]).reshape(1, 256)
            shared["sublng" + sfx] = np.ascontiguousarray(subln_g[layer].reshape(512, 1))
        maps = [dict(shared, hin=hin[core // 4]) for core in cores]
        res = run_bass_kernel_spmd(nc, maps, core_ids=cores)
        return np.stack([np.asarray(res.results[0]["out"]), np.asarray(res.results[4]["out"])], axis=0).astype(np.float32)

    mixes = []
    mix_prev = None
    for layer in range(2):
        li = 0.8 - 0.6 * math.exp(-0.3 * layer)
        nc = _prog(("layer", layer), lambda: build_layer(layer > 0, li))
        res = run_bass_kernel_spmd(nc, layer_maps(layer, mix_prev), core_ids=cores)
        outs = [np.asarray(r["mixo"]) for r in res.results]
        mix_prev = [np.ascontiguousarray(np.concatenate(outs[4 * b:4 * b + 4], axis=0)) for b in range(2)]
        mixes.append(mix_prev)

    ncf = _prog("final", build_final)
    maps = []
    for core in cores:
        b, q = core // 4, core % 4
        t0 = NMETA + 2048 * q
        maps.append({
            "xq": np.ascontiguousarray(x[b, 2048 * q:2048 * (q + 1)]),
            "m1": np.ascontiguousarray(mixes[0][b][:, t0:t0 + 2048]),
            "m2": np.ascontiguousarray(mixes[1][b][:, t0:t0 + 2048]),
            "wo0": np.ascontiguousarray(w_out[0][perm]),
            "wo1": np.ascontiguousarray(w_out[1][perm]),
            "fg": final_g.reshape(1, D),
        })
    res = run_bass_kernel_spmd(ncf, maps, core_ids=cores)
    out = np.empty((2, SEQ, D), np.float32)
    for core in cores:
        b, q = core // 4, core % 4
        out[b, 2048 * q:2048 * (q + 1)] = np.asarray(res.results[core]["out"])
    return out
```

```python
import math
from contextlib import ExitStack

import numpy as np
import ml_dtypes

import concourse.bass as bass
import concourse.mybir as mybir
from concourse.bass_utils import run_bass_kernel_spmd

F32 = mybir.dt.float32
BF16 = mybir.dt.bfloat16
AF = mybir.ActivationFunctionType
ALU = mybir.AluOpType

D = 1024
SEQ = 8192
NMETA = 16
L = SEQ + NMETA
NCH = 17
NT = 65
NCOL = 1282
EPS = 1e-6
NPB = np.dtype(ml_dtypes.bfloat16)

C_DQ, C_DQS, C_DK, C_DKS = 0, 128, 256, 384
C_V = 512
C_DZ, C_FZ = 768, 896
C_FQA, C_FQB, C_FKA, C_FKB = 1024, 1088, 1152, 1216
C_FLA, C_FLB = 1280, 1281


def tile_pos(t):
    return (0, 16) if t == 0 else (16 + 128 * (t - 1), 128)


def chunk_tiles(c):
    return [0] if c == 0 else list(range(4 * (c - 1) + 1, 4 * c + 1))


def chunk_pos(c):
    return (0, 16) if c == 0 else (16 + 512 * (c - 1), 512)


class Buf:
    def __init__(self, name):
        self.name = name
        self.w = None
        self.r = {}


class Sched:
    def __init__(self, nc, es):
        self.nc = nc
        self.es = es
        self.eng = {"pe": nc.tensor, "act": nc.scalar, "dve": nc.vector, "pool": nc.gpsimd, "sp": nc.sync}
        self.sem = {k: es.enter_context(nc.semaphore("s_" + k)) for k in ("pe", "act", "dve", "pool")}
        self.cnt = {k: 0 for k in self.sem}
        self.seen = {k: {} for k in self.eng}
        self.dsems = []
        self.nsem = 0

    def _deps(self, e, reads, writes):
        deps = {}

        def add(tok, raw=False):
            if tok is None:
                return
            sem, val, owner = tok
            if owner == e and not (raw and e != "pe"):
                return
            if deps.get(sem, 0) < val:
                deps[sem] = val

        for b in reads:
            add(b.w, raw=True)
        for b in writes:
            add(b.w)
            for t in b.r.values():
                add(t)
        for sem, val in deps.items():
            if self.seen[e].get(sem, 0) < val:
                self.eng[e].wait_ge(sem, val)
                self.seen[e][sem] = val

    def op(self, e, fn, reads=(), writes=()):
        self._deps(e, reads, writes)
        ins = fn(self.eng[e])
        self.cnt[e] += 1
        ins.then_inc(self.sem[e], 1)
        tok = (self.sem[e], self.cnt[e], e)
        for b in reads:
            b.r[e] = tok
        for b in writes:
            b.w = tok
            b.r = {}

    def new_slot(self, name):
        sem = self.es.enter_context(self.nc.semaphore("d_" + name))
        slot = {"sem": sem, "total": 0, "name": name}
        self.dsems.append(slot)
        return slot

    def dma(self, q, slot, out, in_, reads=(), writes=(), same_batch=False):
        self._deps(q, reads, writes)
        if not same_batch and slot["total"] > 0:
            if self.seen[q].get(slot["sem"], 0) < slot["total"]:
                self.eng[q].wait_ge(slot["sem"], slot["total"])
                self.seen[q][slot["sem"]] = slot["total"]
        ins = self.eng[q].dma_start(out=out, in_=in_)
        slot["total"] += 16
        ins.then_inc(slot["sem"], 16)
        tok = (slot["sem"], slot["total"], "dma:" + slot["name"])
        if not same_batch:
            slot["bw"], slot["br"] = [], []
        for b in writes:
            b.r = {}
        slot.setdefault("bw", []).extend(writes)
        slot.setdefault("br", []).extend(reads)
        for b in slot["br"]:
            b.r["dma:" + slot["name"]] = tok
        for b in slot["bw"]:
            b.w = tok

    def barrier(self):
        for e in self.eng:
            for k in self.sem:
                if k != e and self.cnt[k] > 0 and self.seen[e].get(self.sem[k], 0) < self.cnt[k]:
                    self.eng[e].wait_ge(self.sem[k], self.cnt[k])
                    self.seen[e][self.sem[k]] = self.cnt[k]
            for slot in self.dsems:
                if slot["total"] > 0 and self.seen[e].get(slot["sem"], 0) < slot["total"]:
                    self.eng[e].wait_ge(slot["sem"], slot["total"])
                    self.seen[e][slot["sem"]] = slot["total"]

    def finish(self, q="sp"):
        for slot in self.dsems:
            if slot["total"] > 0:
                self.eng[q].wait_ge(slot["sem"], slot["total"])


def _sb(nc, es, name, shape, dt):
    return es.enter_context(nc.sbuf_tensor(name, shape, dt))


def _ps(nc, es, name, shape, dt):
    return es.enter_context(nc.psum_tensor(name, shape, dt))


def build_layer(has_prev, lambda_init, nchunks=NCH, dbg=False, fused=False):
    nc = bass.Bass("TRN2", target_bir_lowering=False)
    hin = nc.dram_tensor("hin", [L, D], F32, kind="ExternalInput").ap()
    cosT = nc.dram_tensor("cosT", [128, L], F32, kind="ExternalInput").ap()
    sinT = nc.dram_tensor("sinT", [128, L], F32, kind="ExternalInput").ap()
    identD = nc.dram_tensor("ident", [128, 128], BF16, kind="ExternalInput").ap()
    triD = nc.dram_tensor("tri", [128, 128], BF16, kind="ExternalInput").ap()
    cfgs = []
    if fused:
        mixall = [nc.dram_tensor("mixall%d" % l, [D, L], BF16).ap() for l in range(2)]
        h1scr = nc.dram_tensor("h1scr", [L, D], F32).ap()
        utscr = nc.dram_tensor("utscr", [128, 8, L], BF16).ap()
        stgall = nc.dram_tensor("stgall", [8, 6, L], BF16).ap()
        woutp0 = nc.dram_tensor("woutp0", [D, D], F32, kind="ExternalInput").ap()
        woutp1 = nc.dram_tensor("woutp1", [D, D], F32, kind="ExternalInput").ap()
        fg = nc.dram_tensor("fg", [1, D], F32, kind="ExternalInput").ap()
        outD = nc.dram_tensor("out", [SEQ, D], F32, kind="ExternalOutput").ap()
        for l in range(2):
            winL = nc.dram_tensor("win%d" % l, [4 * D, NCOL], F32, kind="ExternalInput").ap()
            normgL = nc.dram_tensor("normg%d" % l, [128, 8], F32, kind="ExternalInput").ap()
            bfgL = nc.dram_tensor("bfg%d" % l, [4, 2], F32, kind="ExternalInput").ap()
            lam4L = nc.dram_tensor("lam4%d" % l, [1, 256], F32, kind="ExternalInput").ap()
            sublngL = nc.dram_tensor("sublng%d" % l, [4 * 128, 1], F32, kind="ExternalInput").ap()
            wflL = nc.dram_tensor("wfl%d" % l, [D, 8], F32, kind="ExternalInput").ap()
            bf8L = nc.dram_tensor("bf8%d" % l, [8, 1], F32, kind="ExternalInput").ap()
            for g in range(4):
                cfg = {"hin": hin, "win": winL[g * D:(g + 1) * D, :], "normg": normgL, "bfg": bfgL[g:g + 1, :],
                       "lam4": lam4L, "sublng": sublngL[g * 128:(g + 1) * 128, :],
                       "has_prev": l > 0, "li": 0.8 - 0.6 * math.exp(-0.3 * l),
                       "mixo": mixall[l][g * 256:(g + 1) * 256, :], "layer": l, "g": g}
                cfg["utscr"] = utscr
                cfg["stgall"] = stgall
                cfg["wfl"] = wflL
                cfg["bf8"] = bf8L
                cfg["fl_store"] = (g == 0)
                cfg["fl_load"] = (g > 0)
                cfg["ut_store"] = (g == 0)
                cfg["ut_load"] = (g > 0)
                if l > 0 and g == 0:
                    cfg["mixp"] = mixall[0]
                    cfg["woutp"] = woutp0
                    cfg["h1out"] = h1scr
                elif l > 0:
                    cfg["hin"] = h1scr
                    cfg["has_prev"] = False
                cfgs.append(cfg)
    else:
        cfg = {
            "hin": hin,
            "win": nc.dram_tensor("win", [D, NCOL], F32, kind="ExternalInput").ap(),
            "normg": nc.dram_tensor("normg", [128, 8], F32, kind="ExternalInput").ap(),
            "bfg": nc.dram_tensor("bfg", [1, 2], F32, kind="ExternalInput").ap(),
            "lam4": nc.dram_tensor("lam4", [1, 256], F32, kind="ExternalInput").ap(),
            "sublng": nc.dram_tensor("sublng", [128, 1], F32, kind="ExternalInput").ap(),
            "has_prev": has_prev, "li": lambda_init, "layer": 0, "g": 0,
        }
        if has_prev:
            cfg["mixp"] = nc.dram_tensor("mixp", [D, L], BF16, kind="ExternalInput").ap()
            cfg["woutp"] = nc.dram_tensor("woutp", [D, D], F32, kind="ExternalInput").ap()
        cfg["mixo"] = nc.dram_tensor("mixo", [256, L], BF16, kind="ExternalOutput").ap()
        cfgs.append(cfg)
    has_prev = has_prev or fused
    cur = {}

    with ExitStack() as es:
        S = Sched(nc, es)
        es2 = es.enter_context(ExitStack())
        sb = lambda name, shape, dt: _sb(nc, es2, name, shape, dt)

        W = sb("W", [128, 8, NCOL], BF16)
        bW = Buf("W")
        if has_prev:
            WO = sb("WO", [128, 8, D], BF16)
            bWO = Buf("WO")
        KdT = sb("KdT", [128, L], BF16)
        KA = sb("KA", [70, L], BF16)
        KB = sb("KB", [70, L], BF16)
        Vd = sb("Vd", [128, NT, 128], BF16)
        Vf = sb("Vf", [128, NT, 192], BF16)
        bKd = [Buf(f"Kd{c}") for c in range(NCH)]
        bKA = [Buf(f"KA{c}") for c in range(NCH)]
        bKB = [Buf(f"KB{c}") for c in range(NCH)]
        bVd = [Buf(f"Vd{c}") for c in range(NCH)]
        bVf = [Buf(f"Vf{c}") for c in range(NCH)]

        qsets = []
        for qi in range(2):
            qsets.append((sb(f"QdT{qi}", [128, 512], BF16), Buf(f"Qd{qi}"),
                          sb(f"QA{qi}", [70, 512], BF16), Buf(f"QA{qi}"),
                          sb(f"QB{qi}", [70, 512], BF16), Buf(f"QB{qi}"),
                          sb(f"ZdT{qi}", [128, 512], BF16), Buf(f"Zd{qi}"),
                          sb(f"ZfT{qi}", [128, 512], BF16), Buf(f"Zf{qi}")))

        xt = [sb(f"xt{i}", [128, D], F32) for i in range(2)]
        bxt = [Buf(f"xt{i}") for i in range(2)]
        sxt = [S.new_slot(f"xt{i}") for i in range(2)]
        sh1 = [S.new_slot(f"h1o{i}") for i in range(2)]
        sutl = S.new_slot("utl")
        suts = S.new_slot("uts")
        ub = [sb(f"ub{i}", [128, D], BF16) for i in range(2)]
        bub = [Buf(f"ub{i}") for i in range(2)]
        uT = sb("uT", [128, 8, 512], BF16); buT = Buf("uT")
        stat = sb("stat", [128, 8], F32); bstat = Buf("stat")
        if has_prev:
            mp = [sb(f"mp{i}", [128, 8, 512], BF16) for i in range(1)]
            bmp = [Buf(f"mp{i}") for i in range(1)]
            smp = [S.new_slot(f"mp{i}") for i in range(1)]
        rope = [sb(f"rope{i}", [128, 2, 512], F32) for i in range(1)]
        brope = [Buf(f"rope{i}") for i in range(1)]
        srope = [S.new_slot(f"rope{i}") for i in range(1)]

        ident = sb("identS", [128, 128], BF16)
        tri = sb("triS", [128, 128], BF16)
        onesb = sb("onesb", [128, 128], BF16)
        onesf = sb("onesf", [128, 128], F32)
        gT = sb("gT", [128, 8], F32)
        cst = sb("cst", [128, 16], F32)
        lamv = sb("lamv", [128, 256], F32)
        bconst = Buf("const")
        sconst = S.new_slot("const")
        bf2 = sb("bf2", [8, 1], F32)
        Wfl = sb("Wfl", [128, 8, 8], BF16); bWfl = Buf("Wfl")
        sstg = S.new_slot("stgo")

        rows = sb("rows", [8, 2, 512], F32); brows = Buf("rows")
        stg = sb("stg", [8, 6, 512], BF16); bstg = Buf("stg")
        onesrow = sb("onesrow", [8, 512], F32)
        carry = sb("carry", [8, 1], F32)
        saug = S.new_slot("aug")

        NP = 6
        Pb = [sb(f"P{i}", [128, 512], BF16) for i in range(NP)]
        bP = [Buf(f"P{i}") for i in range(NP)]
        ep = [sb(f"ep{i}", [128, 512], F32) for i in range(6)]
        bep = [Buf(f"ep{i}") for i in range(6)]
        t1, bt1, t2, bt2 = ep[4], bep[4], ep[5], bep[5]
        mixd = [sb(f"mixd{i}", [128, 512], BF16) for i in range(2)]
        bmixd = [Buf(f"mixd{i}") for i in range(2)]
        smixd = [S.new_slot(f"mixd{i}") for i in range(2)]
        mixf = [sb(f"mixf{i}", [128, 512], BF16) for i in range(2)]
        bmixf = [Buf(f"mixf{i}") for i in range(2)]
        smixf = [S.new_slot(f"mixf{i}") for i in range(2)]

        bank = [_ps(nc, es, f"bank{i}", [128, 512], F32) for i in range(8)]
        bbank = [Buf(f"bank{i}") for i in range(8)]

        acc = [sb(f"acc{i}", [128, 512], F32) for i in range(2)]
        bacc = [Buf(f"acc{i}") for i in range(2)]
        ones32 = sb("ones32", [128, 128], F32)

        def setup():
            first = not state.get("setup_done")
            state["setup_done"] = True
            S.dma("sp", sconst, gT[:, :], cur["normg"][:, :], writes=[bconst])
            if first:
                S.dma("sp", sconst, ident[:, :], identD[:, :], writes=[bconst], same_batch=True)
                S.dma("sp", sconst, tri[:, :], triD[:, :], writes=[bconst], same_batch=True)
            S.dma("sp", sconst, lamv[:, :], cur["lam4"][0:1, :].partition_broadcast(128), writes=[bconst], same_batch=True)
            S.dma("sp", sconst, cst[:, 4:5], cur["sublng"][:, :], writes=[bconst], same_batch=True)
            if cur.get("fl_store"):
                S.dma("sp", sconst, bf2[0:8, 0:1], cur["bf8"][:, :], writes=[bconst], same_batch=True)
                S.dma("sp", sconst, ep[3][:, 0:64].rearrange("p (c h) -> p c h", c=8),
                      cur["wfl"].rearrange("(c p) h -> p c h", p=128), writes=[bconst, bep[3]], same_batch=True)
            else:
                S.dma("sp", sconst, bf2[0:1, 0:1], cur["bfg"][0:1, 0:1], writes=[bconst], same_batch=True)
                S.dma("sp", sconst, bf2[1:2, 0:1], cur["bfg"][0:1, 1:2], writes=[bconst], same_batch=True)

            S.op("pool", lambda e: e.memset(carry[:, :], 0.0), writes=[brows])
            if first:
                S.op("pool", lambda e: e.memset(onesb[:, :], 1.0), writes=[bconst])
                S.op("pool", lambda e: e.memset(onesf[:, :], 1.0 / 128.0), writes=[bconst])
                S.op("pool", lambda e: e.memset(ones32[:, :], 1.0), writes=[bconst])
                S.op("pool", lambda e: e.memset(cst[:, 0:1], -0.5), writes=[bconst])
                S.op("pool", lambda e: e.memset(cst[:, 1:2], EPS), writes=[bconst])
                S.op("pool", lambda e: e.memset(onesrow[:, :], 1.0), writes=[bconst])
                S.op("pool", lambda e: e.memset(Vf[:, :, :], 1.0), writes=bVf)
                S.op("pool", lambda e: e.memset(KA[64:70, :], 1.0), writes=bKA)
                S.op("pool", lambda e: e.memset(KB[64:70, :], 1.0), writes=bKB)
                for qs_ in qsets:
                    S.op("pool", lambda e: e.memset(qs_[2][64:70, :], 1.0), writes=[qs_[3]])
                    S.op("pool", lambda e: e.memset(qs_[4][64:70, :], 1.0), writes=[qs_[5]])

            S.op("dve", lambda e: e.tensor_tensor(out=ep[2][:, 0:64], in0=lamv[:, 0:64], in1=lamv[:, 64:128], op=ALU.mult),
                 reads=[bconst], writes=[bep[2]])
            S.op("dve", lambda e: e.tensor_tensor(out=ep[2][:, 64:128], in0=lamv[:, 128:192], in1=lamv[:, 192:256],
                                                  op=ALU.mult), reads=[bconst], writes=[bep[2]])
            S.op("dve", lambda e: e.reduce_sum(out=cst[:, 5:6], in_=ep[2][:, 0:64], axis=mybir.AxisListType.X),
                 reads=[bep[2]], writes=[bconst])
            S.op("dve", lambda e: e.reduce_sum(out=cst[:, 6:7], in_=ep[2][:, 64:128], axis=mybir.AxisListType.X),
                 reads=[bep[2]], writes=[bconst])
            S.op("act", lambda e: e.activation(out=cst[:, 7:9], in_=cst[:, 5:7], func=AF.Exp), reads=[bconst], writes=[bconst])
            S.op("dve", lambda e: e.tensor_tensor(out=cst[:, 9:10], in0=cst[:, 8:9], in1=cst[:, 7:8], op=ALU.subtract),
                 reads=[bconst], writes=[bconst])
            S.op("dve", lambda e: e.tensor_scalar(out=cst[:, 3:4], in0=cst[:, 9:10], scalar1=-float(cur["li"]),
                                                  scalar2=None, op0=ALU.add), reads=[bconst], writes=[bconst])
            S.op("dve", lambda e: e.tensor_scalar(out=cst[:, 2:3], in0=cst[:, 4:5], scalar1=float(1.0 - cur["li"]),
                                                  scalar2=None, op0=ALU.mult), reads=[bconst], writes=[bconst])
            if cur.get("fl_store"):
                S.op("dve", lambda e: e.tensor_copy(out=Wfl[:, :, :], in_=ep[3][:, 0:64].rearrange("p (c h) -> p c h", c=8)),
                     reads=[bconst, bep[3]], writes=[bWfl])
            S.op("dve", lambda e: e.tensor_scalar(out=bf2[:, :], in0=bf2[:, :], scalar1=-1.0, scalar2=None, op0=ALU.mult),
                 reads=[bconst], writes=[bconst])

            for fc in range(8):
                S.dma("sp", sxt[0], xt[0][:, :], cur["win"][fc * 128:(fc + 1) * 128, 0:D], writes=[bxt[0]])
                S.dma("sp", sxt[1], xt[1][:, 0:NCOL - D], cur["win"][fc * 128:(fc + 1) * 128, D:NCOL], writes=[bxt[1]])
                S.op("dve", lambda e: e.tensor_copy(out=W[:, fc, 0:D], in_=xt[0][:, :]), reads=[bxt[0]], writes=[bW])
                S.op("dve", lambda e: e.tensor_scalar(out=W[:, fc, C_DZ:C_DZ + 256], in0=xt[0][:, C_DZ:C_DZ + 256],
                                                      scalar1=0.5, scalar2=None, op0=ALU.mult), reads=[bxt[0]], writes=[bW])
                S.op("pool", lambda e: e.tensor_copy(out=W[:, fc, D:NCOL], in_=xt[1][:, 0:NCOL - D]), reads=[bxt[1]],
                     writes=[bW])
            if cur["has_prev"]:
                for fc in range(8):
                    s = fc % 2
                    S.dma("sp", sxt[s], xt[s][:, :], cur["woutp"][fc * 128:(fc + 1) * 128, :], writes=[bxt[s]])
                    S.op("dve", lambda e: e.tensor_copy(out=WO[:, fc, :], in_=xt[s][:, :]), reads=[bxt[s]], writes=[bWO])

        state = {"xt": 0, "pb": 0, "sbank": 0, "p": 0, "mix": 0}

        def proj_bank():
            pool = (3, 5, 0, 1, 6, 7, 2, 4) if state.get("drain") else (3, 5)
            i = pool[state["pb"] % len(pool)]
            state["pb"] += 1
            return bank[i], bbank[i]

        def s_bank(pool=(0, 1)):
            i = pool[state["sbank"] % len(pool)]
            state["sbank"] += 1
            return bank[i], bbank[i]

        def project(c):
            QdT, bQd, QA, bQA, QB, bQB, ZdT, bZd, ZfT, bZf = qsets[c % 2]
            p0, w = chunk_pos(c)
            tiles = chunk_tiles(c)
            rs = 0
            S.dma("sp", srope[rs], rope[rs][:, 0, 0:w], cosT[:, p0:p0 + w], writes=[brope[rs]])
            S.dma("sp", srope[rs], rope[rs][:, 1, 0:w], sinT[:, p0:p0 + w], writes=[brope[rs]], same_batch=True)
            if cur["has_prev"]:
                ms = 0
                S.dma("sp", smp[ms], mp[ms][:, :, 0:w],
                      cur["mixp"].rearrange("(c p) t -> p c t", p=128)[:, :, p0:p0 + w], writes=[bmp[ms]])
            tinfo = []
            if cur.get("ut_load"):
                S.dma("sp", sutl, uT[:, :, 0:w], cur["utscr"][:, :, p0:p0 + w], writes=[buT])
                yield True
            else:
                for j, t in enumerate(tiles):
                    tp, n = tile_pos(t)
                    xs = state["xt"] % 2
                    state["xt"] += 1
                    tinfo.append((j, t, tp, n, xs))
            def load(k):
                j, t, tp, n, xs = tinfo[k]
                S.dma("sp", sxt[xs], xt[xs][0:n, :], cur["hin"][tp:tp + n, :], writes=[bxt[xs]])
            if tinfo:
                load(0)
                yield True
            for k, (j, t, tp, n, xs) in enumerate(tinfo):
                X, bX = xt[xs], bxt[xs]
                U, bU = ub[xs], bub[xs]
                if k + 1 < len(tinfo):
                    load(k + 1)
                if cur["has_prev"]:
                    for half in range(2):
                        pb, bpb = proj_bank()
                        for fc in range(8):
                            S.op("pe", lambda e: e.matmul(
                                pb[0:n, :], lhsT=mp[ms][:, fc, 128 * j:128 * j + n],
                                rhs=WO[:, fc, half * 512:(half + 1) * 512], start=(fc == 0), stop=(fc == 7)),
                                reads=[bmp[ms], bWO], writes=[bpb])
                        yield False
                        S.op("dve", lambda e: e.tensor_tensor(
                            out=X[0:n, half * 512:(half + 1) * 512], in0=pb[0:n, :],
                            in1=X[0:n, half * 512:(half + 1) * 512], op=ALU.add), reads=[bpb, bX], writes=[bX])
                        yield True
                if cur.get("h1out") is not None:
                    S.dma("pool", sh1[xs], cur["h1out"][tp:tp + n, :], X[0:n, :], reads=[bX])
                for hf in range(2):
                    S.op("dve", lambda e: e.tensor_tensor(out=ep[2 + hf][0:n, :], in0=X[0:n, 512 * hf:512 * hf + 512],
                                                          in1=X[0:n, 512 * hf:512 * hf + 512], op=ALU.mult),
                         reads=[bX], writes=[bep[2 + hf]])
                    S.op("dve", lambda e: e.reduce_sum(out=stat[0:n, 4 + hf:5 + hf], in_=ep[2 + hf][0:n, :],
                                                       axis=mybir.AxisListType.X), reads=[bep[2 + hf]], writes=[bstat])
                S.op("dve", lambda e: e.tensor_tensor(out=stat[0:n, 0:1], in0=stat[0:n, 4:5], in1=stat[0:n, 5:6], op=ALU.add),
                     reads=[bstat], writes=[bstat])
                S.op("dve", lambda e: e.tensor_scalar(out=stat[0:n, 1:2], in0=stat[0:n, 0:1], scalar1=1.0 / D,
                                                      scalar2=EPS, op0=ALU.mult, op1=ALU.add),
                     reads=[bstat], writes=[bstat])
                yield True
                S.op("act", lambda e: e.activation(out=stat[0:n, 3:4], in_=stat[0:n, 1:2], func=AF.Ln),
                     reads=[bstat], writes=[bstat])
                S.op("act", lambda e: e.activation(out=stat[0:n, 2:3], in_=stat[0:n, 3:4], func=AF.Exp, scale=-0.5),
                     reads=[bstat], writes=[bstat])
                S.op("act", lambda e: e.activation(out=U[0:n, :], in_=X[0:n, :], func=AF.Copy, scale=stat[0:n, 2:3]),
                     reads=[bX, bstat], writes=[bU])
                yield True
                tb, btb = proj_bank()
                tbv = tb[:, :].bitcast(BF16)
                for fc in range(8):
                    S.op("pe", lambda e: e.transpose(out=tbv[:, fc * 128:fc * 128 + n],
                                                     in_=U[0:n, fc * 128:(fc + 1) * 128], identity=ident[0:n, 0:n]),
                         reads=[bU, bconst], writes=[btb])
                yield False
                for fc in range(8):
                    S.op("dve", lambda e: e.tensor_scalar(out=uT[:, fc, 128 * j:128 * j + n],
                                                          in0=tbv[:, fc * 128:fc * 128 + n],
                                                          scalar1=gT[:, fc:fc + 1], scalar2=None, op0=ALU.mult),
                         reads=[btb, bconst], writes=[buT])
                yield True
            if cur.get("ut_store"):
                S.dma("pool", suts, cur["utscr"][:, :, p0:p0 + w], uT[:, :, 0:w], reads=[buT])

            def fm(col, M):
                pb, bpb = proj_bank()
                for fc in range(8):
                    S.op("pe", lambda e: e.matmul(pb[0:M, 0:w], lhsT=W[:, fc, col:col + M], rhs=uT[:, fc, 0:w],
                                                  start=(fc == 0), stop=(fc == 7)),
                         reads=[bW, buT], writes=[bpb])
                return pb, bpb

            for (cn, cs, dst, bdst, dcol) in ((C_DQ, C_DQS, QdT, bQd, 0), (C_DK, C_DKS, KdT, bKd[c], p0)):
                pa, bpa = fm(cn, 128)
                pbk, bpbk = fm(cs, 128)
                yield False
                S.op("dve", lambda e: e.tensor_tensor(out=t1[:, 0:w], in0=pa[:, 0:w], in1=rope[rs][:, 0, 0:w],
                                                      op=ALU.mult), reads=[bpa, brope[rs]], writes=[bt1])
                S.op("dve", lambda e: e.tensor_tensor(out=t2[:, 0:w], in0=pbk[:, 0:w], in1=rope[rs][:, 1, 0:w],
                                                      op=ALU.mult), reads=[bpbk, brope[rs]], writes=[bt2])
                S.op("dve", lambda e: e.tensor_tensor(out=dst[:, dcol:dcol + w], in0=t1[:, 0:w], in1=t2[:, 0:w],
                                                      op=ALU.add), reads=[bt1, bt2], writes=[bdst])
                yield True
            for (col, dA, bdA, dB, bdB, dcol) in ((C_FQA, QA, bQA, QB, bQB, 0), (C_FKA, KA, bKA[c], KB, bKB[c], p0)):
                pa, bpa = fm(col, 128)
                yield False
                S.op("dve", lambda e: e.tensor_copy(out=dA[0:64, dcol:dcol + w], in_=pa[0:64, 0:w]),
                     reads=[bpa], writes=[bdA])
                S.op("dve", lambda e: e.tensor_copy(out=dB[0:64, dcol:dcol + w], in_=pa[64:128, 0:w]),
                     reads=[bpa], writes=[bdB])
                yield True
            for (col, dst, bdst) in ((C_DZ, ZdT, bZd), (C_FZ, ZfT, bZf)):
                pa, bpa = fm(col, 128)
                yield False
                S.op("act", lambda e: e.activation(out=t1[:, 0:w], in_=pa[:, 0:w], func=AF.Tanh),
                     reads=[bpa], writes=[bt1])
                yield False
                S.op("dve", lambda e: e.scalar_tensor_tensor(out=dst[:, 0:w], in0=t1[:, 0:w], scalar=1.0, in1=pa[:, 0:w],
                                                             op0=ALU.add, op1=ALU.mult),
                     reads=[bt1, bpa], writes=[bdst])
                yield True
            gofs = 2 * cur.get("g", 0)
            if cur.get("fl_load"):
                first = True
                for hh in range(2):
                    Qt, bQt = (QA, bQA) if hh == 0 else (QB, bQB)
                    Kt, bKt = (KA, bKA[c]) if hh == 0 else (KB, bKB[c])
                    for i in range(3):
                        S.dma("sp", saug, Qt[64 + i:65 + i, 0:w], cur["stgall"][gofs + hh:gofs + hh + 1, i, p0:p0 + w],
                              writes=[bQt], same_batch=not first)
                        first = False
                        S.dma("sp", saug, Kt[67 + i:68 + i, p0:p0 + w],
                              cur["stgall"][gofs + hh:gofs + hh + 1, 3 + i, p0:p0 + w], writes=[bKt], same_batch=True)
                yield True
            else:
                NHh = 8 if cur.get("fl_store") else 2
                if cur.get("fl_store"):
                    pa, bpa = proj_bank()
                    for fc in range(8):
                        S.op("pe", lambda e: e.matmul(pa[0:8, 0:w], lhsT=Wfl[:, fc, 0:8], rhs=uT[:, fc, 0:w],
                                                      start=(fc == 0), stop=(fc == 7)), reads=[bWfl, buT], writes=[bpa])
                else:
                    pa, bpa = fm(C_FLA, 2)
                yield False
                ra, rb = rows[0:NHh, 0, 0:w], rows[0:NHh, 1, 0:w]
                S.op("act", lambda e: e.activation(out=ra, in_=pa[0:NHh, 0:w], func=AF.Exp, scale=-1.0,
                                                   bias=bf2[0:NHh, 0:1]), reads=[bpa, bconst], writes=[brows])
                S.op("act", lambda e: e.activation(out=ra, in_=ra, func=AF.Ln, bias=1.0, scale=1.0),
                     reads=[brows], writes=[brows])
                yield True
                S.op("dve", lambda e: e.tensor_tensor_scan(out=rb, data0=onesrow[0:NHh, 0:w], data1=ra,
                                                           initial=carry[0:NHh, 0:1], op0=ALU.mult, op1=ALU.subtract),
                     reads=[brows, bconst], writes=[brows])
                S.op("dve", lambda e: e.tensor_copy(out=carry[0:NHh, 0:1], in_=rows[0:NHh, 1, w - 1:w]),
                     reads=[brows], writes=[brows])
                S.op("dve", lambda e: e.tensor_scalar(out=ra, in0=rb, scalar1=8.0, scalar2=None, op0=ALU.mult),
                     reads=[brows], writes=[brows])
                S.op("dve", lambda e: e.tensor_copy(out=stg[0:NHh, 0, 0:w], in_=ra), reads=[brows], writes=[bstg])
                S.op("dve", lambda e: e.tensor_tensor(out=rb, in0=ra, in1=stg[0:NHh, 0, 0:w], op=ALU.subtract),
                     reads=[brows, bstg], writes=[brows])
                S.op("dve", lambda e: e.tensor_copy(out=stg[0:NHh, 1, 0:w], in_=rb), reads=[brows], writes=[bstg])
                S.op("dve", lambda e: e.tensor_tensor(out=ra, in0=rb, in1=stg[0:NHh, 1, 0:w], op=ALU.subtract),
                     reads=[brows, bstg], writes=[brows])
                S.op("dve", lambda e: e.tensor_copy(out=stg[0:NHh, 2, 0:w], in_=ra), reads=[brows], writes=[bstg])
                S.op("dve", lambda e: e.tensor_scalar(out=stg[0:NHh, 3:6, 0:w], in0=stg[0:NHh, 0:3, 0:w], scalar1=-1.0,
                                                      scalar2=None, op0=ALU.mult), reads=[bstg], writes=[bstg])
                yield True
                if cur.get("fl_store"):
                    S.dma("pool", sstg, cur["stgall"][:, :, p0:p0 + w], stg[0:8, :, 0:w], reads=[bstg])
                first = True
                for hh in range(2):
                    Qt, bQt = (QA, bQA) if hh == 0 else (QB, bQB)
                    Kt, bKt = (KA, bKA[c]) if hh == 0 else (KB, bKB[c])
                    for i in range(3):
                        S.dma("sp", saug, Qt[64 + i:65 + i, 0:w], stg[gofs + hh:gofs + hh + 1, i, 0:w], reads=[bstg],
                              writes=[bQt], same_batch=not first)
                        first = False
                        S.dma("sp", saug, Kt[67 + i:68 + i, p0:p0 + w], stg[gofs + hh:gofs + hh + 1, 3 + i, 0:w],
                              reads=[bstg], writes=[bKt], same_batch=True)
                yield True
            for j, t in enumerate(tiles):
                tp, n = tile_pos(t)
                pb, bpb = proj_bank()
                for fc in range(8):
                    S.op("pe", lambda e: e.matmul(pb[0:n, 0:256], lhsT=uT[:, fc, 128 * j:128 * j + n],
                                                  rhs=W[:, fc, C_V:C_V + 256], start=(fc == 0), stop=(fc == 7)),
                         reads=[bW, buT], writes=[bpb])
                yield False
                S.op("dve", lambda e: e.tensor_copy(out=Vd[0:n, t, :], in_=pb[0:n, 0:128]), reads=[bpb], writes=[bVd[c]])
                S.op("dve", lambda e: e.tensor_copy(out=Vf[0:n, t, 0:64], in_=pb[0:n, 128:192]), reads=[bpb],
                     writes=[bVf[c]])
                S.op("dve", lambda e: e.tensor_copy(out=Vf[0:n, t, 128:192], in_=pb[0:n, 192:256]), reads=[bpb],
                     writes=[bVf[c]])
                yield True

        def attention(c, nxt=None):
            QdT, bQd, QA, bQA, QB, bQB, ZdT, bZd, ZfT, bZf = qsets[c % 2]
            p0, w = chunk_pos(c)
            last_t = chunk_tiles(c)[-1]
            kts = list(range(0, last_t + 1))
            first_diag = chunk_tiles(c)[0]

            def kinfo(kt):
                kp, kn = tile_pos(kt)
                if kt >= first_diag:
                    i = kt - first_diag
                    return kp, kn, 128 * i, True
                return kp, kn, 0, False

            def kchunk(kt):
                return 0 if kt == 0 else (kt - 1) // 4 + 1

            O1, bO1 = bank[2], bbank[2]
            L1, bL1 = bank[3], bbank[3]
            O2, bO2 = bank[4], bbank[4]
            L2, bL2 = bank[5], bbank[5]
            OA, bOA = bank[6], bbank[6]
            OB, bOB = bank[7], bbank[7]

            steps = []
            for kt in kts:
                steps.append(("d1", kt))
                steps.append(("d2", kt))
            for kt in kts:
                steps.append(("fA", kt))
            for kt in kts:
                steps.append(("fB", kt))
            pend = {}

            def qk(si):
                u, kt = steps[si]
                kp, kn, qlo, diag = kinfo(kt)
                kc = kchunk(kt)
                if c == 0:
                    spool = (0, 1)
                elif u in ("d1", "d2"):
                    spool = (0, 1, 6, 7)
                elif u == "fA":
                    spool = (0, 1, 7)
                else:
                    spool = (0, 1, 2, 4)
                sbk, bsbk = s_bank(spool)
                pi = state["p"] % NP
                state["p"] += 1
                P, bPi = Pb[pi], bP[pi]
                if u == "d1":
                    lhsT, rhs, rd = KdT[0:64, kp:kp + kn], QdT[0:64, qlo:w], [bKd[kc], bQd]
                elif u == "d2":
                    lhsT, rhs, rd = KdT[64:128, kp:kp + kn], QdT[64:128, qlo:w], [bKd[kc], bQd]
                elif u == "fA":
                    lhsT, rhs, rd = KA[0:70, kp:kp + kn], QA[0:70, qlo:w], [bKA[kc], bQA]
                else:
                    lhsT, rhs, rd = KB[0:70, kp:kp + kn], QB[0:70, qlo:w], [bKB[kc], bQB]
                S.op("pe", lambda e: e.matmul(sbk[0:kn, qlo:w], lhsT=lhsT, rhs=rhs, start=True, stop=True),
                     reads=rd, writes=[bsbk])
                pend[si] = (P, bPi, sbk, bsbk)

            def qk_post(si):
                u, kt = steps[si]
                kp, kn, qlo, diag = kinfo(kt)
                P, bPi, sbk, bsbk = pend[si]
                S.op("act", lambda e: e.activation(out=P[0:kn, qlo:w], in_=sbk[0:kn, qlo:w], func=AF.Exp, scale=0.125),
                     reads=[bsbk], writes=[bPi])
                if diag:
                    dw = min(128, w - qlo)
                    S.op("pool", lambda e: e.tensor_tensor(out=P[0:kn, qlo:qlo + dw], in0=P[0:kn, qlo:qlo + dw],
                                                           in1=tri[0:kn, 0:dw], op=ALU.mult),
                         reads=[bPi, bconst], writes=[bPi])

            def pv(si):
                u, kt = steps[si]
                kp, kn, qlo, diag = kinfo(kt)
                kc = kchunk(kt)
                P, bPi = pend.pop(si)[0:2]
                st, sp_ = (kt == kts[0]), (kt == kts[-1])
                rhs = P[0:kn, qlo:w]
                if u in ("d1", "d2"):
                    O, bO, Lb, bLb = (O1, bO1, L1, bL1) if u == "d1" else (O2, bO2, L2, bL2)
                    S.op("pe", lambda e: e.matmul(O[:, qlo:w], lhsT=Vd[0:kn, kt, :], rhs=rhs, start=st, stop=sp_),
                         reads=[bVd[kc], bPi], writes=[bO])
                    ai = 0 if u == "d1" else 1
                    S.op("dve", lambda e: e.tensor_tensor(out=acc[ai][0:kn, qlo:w], in0=acc[ai][0:kn, qlo:w],
                                                          in1=P[0:kn, qlo:w], op=ALU.add),
                         reads=[bPi, bacc[ai]], writes=[bacc[ai]])
                elif u == "fA":
                    S.op("pe", lambda e: e.matmul(OA[:, qlo:w], lhsT=Vf[0:kn, kt, 0:128], rhs=rhs, start=st, stop=sp_),
                         reads=[bVf[kc], bPi], writes=[bOA])
                else:
                    S.op("pe", lambda e: e.matmul(OB[:, qlo:w], lhsT=Vf[0:kn, kt, 64:192], rhs=rhs, start=st, stop=sp_),
                         reads=[bVf[kc], bPi], writes=[bOB])

            for ai in range(2):
                S.op("pool", lambda e: e.memset(acc[ai][:, 0:w], 0.0), writes=[bacc[ai]])
            LA = 2 if c >= 1 else 1
            nd = 2 * len(kts)
            nfa = nd + len(kts)
            groups = []
            i = 0
            while i < len(steps):
                if steps[i][0] == "d1":
                    groups.append([i, i + 1])
                    i += 2
                else:
                    groups.append([i])
                    i += 1
            per = max(1, -(-70 // len(groups)))
            pst = {"clean": True}

            def pull(k):
                if nxt is None:
                    return
                for _ in range(k):
                    r = next(nxt, None)
                    if r is None:
                        pst["clean"] = True
                        return
                    pst["clean"] = r

            def make_clean():
                while not pst["clean"]:
                    pull(1)

            for gi in range(len(groups) + LA):
                if gi > 0:
                    pull(per)
                if gi < len(groups):
                    for si in groups[gi]:
                        qk(si)
                    for si in groups[gi]:
                        qk_post(si)
                if gi - LA >= 0:
                    for si in groups[gi - LA]:
                        pv(si)
                        done = si + 1
                        if done in (nd, nfa, len(steps)):
                            make_clean()
                        if done == nd:
                            epi_diff(c, p0, w, O1, bO1, L1, bL1, O2, bO2, L2, bL2)
                        elif done == nfa:
                            epi_fox_a(c, p0, w, OA, bOA)
                        elif done == len(steps):
                            epi_fox_b(c, p0, w, OB, bOB)

        def epi_diff(c, p0, w, O1, bO1, L1, bL1, O2, bO2, L2, bL2):
            QdT, bQd, QA, bQA, QB, bQB, ZdT, bZd, ZfT, bZf = qsets[c % 2]
            r1, r2, a, b, o, sq = (ep[i] for i in range(6))
            br1, br2, ba, bb, bo, bsq = (bep[i] for i in range(6))
            S.op("pe", lambda e: e.matmul(L1[:, 0:w], lhsT=ones32[:, :], rhs=acc[0][:, 0:w], start=True, stop=True),
                 reads=[bacc[0], bconst], writes=[bL1])
            S.op("pe", lambda e: e.matmul(L2[:, 0:w], lhsT=ones32[:, :], rhs=acc[1][:, 0:w], start=True, stop=True),
                 reads=[bacc[1], bconst], writes=[bL2])
            S.op("dve", lambda e: e.reciprocal(out=r1[:, 0:w], in_=L1[:, 0:w]), reads=[bL1], writes=[br1])
            S.op("dve", lambda e: e.reciprocal(out=r2[:, 0:w], in_=L2[:, 0:w]), reads=[bL2], writes=[br2])
            S.op("dve", lambda e: e.tensor_tensor(out=a[:, 0:w], in0=O1[:, 0:w], in1=r1[:, 0:w], op=ALU.mult),
                 reads=[bO1, br1], writes=[ba])
            S.op("dve", lambda e: e.tensor_tensor(out=b[:, 0:w], in0=O2[:, 0:w], in1=r2[:, 0:w], op=ALU.mult),
                 reads=[bO2, br2], writes=[bb])
            S.op("dve", lambda e: e.scalar_tensor_tensor(out=o[:, 0:w], in0=b[:, 0:w], scalar=cst[:, 3:4], in1=a[:, 0:w],
                                                         op0=ALU.mult, op1=ALU.add), reads=[ba, bb, bconst], writes=[bo])
            S.op("pool", lambda e: e.tensor_tensor(out=sq[:, 0:w], in0=o[:, 0:w], in1=o[:, 0:w], op=ALU.mult),
                 reads=[bo], writes=[bsq])
            mb, bmb = s_bank()
            S.op("pe", lambda e: e.matmul(mb[:, 0:w], lhsT=onesf[:, :], rhs=sq[:, 0:w], start=True, stop=True),
                 reads=[bsq, bconst], writes=[bmb])
            S.op("dve", lambda e: e.tensor_scalar(out=r1[:, 0:w], in0=mb[:, 0:w], scalar1=EPS, scalar2=None, op0=ALU.add),
                 reads=[bmb], writes=[br1])
            S.op("act", lambda e: e.activation(out=r1[:, 0:w], in_=r1[:, 0:w], func=AF.Ln), reads=[br1], writes=[br1])
            S.op("act", lambda e: e.activation(out=r2[:, 0:w], in_=r1[:, 0:w], func=AF.Exp, scale=-0.5),
                 reads=[br1], writes=[br2])
            S.op("dve", lambda e: e.tensor_tensor(out=a[:, 0:w], in0=o[:, 0:w], in1=r2[:, 0:w], op=ALU.mult),
                 reads=[bo, br2], writes=[ba])
            mi = state["mix"] % 2
            S.op("dve", lambda e: e.scalar_tensor_tensor(out=mixd[mi][:, 0:w], in0=a[:, 0:w], scalar=cst[:, 2:3],
                                                         in1=ZdT[:, 0:w], op0=ALU.mult, op1=ALU.mult),
                 reads=[ba, bconst, bZd, bmixd[mi]], writes=[bmixd[mi]])
            S.dma("pool", smixd[mi], cur["mixo"][0:128, p0:p0 + w], mixd[mi][:, 0:w], reads=[bmixd[mi]])

        def epi_fox_a(c, p0, w, OA, bOA):
            QdT, bQd, QA, bQA, QB, bQB, ZdT, bZd, ZfT, bZf = qsets[c % 2]
            mi = state["mix"] % 2
            rl, t = ep[0], ep[1]
            brl, bt = bep[0], bep[1]
            S.op("dve", lambda e: e.reciprocal(out=rl[0:64, 0:w], in_=OA[64:128, 0:w]), reads=[bOA], writes=[brl])
            S.op("dve", lambda e: e.tensor_tensor(out=t[0:64, 0:w], in0=OA[0:64, 0:w], in1=rl[0:64, 0:w], op=ALU.mult),
                 reads=[bOA, brl], writes=[bt])
            S.op("pool", lambda e: e.tensor_tensor(out=mixf[mi][0:64, 0:w], in0=t[0:64, 0:w], in1=ZfT[0:64, 0:w],
                                                   op=ALU.mult), reads=[bt, bZf, bmixf[mi]], writes=[bmixf[mi]])

        def epi_fox_b(c, p0, w, OB, bOB):
            QdT, bQd, QA, bQA, QB, bQB, ZdT, bZd, ZfT, bZf = qsets[c % 2]
            mi = state["mix"] % 2
            rl, t = ep[2], ep[3]
            brl, bt = bep[2], bep[3]
            S.op("dve", lambda e: e.reciprocal(out=rl[64:128, 0:w], in_=OB[0:64, 0:w]), reads=[bOB], writes=[brl])
            S.op("dve", lambda e: e.tensor_tensor(out=t[64:128, 0:w], in0=OB[64:128, 0:w], in1=rl[64:128, 0:w],
                                                  op=ALU.mult), reads=[bOB, brl], writes=[bt])
            S.op("pool", lambda e: e.tensor_tensor(out=mixf[mi][64:128, 0:w], in0=t[64:128, 0:w], in1=ZfT[64:128, 0:w],
                                                   op=ALU.mult), reads=[bt, bZf, bmixf[mi]], writes=[bmixf[mi]])
            S.dma("pool", smixf[mi], cur["mixo"][128:256, p0:p0 + w], mixf[mi][:, 0:w], reads=[bmixf[mi]])
            state["mix"] += 1

        for ci, cfg in enumerate(cfgs):
            if fused and ci in (1, 4, 5):
                S.barrier()
            cur.clear()
            cur.update(cfg)
            setup()
            state["drain"] = True
            for _ in project(0):
                pass
            state["drain"] = False
            for c in range(nchunks):
                nxt = project(c + 1) if c + 1 < nchunks else None
                serial = c < SERIAL_CUT or (c < 2 and cur.get("ut_load"))
                attention(c, None if serial else nxt)
                if nxt is not None:
                    state["drain"] = True
                    for _ in nxt:
                        pass
                    state["drain"] = False
        if fused:
            S.barrier()
            es2.close()
            emit_final(nc, es, S, bank, bbank, h1scr, [(mixall[1], woutp1)], fg, outD)
        S.finish("sp")
    return nc


def emit_final(nc, es, S, bank, bbank, hin, pairs, fg, out):
    NPR = len(pairs)
    sb = lambda name, shape, dt: _sb(nc, es, name, shape, dt)
    WO = [sb(f"fWO{i}", [128, 8, D], BF16) for i in range(NPR)]
    bWO = Buf("fWO")
    wst = [sb(f"fwst{i}", [128, D], F32) for i in range(2)]
    bwst = [Buf(f"fwst{i}") for i in range(2)]
    swst = [S.new_slot(f"fwst{i}") for i in range(2)]
    fgt = sb("fgt", [128, D], F32); bconst = Buf("fconst"); sconst = S.new_slot("fconst")
    cst = sb("fcst", [128, 2], F32)
    xt = [sb(f"fxt{i}", [128, D], F32) for i in range(2)]
    bxt = [Buf(f"fxt{i}") for i in range(2)]
    sxt = [S.new_slot(f"fxt{i}") for i in range(2)]
    mt = [sb(f"fmt{i}", [128, 2, 8, 128], BF16) for i in range(2)]
    bmt = [Buf(f"fmt{i}") for i in range(2)]
    smt = [S.new_slot(f"fmt{i}") for i in range(2)]
    ot = [sb(f"fot{i}", [128, D], F32) for i in range(2)]
    bot = [Buf(f"fot{i}") for i in range(2)]
    sot = [S.new_slot(f"fot{i}") for i in range(2)]
    junk = sb("fjunk", [128, D], F32); bjunk = Buf("fjunk")
    stat = sb("fstat", [128, 4], F32); bstat = Buf("fstat")
    S.dma("sp", sconst, fgt[:, :], fg[0:1, :].partition_broadcast(128), writes=[bconst])
    S.op("pool", lambda e: e.memset(cst[:, 0:1], -0.5), writes=[bconst])
    wi = 0
    for li, wsrc in enumerate([p[1] for p in pairs]):
        for fc in range(8):
            s = wi % 2; wi += 1
            S.dma("sp", swst[s], wst[s][:, :], wsrc[fc * 128:(fc + 1) * 128, :], writes=[bwst[s]])
            S.op("dve", lambda e: e.tensor_copy(out=WO[li][:, fc, :], in_=wst[s][:, :]), reads=[bwst[s]], writes=[bWO])
    pbi = 0
    for t in range(SEQ // 128):
        s = t % 2
        X = xt[s]
        p0 = NMETA + 128 * t
        S.dma("sp", sxt[s], X[:, :], hin[p0:p0 + 128, :], writes=[bxt[s]])
        for li in range(NPR):
            S.dma("sp", smt[s], mt[s][:, li, :, :], pairs[li][0].rearrange("(c p) t -> p c t", p=128)[:, :, p0:p0 + 128],
                  writes=[bmt[s]], same_batch=(li > 0))
        for half in range(2):
            pb, bpb = bank[pbi % 4], bbank[pbi % 4]
            pbi += 1
            k = 0
            for li in range(NPR):
                for fc in range(8):
                    S.op("pe", lambda e: e.matmul(pb[:, :], lhsT=mt[s][:, li, fc, :], rhs=WO[li][:, fc, half * 512:(half + 1) * 512],
                                                  start=(k == 0), stop=(k == 8 * NPR - 1)), reads=[bmt[s], bWO], writes=[bpb])
                    k += 1
            S.op("dve", lambda e: e.tensor_tensor(out=X[:, half * 512:(half + 1) * 512], in0=pb[:, :],
                                                  in1=X[:, half * 512:(half + 1) * 512], op=ALU.add),
                 reads=[bpb, bxt[s]], writes=[bxt[s]])
        S.op("dve", lambda e: e.tensor_tensor(out=junk[:, :], in0=X[:, :], in1=X[:, :], op=ALU.mult), reads=[bxt[s]], writes=[bjunk])
        S.op("dve", lambda e: e.reduce_sum(out=stat[:, 0:1], in_=junk[:, :], axis=mybir.AxisListType.X), reads=[bjunk], writes=[bstat])
        S.op("dve", lambda e: e.tensor_scalar(out=stat[:, 1:2], in0=stat[:, 0:1], scalar1=1.0 / D, scalar2=EPS,
                                              op0=ALU.mult, op1=ALU.add), reads=[bstat], writes=[bstat])
        S.op("act", lambda e: e.activation(out=stat[:, 3:4], in_=stat[:, 1:2], func=AF.Ln), reads=[bstat], writes=[bstat])
        S.op("act", lambda e: e.activation(out=stat[:, 2:3], in_=stat[:, 3:4], func=AF.Exp, scale=-0.5),
             reads=[bstat], writes=[bstat])
        S.op("dve", lambda e: e.scalar_tensor_tensor(out=ot[s][:, :], in0=X[:, :], scalar=stat[:, 2:3], in1=fgt[:, :],
                                                     op0=ALU.mult, op1=ALU.mult),
             reads=[bxt[s], bstat, bconst, bot[s]], writes=[bot[s]])
        S.dma("pool", sot[s], out[t * 128:(t + 1) * 128, :], ot[s][:, :], reads=[bot[s]])


def build_final():
    NTOK = 2048
    nc = bass.Bass("TRN2", target_bir_lowering=False)
    xq = nc.dram_tensor("xq", [NTOK, D], F32, kind="ExternalInput").ap()
    m1 = nc.dram_tensor("m1", [D, NTOK], BF16, kind="ExternalInput").ap()
    m2 = nc.dram_tensor("m2", [D, NTOK], BF16, kind="ExternalInput").ap()
    wo0 = nc.dram_tensor("wo0", [D, D], F32, kind="ExternalInput").ap()
    wo1 = nc.dram_tensor("wo1", [D, D], F32, kind="ExternalInput").ap()
    fg = nc.dram_tensor("fg", [1, D], F32, kind="ExternalInput").ap()
    out = nc.dram_tensor("out", [NTOK, D], F32, kind="ExternalOutput").ap()
    with ExitStack() as es:
        S = Sched(nc, es)
        sb = lambda name, shape, dt: _sb(nc, es, name, shape, dt)
        WO = [sb(f"WO{i}", [128, 8, D], BF16) for i in range(2)]
        bWO = Buf("WO")
        wst = [sb(f"wst{i}", [128, D], F32) for i in range(2)]
        bwst = [Buf(f"wst{i}") for i in range(2)]
        swst = [S.new_slot(f"wst{i}") for i in range(2)]
        fgt = sb("fgt", [128, D], F32); bconst = Buf("const"); sconst = S.new_slot("const")
        cst = sb("cst", [128, 2], F32)
        xt = [sb(f"xt{i}", [128, D], F32) for i in range(2)]
        bxt = [Buf(f"xt{i}") for i in range(2)]
        sxt = [S.new_slot(f"xt{i}") for i in range(2)]
        mt = [sb(f"mt{i}", [128, 2, 8, 128], BF16) for i in range(2)]
        bmt = [Buf(f"mt{i}") for i in range(2)]
        smt = [S.new_slot(f"mt{i}") for i in range(2)]
        ot = [sb(f"ot{i}", [128, D], F32) for i in range(2)]
        bot = [Buf(f"ot{i}") for i in range(2)]
        sot = [S.new_slot(f"ot{i}") for i in range(2)]
        junk = sb("junk", [128, D], F32); bjunk = Buf("junk")
        stat = sb("stat", [128, 4], F32); bstat = Buf("stat")
        bank = [_ps(nc, es, f"bank{i}", [128, 512], F32) for i in range(4)]
        bbank = [Buf(f"bank{i}") for i in range(4)]

        S.dma("sp", sconst, fgt[:, :], fg[0:1, :].partition_broadcast(128), writes=[bconst])
        S.op("pool", lambda e: e.memset(cst[:, 0:1], -0.5), writes=[bconst])
        wi = 0
        for li, wsrc in enumerate((wo0, wo1)):
            for fc in range(8):
                s = wi % 2; wi += 1
                S.dma("sp", swst[s], wst[s][:, :], wsrc[fc * 128:(fc + 1) * 128, :], writes=[bwst[s]])
                S.op("dve", lambda e, s=s, fc=fc, li=li: e.tensor_copy(out=WO[li][:, fc, :], in_=wst[s][:, :]),
                     reads=[bwst[s]], writes=[bWO])
        pbi = 0
        for t in range(NTOK // 128):
            s = t % 2
            X = xt[s]
            S.dma("sp", sxt[s], X[:, :], xq[t * 128:(t + 1) * 128, :], writes=[bxt[s]])
            S.dma("sp", smt[s], mt[s][:, 0, :, :], m1.rearrange("(c p) t -> p c t", p=128)[:, :, t * 128:(t + 1) * 128],
                  writes=[bmt[s]])
            S.dma("sp", smt[s], mt[s][:, 1, :, :], m2.rearrange("(c p) t -> p c t", p=128)[:, :, t * 128:(t + 1) * 128],
                  writes=[bmt[s]], same_batch=True)
            for half in range(2):
                pb, bpb = bank[pbi % 4], bbank[pbi % 4]
                pbi += 1
                k = 0
                for li in range(2):
                    for fc in range(8):
                        S.op("pe", lambda e, li=li, fc=fc, k=k: e.matmul(pb[:, :], lhsT=mt[s][:, li, fc, :],
                                                                          rhs=WO[li][:, fc, half * 512:(half + 1) * 512],
                                                                          start=(k == 0), stop=(k == 15)),
                             reads=[bmt[s], bWO], writes=[bpb])
                        k += 1
                S.op("dve", lambda e: e.tensor_tensor(out=X[:, half * 512:(half + 1) * 512], in0=pb[:, :],
                                                      in1=X[:, half * 512:(half + 1) * 512], op=ALU.add),
                     reads=[bpb, bxt[s]], writes=[bxt[s]])
            S.op("dve", lambda e: e.tensor_tensor(out=junk[:, :], in0=X[:, :], in1=X[:, :], op=ALU.mult),
                 reads=[bxt[s]], writes=[bjunk])
            S.op("dve", lambda e: e.reduce_sum(out=stat[:, 0:1], in_=junk[:, :], axis=mybir.AxisListType.X),
                 reads=[bjunk], writes=[bstat])
            S.op("dve", lambda e: e.tensor_scalar(out=stat[:, 1:2], in0=stat[:, 0:1], scalar1=1.0 / D, scalar2=EPS,
                                                  op0=ALU.mult, op1=ALU.add), reads=[bstat], writes=[bstat])
            S.op("act", lambda e: e.activation(out=stat[:, 3:4], in_=stat[:, 1:2], func=AF.Ln), reads=[bstat], writes=[bstat])
            S.op("act", lambda e: e.activation(out=stat[:, 2:3], in_=stat[:, 3:4], func=AF.Exp, scale=-0.5),
                 reads=[bstat], writes=[bstat])
            S.op("dve", lambda e: e.scalar_tensor_tensor(out=ot[s][:, :], in0=X[:, :], scalar=stat[:, 2:3], in1=fgt[:, :],
                                                         op0=ALU.mult, op1=ALU.mult),
                 reads=[bxt[s], bstat, bconst, bot[s]], writes=[bot[s]])
            S.dma("pool", sot[s], out[t * 128:(t + 1) * 128, :], ot[s][:, :], reads=[bot[s]])
        S.finish("sp")
    return nc


def _core_cols(g):
    dq = [g * 128 + r for r in range(128)]
    dqs = [g * 128 + (r // 64) * 64 + ((r % 64) + 32) % 64 for r in range(128)]
    dk = [512 + x for x in dq]
    dks = [512 + x for x in dqs]
    dv = [1024 + g * 128 + r for r in range(128)]
    a, b = 2 * g, 2 * g + 1
    fva = [3072 + a * 64 + r for r in range(64)]
    fvb = [3072 + b * 64 + r for r in range(64)]
    dz = [1536 + g * 128 + r for r in range(128)]
    fz = [3584 + a * 64 + r for r in range(128)]
    fqa = [2048 + a * 64 + r for r in range(64)]
    fqb = [2048 + b * 64 + r for r in range(64)]
    fka = [2560 + a * 64 + r for r in range(64)]
    fkb = [2560 + b * 64 + r for r in range(64)]
    fl = [4096 + a, 4096 + b]
    cols = dq + dqs + dk + dks + dv + fva + fvb + dz + fz + fqa + fqb + fka + fkb + fl
    assert len(cols) == NCOL
    return np.array(cols)


def _wout_perm():
    rows = []
    for g in range(4):
        rows += list(range(g * 128, (g + 1) * 128))
        rows += list(range(512 + g * 128, 512 + (g + 1) * 128))
    return np.array(rows)


def _rope_tables():
    half = 32
    inv = (10000.0 ** (-np.arange(half, dtype=np.float32) / half)).astype(np.float32)
    pos = np.arange(L, dtype=np.float32)
    ang = (pos[:, None] * inv[None, :]).astype(np.float32)
    cos = np.cos(ang).astype(np.float32).T
    sin = np.sin(ang).astype(np.float32).T
    cosT = np.concatenate([cos, cos, cos, cos], axis=0)
    sinT = np.concatenate([-sin, sin, -sin, sin], axis=0)
    return np.ascontiguousarray(cosT), np.ascontiguousarray(sinT)


_PROGS = {}
SERIAL_CUT = 0
FUSED = True


def _prog(key, fn):
    if key not in _PROGS:
        _PROGS[key] = fn()
    return _PROGS[key]


def kernel(x, meta_tokens, norm_g, w_in, b_forget, lam_q1, lam_k1, lam_q2, lam_k2, subln_g, w_out, final_g):
    x = np.asarray(x, np.float32)
    f = lambda a: np.asarray(a, np.float32)
    meta_tokens, norm_g, w_in, b_forget = f(meta_tokens), f(norm_g), f(w_in), f(b_forget)
    lam_q1, lam_k1, lam_q2, lam_k2, subln_g, w_out, final_g = map(f, (lam_q1, lam_k1, lam_q2, lam_k2, subln_g, w_out, final_g))
    cores = list(range(8))
    cosT, sinT = _rope_tables()
    ident = np.eye(128, dtype=np.float32).astype(NPB)
    tri = np.triu(np.ones((128, 128), np.float32)).astype(NPB)
    perm = _wout_perm()
    hin = [np.ascontiguousarray(np.concatenate([meta_tokens, x[b]], axis=0)) for b in range(2)]

    def layer_maps(layer, mix_prev):
        maps = []
        for core in cores:
            b, g = core // 4, core % 4
            m = {
                "hin": hin[b],
                "win": np.ascontiguousarray(w_in[layer][:, _core_cols(g)]),
                "normg": np.ascontiguousarray(norm_g[layer].reshape(8, 128).T),
                "bfg": np.ascontiguousarray(b_forget[layer][2 * g:2 * g + 2].reshape(1, 2)),
                "lam4": np.concatenate([lam_q1[layer], lam_k1[layer], lam_q2[layer], lam_k2[layer]]).reshape(1, 256),
                "sublng": np.ascontiguousarray(subln_g[layer][g * 128:(g + 1) * 128].reshape(128, 1)),
                "cosT": cosT, "sinT": sinT, "ident": ident, "tri": tri,
            }
            if mix_prev is not None:
                m["mixp"] = mix_prev[b]
                m["woutp"] = np.ascontiguousarray(w_out[layer - 1][perm])
            maps.append(m)
        return maps

    if FUSED:
        nc = _prog("fused", lambda: build_layer(True, 0.0, fused=True))
        shared = {"cosT": cosT, "sinT": sinT, "ident": ident, "tri": tri,
                  "woutp0": np.ascontiguousarray(w_out[0][perm]), "woutp1": np.ascontiguousarray(w_out[1][perm]),
                  "fg": np.ascontiguousarray(final_g.reshape(1, D))}
        for layer in range(2):
            sfx = str(layer)
            shared["win" + sfx] = np.ascontiguousarray(
                np.concatenate([w_in[layer][:, _core_cols(g)] for g in range(4)], axis=0))
            shared["normg" + sfx] = np.ascontiguousarray(norm_g[layer].reshape(8, 128).T)
            shared["bfg" + sfx] = np.ascontiguousarray(b_forget[layer].reshape(4, 2))
            shared["lam4" + sfx] = np.concatenate([lam_q1[layer], lam_k1[layer], lam_q2[layer], lam_k2[layer]]).reshape(1, 256)
            shared["sublng" + sfx] = np.ascontiguousarray(subln_g[layer].reshape(512, 1))
            shared["wfl" + sfx] = np.ascontiguousarray(w_in[layer][:, 4096:4104])
            shared["bf8" + sfx] = np.ascontiguousarray(b_forget[layer].reshape(8, 1))
        maps = [dict(shared, hin=hin[core // 4]) for core in cores]
        res = run_bass_kernel_spmd(nc, maps, core_ids=cores)
        return np.stack([np.asarray(res.results[0]["out"]), np.asarray(res.results[4]["out"])], axis=0).astype(np.float32)

    mixes = []
    mix_prev = None
    for layer in range(2):
        li = 0.8 - 0.6 * math.exp(-0.3 * layer)
        nc = _prog(("layer", layer), lambda: build_layer(layer > 0, li))
        res = run_bass_kernel_spmd(nc, layer_maps(layer, mix_prev), core_ids=cores)
        outs = [np.asarray(r["mixo"]) for r in res.results]
        mix_prev = [np.ascontiguousarray(np.concatenate(outs[4 * b:4 * b + 4], axis=0)) for b in range(2)]
        mixes.append(mix_prev)

    ncf = _prog("final", build_final)
    maps = []
    for core in cores:
        b, q = core // 4, core % 4
        t0 = NMETA + 2048 * q
        maps.append({
            "xq": np.ascontiguousarray(x[b, 2048 * q:2048 * (q + 1)]),
            "m1": np.ascontiguousarray(mixes[0][b][:, t0:t0 + 2048]),
            "m2": np.ascontiguousarray(mixes[1][b][:, t0:t0 + 2048]),
            "wo0": np.ascontiguousarray(w_out[0][perm]),
            "wo1": np.ascontiguousarray(w_out[1][perm]),
            "fg": final_g.reshape(1, D),
        })
    res = run_bass_kernel_spmd(ncf, maps, core_ids=cores)
    out = np.empty((2, SEQ, D), np.float32)
    for core in cores:
        b, q = core // 4, core % 4
        out[b, 2048 * q:2048 * (q + 1)] = np.asarray(res.results[core]["out"])
    return out
```

```python
import math
from contextlib import ExitStack

import numpy as np
import ml_dtypes

import concourse.bass as bass
import concourse.mybir as mybir
from concourse.bass_utils import run_bass_kernel_spmd

F32 = mybir.dt.float32
BF16 = mybir.dt.bfloat16
AF = mybir.ActivationFunctionType
ALU = mybir.AluOpType

D = 1024
SEQ = 8192
NMETA = 16
L = SEQ + NMETA
NCH = 17
NT = 65
NCOL = 1282
EPS = 1e-6
NPB = np.dtype(ml_dtypes.bfloat16)

C_DQ, C_DQS, C_DK, C_DKS = 0, 128, 256, 384
C_V = 512
C_DZ, C_FZ = 768, 896
C_FQA, C_FQB, C_FKA, C_FKB = 1024, 1088, 1152, 1216
C_FLA, C_FLB = 1280, 1281


def tile_pos(t):
    return (0, 16) if t == 0 else (16 + 128 * (t - 1), 128)


def chunk_tiles(c):
    return [0] if c == 0 else list(range(4 * (c - 1) + 1, 4 * c + 1))


def chunk_pos(c):
    return (0, 16) if c == 0 else (16 + 512 * (c - 1), 512)


class Buf:
    def __init__(self, name):
        self.name = name
        self.w = None
        self.r = {}


class Sched:
    def __init__(self, nc, es):
        self.nc = nc
        self.es = es
        self.eng = {"pe": nc.tensor, "act": nc.scalar, "dve": nc.vector, "pool": nc.gpsimd, "sp": nc.sync}
        self.sem = {k: es.enter_context(nc.semaphore("s_" + k)) for k in ("pe", "act", "dve", "pool")}
        self.cnt = {k: 0 for k in self.sem}
        self.seen = {k: {} for k in self.eng}
        self.dsems = []
        self.nsem = 0

    def _deps(self, e, reads, writes):
        deps = {}

        def add(tok, raw=False):
            if tok is None:
                return
            sem, val, owner = tok
            if owner == e and not (raw and e != "pe"):
                return
            if deps.get(sem, 0) < val:
                deps[sem] = val

        for b in reads:
            add(b.w, raw=True)
        for b in writes:
            add(b.w)
            for t in b.r.values():
                add(t)
        for sem, val in deps.items():
            if self.seen[e].get(sem, 0) < val:
                self.eng[e].wait_ge(sem, val)
                self.seen[e][sem] = val

    def op(self, e, fn, reads=(), writes=()):
        self._deps(e, reads, writes)
        ins = fn(self.eng[e])
        self.cnt[e] += 1
        ins.then_inc(self.sem[e], 1)
        tok = (self.sem[e], self.cnt[e], e)
        for b in reads:
            b.r[e] = tok
        for b in writes:
            b.w = tok
            b.r = {}

    def new_slot(self, name):
        sem = self.es.enter_context(self.nc.semaphore("d_" + name))
        slot = {"sem": sem, "total": 0, "name": name}
        self.dsems.append(slot)
        return slot

    def dma(self, q, slot, out, in_, reads=(), writes=(), same_batch=False):
        self._deps(q, reads, writes)
        if not same_batch and slot["total"] > 0:
            if self.seen[q].get(slot["sem"], 0) < slot["total"]:
                self.eng[q].wait_ge(slot["sem"], slot["total"])
                self.seen[q][slot["sem"]] = slot["total"]
        ins = self.eng[q].dma_start(out=out, in_=in_)
        slot["total"] += 16
        ins.then_inc(slot["sem"], 16)
        tok = (slot["sem"], slot["total"], "dma:" + slot["name"])
        if not same_batch:
            slot["bw"], slot["br"] = [], []
        for b in writes:
            b.r = {}
        slot.setdefault("bw", []).extend(writes)
        slot.setdefault("br", []).extend(reads)
        for b in slot["br"]:
            b.r["dma:" + slot["name"]] = tok
        for b in slot["bw"]:
            b.w = tok

    def barrier(self):
        for e in self.eng:
            for k in self.sem:
                if k != e and self.cnt[k] > 0 and self.seen[e].get(self.sem[k], 0) < self.cnt[k]:
                    self.eng[e].wait_ge(self.sem[k], self.cnt[k])
                    self.seen[e][self.sem[k]] = self.cnt[k]
            for slot in self.dsems:
                if slot["total"] > 0 and self.seen[e].get(slot["sem"], 0) < slot["total"]:
                    self.eng[e].wait_ge(slot["sem"], slot["total"])
                    self.seen[e][slot["sem"]] = slot["total"]

    def finish(self, q="sp"):
        for slot in self.dsems:
            if slot["total"] > 0:
                self.eng[q].wait_ge(slot["sem"], slot["total"])


def _sb(nc, es, name, shape, dt):
    return es.enter_context(nc.sbuf_tensor(name, shape, dt))


def _ps(nc, es, name, shape, dt):
    return es.enter_context(nc.psum_tensor(name, shape, dt))


def build_layer(has_prev, lambda_init, nchunks=NCH, dbg=False, fused=False):
    nc = bass.Bass("TRN2", target_bir_lowering=False)
    hin = nc.dram_tensor("hin", [L, D], F32, kind="ExternalInput").ap()
    cosT = nc.dram_tensor("cosT", [128, L], F32, kind="ExternalInput").ap()
    sinT = nc.dram_tensor("sinT", [128, L], F32, kind="ExternalInput").ap()
    identD = nc.dram_tensor("ident", [128, 128], BF16, kind="ExternalInput").ap()
    triD = nc.dram_tensor("tri", [128, 128], BF16, kind="ExternalInput").ap()
    cfgs = []
    if fused:
        mixall = [nc.dram_tensor("mixall%d" % l, [D, L], BF16).ap() for l in range(2)]
        h1scr = nc.dram_tensor("h1scr", [L, D], F32).ap()
        utscr = nc.dram_tensor("utscr", [128, 8, L], BF16).ap()
        stgall = nc.dram_tensor("stgall", [8, 6, L], BF16).ap()
        woutp0 = nc.dram_tensor("woutp0", [D, D], F32, kind="ExternalInput").ap()
        woutp1 = nc.dram_tensor("woutp1", [D, D], F32, kind="ExternalInput").ap()
        fg = nc.dram_tensor("fg", [1, D], F32, kind="ExternalInput").ap()
        outD = nc.dram_tensor("out", [SEQ, D], F32, kind="ExternalOutput").ap()
        for l in range(2):
            winL = nc.dram_tensor("win%d" % l, [4 * D, NCOL], F32, kind="ExternalInput").ap()
            normgL = nc.dram_tensor("normg%d" % l, [128, 8], F32, kind="ExternalInput").ap()
            bfgL = nc.dram_tensor("bfg%d" % l, [4, 2], F32, kind="ExternalInput").ap()
            lam4L = nc.dram_tensor("lam4%d" % l, [1, 256], F32, kind="ExternalInput").ap()
            sublngL = nc.dram_tensor("sublng%d" % l, [4 * 128, 1], F32, kind="ExternalInput").ap()
            wflL = nc.dram_tensor("wfl%d" % l, [D, 8], F32, kind="ExternalInput").ap()
            bf8L = nc.dram_tensor("bf8%d" % l, [8, 1], F32, kind="ExternalInput").ap()
            for g in range(4):
                cfg = {"hin": hin, "win": winL[g * D:(g + 1) * D, :], "normg": normgL, "bfg": bfgL[g:g + 1, :],
                       "lam4": lam4L, "sublng": sublngL[g * 128:(g + 1) * 128, :],
                       "has_prev": l > 0, "li": 0.8 - 0.6 * math.exp(-0.3 * l),
                       "mixo": mixall[l][g * 256:(g + 1) * 256, :], "layer": l, "g": g}
                cfg["utscr"] = utscr
                cfg["stgall"] = stgall
                cfg["wfl"] = wflL
                cfg["bf8"] = bf8L
                cfg["fl_store"] = (g == 0)
                cfg["fl_load"] = (g > 0)
                cfg["ut_store"] = (g == 0)
                cfg["ut_load"] = (g > 0)
                if l > 0 and g == 0:
                    cfg["mixp"] = mixall[0]
                    cfg["woutp"] = woutp0
                    cfg["h1out"] = h1scr
                elif l > 0:
                    cfg["hin"] = h1scr
                    cfg["has_prev"] = False
                cfgs.append(cfg)
    else:
        cfg = {
            "hin": hin,
            "win": nc.dram_tensor("win", [D, NCOL], F32, kind="ExternalInput").ap(),
            "normg": nc.dram_tensor("normg", [128, 8], F32, kind="ExternalInput").ap(),
            "bfg": nc.dram_tensor("bfg", [1, 2], F32, kind="ExternalInput").ap(),
            "lam4": nc.dram_tensor("lam4", [1, 256], F32, kind="ExternalInput").ap(),
            "sublng": nc.dram_tensor("sublng", [128, 1], F32, kind="ExternalInput").ap(),
            "has_prev": has_prev, "li": lambda_init, "layer": 0, "g": 0,
        }
        if has_prev:
            cfg["mixp"] = nc.dram_tensor("mixp", [D, L], BF16, kind="ExternalInput").ap()
            cfg["woutp"] = nc.dram_tensor("woutp", [D, D], F32, kind="ExternalInput").ap()
        cfg["mixo"] = nc.dram_tensor("mixo", [256, L], BF16, kind="ExternalOutput").ap()
        cfgs.append(cfg)
    has_prev = has_prev or fused
    cur = {}

    with ExitStack() as es:
        S = Sched(nc, es)
        es2 = es.enter_context(ExitStack())
        sb = lambda name, shape, dt: _sb(nc, es2, name, shape, dt)

        W = sb("W", [128, 8, NCOL], BF16)
        bW = Buf("W")
        if has_prev:
            WO = sb("WO", [128, 8, D], BF16)
            bWO = Buf("WO")
        KdT = sb("KdT", [128, L], BF16)
        KA = sb("KA", [70, L], BF16)
        KB = sb("KB", [70, L], BF16)
        Vd = sb("Vd", [128, NT, 128], BF16)
        Vf = sb("Vf", [128, NT, 192], BF16)
        bKd = [Buf(f"Kd{c}") for c in range(NCH)]
        bKA = [Buf(f"KA{c}") for c in range(NCH)]
        bKB = [Buf(f"KB{c}") for c in range(NCH)]
        bVd = [Buf(f"Vd{c}") for c in range(NCH)]
        bVf = [Buf(f"Vf{c}") for c in range(NCH)]

        qsets = []
        for qi in range(2):
            qsets.append((sb(f"QdT{qi}", [128, 512], BF16), Buf(f"Qd{qi}"),
                          sb(f"QA{qi}", [70, 512], BF16), Buf(f"QA{qi}"),
                          sb(f"QB{qi}", [70, 512], BF16), Buf(f"QB{qi}"),
                          sb(f"ZdT{qi}", [128, 512], BF16), Buf(f"Zd{qi}"),
                          sb(f"ZfT{qi}", [128, 512], BF16), Buf(f"Zf{qi}")))

        xt = [sb(f"xt{i}", [128, D], F32) for i in range(2)]
        bxt = [Buf(f"xt{i}") for i in range(2)]
        sxt = [S.new_slot(f"xt{i}") for i in range(2)]
        sh1 = [S.new_slot(f"h1o{i}") for i in range(2)]
        sutl = S.new_slot("utl")
        suts = S.new_slot("uts")
        ub = [sb(f"ub{i}", [128, D], BF16) for i in range(2)]
        bub = [Buf(f"ub{i}") for i in range(2)]
        uT = sb("uT", [128, 8, 512], BF16); buT = Buf("uT")
        stat = sb("stat", [128, 8], F32); bstat = Buf("stat")
        if has_prev:
            mp = [sb(f"mp{i}", [128, 8, 512], BF16) for i in range(1)]
            bmp = [Buf(f"mp{i}") for i in range(1)]
            smp = [S.new_slot(f"mp{i}") for i in range(1)]
        rope = [sb(f"rope{i}", [128, 2, 512], F32) for i in range(1)]
        brope = [Buf(f"rope{i}") for i in range(1)]
        srope = [S.new_slot(f"rope{i}") for i in range(1)]

        ident = sb("identS", [128, 128], BF16)
        tri = sb("triS", [128, 128], BF16)
        onesb = sb("onesb", [128, 128], BF16)
        onesf = sb("onesf", [128, 128], F32)
        gT = sb("gT", [128, 8], F32)
        cst = sb("cst", [128, 16], F32)
        lamv = sb("lamv", [128, 256], F32)
        bconst = Buf("const")
        sconst = S.new_slot("const")
        bf2 = sb("bf2", [8, 1], F32)
        Wfl = sb("Wfl", [128, 8, 8], BF16); bWfl = Buf("Wfl")
        sstg = S.new_slot("stgo")

        rows = sb("rows", [8, 2, 512], F32); brows = Buf("rows")
        stg = sb("stg", [8, 6, 512], BF16); bstg = Buf("stg")
        onesrow = sb("onesrow", [8, 512], F32)
        carry = sb("carry", [8, 1], F32)
        saug = S.new_slot("aug")

        NP = 6
        Pb = [sb(f"P{i}", [128, 512], BF16) for i in range(NP)]
        bP = [Buf(f"P{i}") for i in range(NP)]
        ep = [sb(f"ep{i}", [128, 512], F32) for i in range(6)]
        bep = [Buf(f"ep{i}") for i in range(6)]
        t1, bt1, t2, bt2 = ep[4], bep[4], ep[5], bep[5]
        mixd = [sb(f"mixd{i}", [128, 512], BF16) for i in range(2)]
        bmixd = [Buf(f"mixd{i}") for i in range(2)]
        smixd = [S.new_slot(f"mixd{i}") for i in range(2)]
        mixf = [sb(f"mixf{i}", [128, 512], BF16) for i in range(2)]
        bmixf = [Buf(f"mixf{i}") for i in range(2)]
        smixf = [S.new_slot(f"mixf{i}") for i in range(2)]

        bank = [_ps(nc, es, f"bank{i}", [128, 512], F32) for i in range(8)]
        bbank = [Buf(f"bank{i}") for i in range(8)]

        acc = [sb(f"acc{i}", [128, 512], F32) for i in range(2)]
        bacc = [Buf(f"acc{i}") for i in range(2)]
        ones32 = sb("ones32", [128, 128], F32)

        def setup():
            first = not state.get("setup_done")
            state["setup_done"] = True
            S.dma("sp", sconst, gT[:, :], cur["normg"][:, :], writes=[bconst])
            if first:
                S.dma("sp", sconst, ident[:, :], identD[:, :], writes=[bconst], same_batch=True)
                S.dma("sp", sconst, tri[:, :], triD[:, :], writes=[bconst], same_batch=True)
            S.dma("sp", sconst, lamv[:, :], cur["lam4"][0:1, :].partition_broadcast(128), writes=[bconst], same_batch=True)
            S.dma("sp", sconst, cst[:, 4:5], cur["sublng"][:, :], writes=[bconst], same_batch=True)
            if cur.get("fl_store"):
                S.dma("sp", sconst, bf2[0:8, 0:1], cur["bf8"][:, :], writes=[bconst], same_batch=True)
                S.dma("sp", sconst, ep[3][:, 0:64].rearrange("p (c h) -> p c h", c=8),
                      cur["wfl"].rearrange("(c p) h -> p c h", p=128), writes=[bconst, bep[3]], same_batch=True)
            else:
                S.dma("sp", sconst, bf2[0:1, 0:1], cur["bfg"][0:1, 0:1], writes=[bconst], same_batch=True)
                S.dma("sp", sconst, bf2[1:2, 0:1], cur["bfg"][0:1, 1:2], writes=[bconst], same_batch=True)

            S.op("pool", lambda e: e.memset(carry[:, :], 0.0), writes=[brows])
            if first:
                S.op("pool", lambda e: e.memset(onesb[:, :], 1.0), writes=[bconst])
                S.op("pool", lambda e: e.memset(onesf[:, :], 1.0 / 128.0), writes=[bconst])
                S.op("pool", lambda e: e.memset(ones32[:, :], 1.0), writes=[bconst])
                S.op("pool", lambda e: e.memset(cst[:, 0:1], -0.5), writes=[bconst])
                S.op("pool", lambda e: e.memset(cst[:, 1:2], EPS), writes=[bconst])
                S.op("pool", lambda e: e.memset(onesrow[:, :], 1.0), writes=[bconst])
                S.op("pool", lambda e: e.memset(Vf[:, :, :], 1.0), writes=bVf)
                S.op("pool", lambda e: e.memset(KA[64:70, :], 1.0), writes=bKA)
                S.op("pool", lambda e: e.memset(KB[64:70, :], 1.0), writes=bKB)
                for qs_ in qsets:
                    S.op("pool", lambda e: e.memset(qs_[2][64:70, :], 1.0), writes=[qs_[3]])
                    S.op("pool", lambda e: e.memset(qs_[4][64:70, :], 1.0), writes=[qs_[5]])

            S.op("dve", lambda e: e.tensor_tensor(out=ep[2][:, 0:64], in0=lamv[:, 0:64], in1=lamv[:, 64:128], op=ALU.mult),
                 reads=[bconst], writes=[bep[2]])
            S.op("dve", lambda e: e.tensor_tensor(out=ep[2][:, 64:128], in0=lamv[:, 128:192], in1=lamv[:, 192:256],
                                                  op=ALU.mult), reads=[bconst], writes=[bep[2]])
            S.op("dve", lambda e: e.reduce_sum(out=cst[:, 5:6], in_=ep[2][:, 0:64], axis=mybir.AxisListType.X),
                 reads=[bep[2]], writes=[bconst])
            S.op("dve", lambda e: e.reduce_sum(out=cst[:, 6:7], in_=ep[2][:, 64:128], axis=mybir.AxisListType.X),
                 reads=[bep[2]], writes=[bconst])
            S.op("act", lambda e: e.activation(out=cst[:, 7:9], in_=cst[:, 5:7], func=AF.Exp), reads=[bconst], writes=[bconst])
            S.op("dve", lambda e: e.tensor_tensor(out=cst[:, 9:10], in0=cst[:, 8:9], in1=cst[:, 7:8], op=ALU.subtract),
                 reads=[bconst], writes=[bconst])
            S.op("dve", lambda e: e.tensor_scalar(out=cst[:, 3:4], in0=cst[:, 9:10], scalar1=-float(cur["li"]),
                                                  scalar2=None, op0=ALU.add), reads=[bconst], writes=[bconst])
            S.op("dve", lambda e: e.tensor_scalar(out=cst[:, 2:3], in0=cst[:, 4:5], scalar1=float(1.0 - cur["li"]),
                                                  scalar2=None, op0=ALU.mult), reads=[bconst], writes=[bconst])
            if cur.get("fl_store"):
                S.op("dve", lambda e: e.tensor_copy(out=Wfl[:, :, :], in_=ep[3][:, 0:64].rearrange("p (c h) -> p c h", c=8)),
                     reads=[bconst, bep[3]], writes=[bWfl])
            S.op("dve", lambda e: e.tensor_scalar(out=bf2[:, :], in0=bf2[:, :], scalar1=-1.0, scalar2=None, op0=ALU.mult),
                 reads=[bconst], writes=[bconst])

            for fc in range(8):
                S.dma("sp", sxt[0], xt[0][:, :], cur["win"][fc * 128:(fc + 1) * 128, 0:D], writes=[bxt[0]])
                S.dma("sp", sxt[1], xt[1][:, 0:NCOL - D], cur["win"][fc * 128:(fc + 1) * 128, D:NCOL], writes=[bxt[1]])
                S.op("dve", lambda e: e.tensor_copy(out=W[:, fc, 0:D], in_=xt[0][:, :]), reads=[bxt[0]], writes=[bW])
                S.op("dve", lambda e: e.tensor_scalar(out=W[:, fc, C_DZ:C_DZ + 256], in0=xt[0][:, C_DZ:C_DZ + 256],
                                                      scalar1=0.5, scalar2=None, op0=ALU.mult), reads=[bxt[0]], writes=[bW])
                S.op("pool", lambda e: e.tensor_copy(out=W[:, fc, D:NCOL], in_=xt[1][:, 0:NCOL - D]), reads=[bxt[1]],
                     writes=[bW])
            if cur["has_prev"]:
                for fc in range(8):
                    s = fc % 2
                    S.dma("sp", sxt[s], xt[s][:, :], cur["woutp"][fc * 128:(fc + 1) * 128, :], writes=[bxt[s]])
                    S.op("dve", lambda e: e.tensor_copy(out=WO[:, fc, :], in_=xt[s][:, :]), reads=[bxt[s]], writes=[bWO])

        state = {"xt": 0, "pb": 0, "sbank": 0, "p": 0, "mix": 0}

        def proj_bank():
            pool = (3, 5, 0, 1, 6, 7, 2, 4) if state.get("drain") else (3, 5)
            i = pool[state["pb"] % len(pool)]
            state["pb"] += 1
            return bank[i], bbank[i]

        def s_bank(pool=(0, 1)):
            i = pool[state["sbank"] % len(pool)]
            state["sbank"] += 1
            return bank[i], bbank[i]

        def project(c):
            QdT, bQd, QA, bQA, QB, bQB, ZdT, bZd, ZfT, bZf = qsets[c % 2]
            p0, w = chunk_pos(c)
            tiles = chunk_tiles(c)
            rs = 0
            S.dma("sp", srope[rs], rope[rs][:, 0, 0:w], cosT[:, p0:p0 + w], writes=[brope[rs]])
            S.dma("sp", srope[rs], rope[rs][:, 1, 0:w], sinT[:, p0:p0 + w], writes=[brope[rs]], same_batch=True)
            if cur["has_prev"]:
                ms = 0
                S.dma("sp", smp[ms], mp[ms][:, :, 0:w],
                      cur["mixp"].rearrange("(c p) t -> p c t", p=128)[:, :, p0:p0 + w], writes=[bmp[ms]])
            tinfo = []
            if cur.get("ut_load"):
                S.dma("sp", sutl, uT[:, :, 0:w], cur["utscr"][:, :, p0:p0 + w], writes=[buT])
                yield True
            else:
                for j, t in enumerate(tiles):
                    tp, n = tile_pos(t)
                    xs = state["xt"] % 2
                    state["xt"] += 1
                    tinfo.append((j, t, tp, n, xs))
            def load(k):
                j, t, tp, n, xs = tinfo[k]
                S.dma("sp", sxt[xs], xt[xs][0:n, :], cur["hin"][tp:tp + n, :], writes=[bxt[xs]])
            if tinfo:
                load(0)
                yield True
            for k, (j, t, tp, n, xs) in enumerate(tinfo):
                X, bX = xt[xs], bxt[xs]
                U, bU = ub[xs], bub[xs]
                if k + 1 < len(tinfo):
                    load(k + 1)
                if cur["has_prev"]:
                    for half in range(2):
                        pb, bpb = proj_bank()
                        for fc in range(8):
                            S.op("pe", lambda e: e.matmul(
                                pb[0:n, :], lhsT=mp[ms][:, fc, 128 * j:128 * j + n],
                                rhs=WO[:, fc, half * 512:(half + 1) * 512], start=(fc == 0), stop=(fc == 7)),
                                reads=[bmp[ms], bWO], writes=[bpb])
                        yield False
                        S.op("dve", lambda e: e.tensor_tensor(
                            out=X[0:n, half * 512:(half + 1) * 512], in0=pb[0:n, :],
                            in1=X[0:n, half * 512:(half + 1) * 512], op=ALU.add), reads=[bpb, bX], writes=[bX])
                        yield True
                if cur.get("h1out") is not None:
                    S.dma("pool", sh1[xs], cur["h1out"][tp:tp + n, :], X[0:n, :], reads=[bX])
                for hf in range(2):
                    S.op("dve", lambda e: e.tensor_tensor(out=ep[2 + hf][0:n, :], in0=X[0:n, 512 * hf:512 * hf + 512],
                                                          in1=X[0:n, 512 * hf:512 * hf + 512], op=ALU.mult),
                         reads=[bX], writes=[bep[2 + hf]])
                    S.op("dve", lambda e: e.reduce_sum(out=stat[0:n, 4 + hf:5 + hf], in_=ep[2 + hf][0:n, :],
                                                       axis=mybir.AxisListType.X), reads=[bep[2 + hf]], writes=[bstat])
                S.op("dve", lambda e: e.tensor_tensor(out=stat[0:n, 0:1], in0=stat[0:n, 4:5], in1=stat[0:n, 5:6], op=ALU.add),
                     reads=[bstat], writes=[bstat])
                S.op("dve", lambda e: e.tensor_scalar(out=stat[0:n, 1:2], in0=stat[0:n, 0:1], scalar1=1.0 / D,
                                                      scalar2=EPS, op0=ALU.mult, op1=ALU.add),
                     reads=[bstat], writes=[bstat])
                yield True
                S.op("act", lambda e: e.activation(out=stat[0:n, 3:4], in_=stat[0:n, 1:2], func=AF.Ln),
                     reads=[bstat], writes=[bstat])
                S.op("act", lambda e: e.activation(out=stat[0:n, 2:3], in_=stat[0:n, 3:4], func=AF.Exp, scale=-0.5),
                     reads=[bstat], writes=[bstat])
                S.op("act", lambda e: e.activation(out=U[0:n, :], in_=X[0:n, :], func=AF.Copy, scale=stat[0:n, 2:3]),
                     reads=[bX, bstat], writes=[bU])
                yield True
                tb, btb = proj_bank()
                tbv = tb[:, :].bitcast(BF16)
                for fc in range(8):
                    S.op("pe", lambda e: e.transpose(out=tbv[:, fc * 128:fc * 128 + n],
                                                     in_=U[0:n, fc * 128:(fc + 1) * 128], identity=ident[0:n, 0:n]),
                         reads=[bU, bconst], writes=[btb])
                yield False
                for fc in range(8):
                    S.op("dve", lambda e: e.tensor_scalar(out=uT[:, fc, 128 * j:128 * j + n],
                                                          in0=tbv[:, fc * 128:fc * 128 + n],
                                                          scalar1=gT[:, fc:fc + 1], scalar2=None, op0=ALU.mult),
                         reads=[btb, bconst], writes=[buT])
                yield True
            if cur.get("ut_store"):
                S.dma("pool", suts, cur["utscr"][:, :, p0:p0 + w], uT[:, :, 0:w], reads=[buT])

            def fm(col, M):
                pb, bpb = proj_bank()
                for fc in range(8):
                    S.op("pe", lambda e: e.matmul(pb[0:M, 0:w], lhsT=W[:, fc, col:col + M], rhs=uT[:, fc, 0:w],
                                                  start=(fc == 0), stop=(fc == 7)),
                         reads=[bW, buT], writes=[bpb])
                return pb, bpb

            for (cn, cs, dst, bdst, dcol) in ((C_DQ, C_DQS, QdT, bQd, 0), (C_DK, C_DKS, KdT, bKd[c], p0)):
                pa, bpa = fm(cn, 128)
                pbk, bpbk = fm(cs, 128)
                yield False
                S.op("dve", lambda e: e.tensor_tensor(out=t1[:, 0:w], in0=pa[:, 0:w], in1=rope[rs][:, 0, 0:w],
                                                      op=ALU.mult), reads=[bpa, brope[rs]], writes=[bt1])
                S.op("dve", lambda e: e.tensor_tensor(out=t2[:, 0:w], in0=pbk[:, 0:w], in1=rope[rs][:, 1, 0:w],
                                                      op=ALU.mult), reads=[bpbk, brope[rs]], writes=[bt2])
                S.op("dve", lambda e: e.tensor_tensor(out=dst[:, dcol:dcol + w], in0=t1[:, 0:w], in1=t2[:, 0:w],
                                                      op=ALU.add), reads=[bt1, bt2], writes=[bdst])
                yield True
            for (col, dA, bdA, dB, bdB, dcol) in ((C_FQA, QA, bQA, QB, bQB, 0), (C_FKA, KA, bKA[c], KB, bKB[c], p0)):
                pa, bpa = fm(col, 128)
                yield False
                S.op("dve", lambda e: e.tensor_copy(out=dA[0:64, dcol:dcol + w], in_=pa[0:64, 0:w]),
                     reads=[bpa], writes=[bdA])
                S.op("dve", lambda e: e.tensor_copy(out=dB[0:64, dcol:dcol + w], in_=pa[64:128, 0:w]),
                     reads=[bpa], writes=[bdB])
                yield True
            for (col, dst, bdst) in ((C_DZ, ZdT, bZd), (C_FZ, ZfT, bZf)):
                pa, bpa = fm(col, 128)
                yield False
                S.op("act", lambda e: e.activation(out=t1[:, 0:w], in_=pa[:, 0:w], func=AF.Tanh),
                     reads=[bpa], writes=[bt1])
                yield False
                S.op("dve", lambda e: e.scalar_tensor_tensor(out=dst[:, 0:w], in0=t1[:, 0:w], scalar=1.0, in1=pa[:, 0:w],
                                                             op0=ALU.add, op1=ALU.mult),
                     reads=[bt1, bpa], writes=[bdst])
                yield True
            gofs = 2 * cur.get("g", 0)
            if cur.get("fl_load"):
                first = True
                for hh in range(2):
                    Qt, bQt = (QA, bQA) if hh == 0 else (QB, bQB)
                    Kt, bKt = (KA, bKA[c]) if hh == 0 else (KB, bKB[c])
                    for i in range(3):
                        S.dma("sp", saug, Qt[64 + i:65 + i, 0:w], cur["stgall"][gofs + hh:gofs + hh + 1, i, p0:p0 + w],
                              writes=[bQt], same_batch=not first)
                        first = False
                        S.dma("sp", saug, Kt[67 + i:68 + i, p0:p0 + w],
                              cur["stgall"][gofs + hh:gofs + hh + 1, 3 + i, p0:p0 + w], writes=[bKt], same_batch=True)
                yield True
            else:
                NHh = 8 if cur.get("fl_store") else 2
                if cur.get("fl_store"):
                    pa, bpa = proj_bank()
                    for fc in range(8):
                        S.op("pe", lambda e: e.matmul(pa[0:8, 0:w], lhsT=Wfl[:, fc, 0:8], rhs=uT[:, fc, 0:w],
                                                      start=(fc == 0), stop=(fc == 7)), reads=[bWfl, buT], writes=[bpa])
                else:
                    pa, bpa = fm(C_FLA, 2)
                yield False
                ra, rb = rows[0:NHh, 0, 0:w], rows[0:NHh, 1, 0:w]
                S.op("act", lambda e: e.activation(out=ra, in_=pa[0:NHh, 0:w], func=AF.Exp, scale=-1.0,
                                                   bias=bf2[0:NHh, 0:1]), reads=[bpa, bconst], writes=[brows])
                S.op("act", lambda e: e.activation(out=ra, in_=ra, func=AF.Ln, bias=1.0, scale=1.0),
                     reads=[brows], writes=[brows])
                yield True
                S.op("dve", lambda e: e.tensor_tensor_scan(out=rb, data0=onesrow[0:NHh, 0:w], data1=ra,
                                                           initial=carry[0:NHh, 0:1], op0=ALU.mult, op1=ALU.subtract),
                     reads=[brows, bconst], writes=[brows])
                S.op("dve", lambda e: e.tensor_copy(out=carry[0:NHh, 0:1], in_=rows[0:NHh, 1, w - 1:w]),
                     reads=[brows], writes=[brows])
                S.op("dve", lambda e: e.tensor_scalar(out=ra, in0=rb, scalar1=8.0, scalar2=None, op0=ALU.mult),
                     reads=[brows], writes=[brows])
                S.op("dve", lambda e: e.tensor_copy(out=stg[0:NHh, 0, 0:w], in_=ra), reads=[brows], writes=[bstg])
                S.op("dve", lambda e: e.tensor_tensor(out=rb, in0=ra, in1=stg[0:NHh, 0, 0:w], op=ALU.subtract),
                     reads=[brows, bstg], writes=[brows])
                S.op("dve", lambda e: e.tensor_copy(out=stg[0:NHh, 1, 0:w], in_=rb), reads=[brows], writes=[bstg])
                S.op("dve", lambda e: e.tensor_tensor(out=ra, in0=rb, in1=stg[0:NHh, 1, 0:w], op=ALU.subtract),
                     reads=[brows, bstg], writes=[brows])
                S.op("dve", lambda e: e.tensor_copy(out=stg[0:NHh, 2, 0:w], in_=ra), reads=[brows], writes=[bstg])
                S.op("dve", lambda e: e.tensor_scalar(out=stg[0:NHh, 3:6, 0:w], in0=stg[0:NHh, 0:3, 0:w], scalar1=-1.0,
                                                      scalar2=None, op0=ALU.mult), reads=[bstg], writes=[bstg])
                yield True
                if cur.get("fl_store"):
                    S.dma("pool", sstg, cur["stgall"][:, :, p0:p0 + w], stg[0:8, :, 0:w], reads=[bstg])
                first = True
                for hh in range(2):
                    Qt, bQt = (QA, bQA) if hh == 0 else (QB, bQB)
                    Kt, bKt = (KA, bKA[c]) if hh == 0 else (KB, bKB[c])
                    for i in range(3):
                        S.dma("sp", saug, Qt[64 + i:65 + i, 0:w], stg[gofs + hh:gofs + hh + 1, i, 0:w], reads=[bstg],
                              writes=[bQt], same_batch=not first)
                        first = False
                        S.dma("sp", saug, Kt[67 + i:68 + i, p0:p0 + w], stg[gofs + hh:gofs + hh + 1, 3 + i, 0:w],
                              reads=[bstg], writes=[bKt], same_batch=True)
                yield True
            for j, t in enumerate(tiles):
                tp, n = tile_pos(t)
                pb, bpb = proj_bank()
                for fc in range(8):
                    S.op("pe", lambda e: e.matmul(pb[0:n, 0:256], lhsT=uT[:, fc, 128 * j:128 * j + n],
                                                  rhs=W[:, fc, C_V:C_V + 256], start=(fc == 0), stop=(fc == 7)),
                         reads=[bW, buT], writes=[bpb])
                yield False
                S.op("dve", lambda e: e.tensor_copy(out=Vd[0:n, t, :], in_=pb[0:n, 0:128]), reads=[bpb], writes=[bVd[c]])
                S.op("dve", lambda e: e.tensor_copy(out=Vf[0:n, t, 0:64], in_=pb[0:n, 128:192]), reads=[bpb],
                     writes=[bVf[c]])
                S.op("dve", lambda e: e.tensor_copy(out=Vf[0:n, t, 128:192], in_=pb[0:n, 192:256]), reads=[bpb],
                     writes=[bVf[c]])
                yield True

        def attention(c, nxt=None):
            QdT, bQd, QA, bQA, QB, bQB, ZdT, bZd, ZfT, bZf = qsets[c % 2]
            p0, w = chunk_pos(c)
            last_t = chunk_tiles(c)[-1]
            kts = list(range(0, last_t + 1))
            first_diag = chunk_tiles(c)[0]

            def kinfo(kt):
                kp, kn = tile_pos(kt)
                if kt >= first_diag:
                    i = kt - first_diag
                    return kp, kn, 128 * i, True
                return kp, kn, 0, False

            def kchunk(kt):
                return 0 if kt == 0 else (kt - 1) // 4 + 1

            O1, bO1 = bank[2], bbank[2]
            L1, bL1 = bank[3], bbank[3]
            O2, bO2 = bank[4], bbank[4]
            L2, bL2 = bank[5], bbank[5]
            OA, bOA = bank[6], bbank[6]
            OB, bOB = bank[7], bbank[7]

            steps = []
            for kt in kts:
                steps.append(("d1", kt))
                steps.append(("d2", kt))
            for kt in kts:
                steps.append(("fA", kt))
            for kt in kts:
                steps.append(("fB", kt))
            pend = {}

            def qk(si):
                u, kt = steps[si]
                kp, kn, qlo, diag = kinfo(kt)
                kc = kchunk(kt)
                if c == 0:
                    spool = (0, 1)
                elif u in ("d1", "d2"):
                    spool = (0, 1, 6, 7)
                elif u == "fA":
                    spool = (0, 1, 7)
                else:
                    spool = (0, 1, 2, 4)
                sbk, bsbk = s_bank(spool)
                pi = state["p"] % NP
                state["p"] += 1
                P, bPi = Pb[pi], bP[pi]
                if u == "d1":
                    lhsT, rhs, rd = KdT[0:64, kp:kp + kn], QdT[0:64, qlo:w], [bKd[kc], bQd]
                elif u == "d2":
                    lhsT, rhs, rd = KdT[64:128, kp:kp + kn], QdT[64:128, qlo:w], [bKd[kc], bQd]
                elif u == "fA":
                    lhsT, rhs, rd = KA[0:70, kp:kp + kn], QA[0:70, qlo:w], [bKA[kc], bQA]
                else:
                    lhsT, rhs, rd = KB[0:70, kp:kp + kn], QB[0:70, qlo:w], [bKB[kc], bQB]
                S.op("pe", lambda e: e.matmul(sbk[0:kn, qlo:w], lhsT=lhsT, rhs=rhs, start=True, stop=True),
                     reads=rd, writes=[bsbk])
                pend[si] = (P, bPi, sbk, bsbk)

            def qk_post(si):
                u, kt = steps[si]
                kp, kn, qlo, diag = kinfo(kt)
                P, bPi, sbk, bsbk = pend[si]
                S.op("act", lambda e: e.activation(out=P[0:kn, qlo:w], in_=sbk[0:kn, qlo:w], func=AF.Exp, scale=0.125),
                     reads=[bsbk], writes=[bPi])
                if diag:
                    dw = min(128, w - qlo)
                    S.op("pool", lambda e: e.tensor_tensor(out=P[0:kn, qlo:qlo + dw], in0=P[0:kn, qlo:qlo + dw],
                                                           in1=tri[0:kn, 0:dw], op=ALU.mult),
                         reads=[bPi, bconst], writes=[bPi])

            def pv(si):
                u, kt = steps[si]
                kp, kn, qlo, diag = kinfo(kt)
                kc = kchunk(kt)
                P, bPi = pend.pop(si)[0:2]
                st, sp_ = (kt == kts[0]), (kt == kts[-1])
                rhs = P[0:kn, qlo:w]
                if u in ("d1", "d2"):
                    O, bO, Lb, bLb = (O1, bO1, L1, bL1) if u == "d1" else (O2, bO2, L2, bL2)
                    S.op("pe", lambda e: e.matmul(O[:, qlo:w], lhsT=Vd[0:kn, kt, :], rhs=rhs, start=st, stop=sp_),
                         reads=[bVd[kc], bPi], writes=[bO])
                    ai = 0 if u == "d1" else 1
                    S.op("dve", lambda e: e.tensor_tensor(out=acc[ai][0:kn, qlo:w], in0=acc[ai][0:kn, qlo:w],
                                                          in1=P[0:kn, qlo:w], op=ALU.add),
                         reads=[bPi, bacc[ai]], writes=[bacc[ai]])
                elif u == "fA":
                    S.op("pe", lambda e: e.matmul(OA[:, qlo:w], lhsT=Vf[0:kn, kt, 0:128], rhs=rhs, start=st, stop=sp_),
                         reads=[bVf[kc], bPi], writes=[bOA])
                else:
                    S.op("pe", lambda e: e.matmul(OB[:, qlo:w], lhsT=Vf[0:kn, kt, 64:192], rhs=rhs, start=st, stop=sp_),
                         reads=[bVf[kc], bPi], writes=[bOB])

            for ai in range(2):
                S.op("pool", lambda e: e.memset(acc[ai][:, 0:w], 0.0), writes=[bacc[ai]])
            LA = 2 if c >= 1 else 1
            nd = 2 * len(kts)
            nfa = nd + len(kts)
            groups = []
            i = 0
            while i < len(steps):
                if steps[i][0] == "d1":
                    groups.append([i, i + 1])
                    i += 2
                else:
                    groups.append([i])
                    i += 1
            per = max(1, -(-70 // len(groups)))
            pst = {"clean": True}

            def pull(k):
                if nxt is None:
                    return
                for _ in range(k):
                    r = next(nxt, None)
                    if r is None:
                        pst["clean"] = True
                        return
                    pst["clean"] = r

            def make_clean():
                while not pst["clean"]:
                    pull(1)

            for gi in range(len(groups) + LA):
                if gi > 0:
                    pull(per)
                if gi < len(groups):
                    for si in groups[gi]:
                        qk(si)
                    for si in groups[gi]:
                        qk_post(si)
                if gi - LA >= 0:
                    for si in groups[gi - LA]:
                        pv(si)
                        done = si + 1
                        if done in (nd, nfa, len(steps)):
                            make_clean()
                        if done == nd:
                            epi_diff(c, p0, w, O1, bO1, L1, bL1, O2, bO2, L2, bL2)
                        elif done == nfa:
                            epi_fox_a(c, p0, w, OA, bOA)
                        elif done == len(steps):
                            epi_fox_b(c, p0, w, OB, bOB)

        def epi_diff(c, p0, w, O1, bO1, L1, bL1, O2, bO2, L2, bL2):
            QdT, bQd, QA, bQA, QB, bQB, ZdT, bZd, ZfT, bZf = qsets[c % 2]
            r1, r2, a, b, o, sq = (ep[i] for i in range(6))
            br1, br2, ba, bb, bo, bsq = (bep[i] for i in range(6))
            S.op("pe", lambda e: e.matmul(L1[:, 0:w], lhsT=ones32[:, :], rhs=acc[0][:, 0:w], start=True, stop=True),
                 reads=[bacc[0], bconst], writes=[bL1])
            S.op("pe", lambda e: e.matmul(L2[:, 0:w], lhsT=ones32[:, :], rhs=acc[1][:, 0:w], start=True, stop=True),
                 reads=[bacc[1], bconst], writes=[bL2])
            S.op("dve", lambda e: e.reciprocal(out=r1[:, 0:w], in_=L1[:, 0:w]), reads=[bL1], writes=[br1])
            S.op("dve", lambda e: e.reciprocal(out=r2[:, 0:w], in_=L2[:, 0:w]), reads=[bL2], writes=[br2])
            S.op("dve", lambda e: e.tensor_tensor(out=a[:, 0:w], in0=O1[:, 0:w], in1=r1[:, 0:w], op=ALU.mult),
                 reads=[bO1, br1], writes=[ba])
            S.op("dve", lambda e: e.tensor_tensor(out=b[:, 0:w], in0=O2[:, 0:w], in1=r2[:, 0:w], op=ALU.mult),
                 reads=[bO2, br2], writes=[bb])
            S.op("dve", lambda e: e.scalar_tensor_tensor(out=o[:, 0:w], in0=b[:, 0:w], scalar=cst[:, 3:4], in1=a[:, 0:w],
                                                         op0=ALU.mult, op1=ALU.add), reads=[ba, bb, bconst], writes=[bo])
            S.op("pool", lambda e: e.tensor_tensor(out=sq[:, 0:w], in0=o[:, 0:w], in1=o[:, 0:w], op=ALU.mult),
                 reads=[bo], writes=[bsq])
            mb, bmb = s_bank()
            S.op("pe", lambda e: e.matmul(mb[:, 0:w], lhsT=onesf[:, :], rhs=sq[:, 0:w], start=True, stop=True),
                 reads=[bsq, bconst], writes=[bmb])
            S.op("dve", lambda e: e.tensor_scalar(out=r1[:, 0:w], in0=mb[:, 0:w], scalar1=EPS, scalar2=None, op0=ALU.add),
                 reads=[bmb], writes=[br1])
            S.op("act", lambda e: e.activation(out=r1[:, 0:w], in_=r1[:, 0:w], func=AF.Ln), reads=[br1], writes=[br1])
            S.op("act", lambda e: e.activation(out=r2[:, 0:w], in_=r1[:, 0:w], func=AF.Exp, scale=-0.5),
                 reads=[br1], writes=[br2])
            S.op("dve", lambda e: e.tensor_tensor(out=a[:, 0:w], in0=o[:, 0:w], in1=r2[:, 0:w], op=ALU.mult),
                 reads=[bo, br2], writes=[ba])
            mi = state["mix"] % 2
            S.op("dve", lambda e: e.scalar_tensor_tensor(out=mixd[mi][:, 0:w], in0=a[:, 0:w], scalar=cst[:, 2:3],
                                                         in1=ZdT[:, 0:w], op0=ALU.mult, op1=ALU.mult),
                 reads=[ba, bconst, bZd, bmixd[mi]], writes=[bmixd[mi]])
            S.dma("pool", smixd[mi], cur["mixo"][0:128, p0:p0 + w], mixd[mi][:, 0:w], reads=[bmixd[mi]])

        def epi_fox_a(c, p0, w, OA, bOA):
            QdT, bQd, QA, bQA, QB, bQB, ZdT, bZd, ZfT, bZf = qsets[c % 2]
            mi = state["mix"] % 2
            rl, t = ep[0], ep[1]
            brl, bt = bep[0], bep[1]
            S.op("dve", lambda e: e.reciprocal(out=rl[0:64, 0:w], in_=OA[64:128, 0:w]), reads=[bOA], writes=[brl])
            S.op("dve", lambda e: e.tensor_tensor(out=t[0:64, 0:w], in0=OA[0:64, 0:w], in1=rl[0:64, 0:w], op=ALU.mult),
                 reads=[bOA, brl], writes=[bt])
            S.op("pool", lambda e: e.tensor_tensor(out=mixf[mi][0:64, 0:w], in0=t[0:64, 0:w], in1=ZfT[0:64, 0:w],
                                                   op=ALU.mult), reads=[bt, bZf, bmixf[mi]], writes=[bmixf[mi]])

        def epi_fox_b(c, p0, w, OB, bOB):
            QdT, bQd, QA, bQA, QB, bQB, ZdT, bZd, ZfT, bZf = qsets[c % 2]
            mi = state["mix"] % 2
            rl, t = ep[2], ep[3]
            brl, bt = bep[2], bep[3]
            S.op("dve", lambda e: e.reciprocal(out=rl[64:128, 0:w], in_=OB[0:64, 0:w]), reads=[bOB], writes=[brl])
            S.op("dve", lambda e: e.tensor_tensor(out=t[64:128, 0:w], in0=OB[64:128, 0:w], in1=rl[64:128, 0:w],
                                                  op=ALU.mult), reads=[bOB, brl], writes=[bt])
            S.op("pool", lambda e: e.tensor_tensor(out=mixf[mi][64:128, 0:w], in0=t[64:128, 0:w], in1=ZfT[64:128, 0:w],
                                                   op=ALU.mult), reads=[bt, bZf, bmixf[mi]], writes=[bmixf[mi]])
            S.dma("pool", smixf[mi], cur["mixo"][128:256, p0:p0 + w], mixf[mi][:, 0:w], reads=[bmixf[mi]])
            state["mix"] += 1

        for ci, cfg in enumerate(cfgs):
            if fused and ci in (1, 4, 5):
                S.barrier()
            cur.clear()
            cur.update(cfg)
            setup()
            state["drain"] = True
            for _ in project(0):
                pass
            state["drain"] = False
            for c in range(nchunks):
                nxt = project(c + 1) if c + 1 < nchunks else None
                serial = c < SERIAL_CUT or (c < 2 and cur.get("ut_load"))
                attention(c, None if serial else nxt)
                if nxt is not None:
                    state["drain"] = True
                    for _ in nxt:
                        pass
                    state["drain"] = False
        if fused:
            S.barrier()
            es2.close()
            emit_final(nc, es, S, bank, bbank, h1scr, [(mixall[1], woutp1)], fg, outD)
        S.finish("sp")
    return nc


def emit_final(nc, es, S, bank, bbank, hin, pairs, fg, out):
    NPR = len(pairs)
    sb = lambda name, shape, dt: _sb(nc, es, name, shape, dt)
    WO = [sb(f"fWO{i}", [128, 8, D], BF16) for i in range(NPR)]
    bWO = Buf("fWO")
    wst = [sb(f"fwst{i}", [128, D], F32) for i in range(2)]
    bwst = [Buf(f"fwst{i}") for i in range(2)]
    swst = [S.new_slot(f"fwst{i}") for i in range(2)]
    fgt = sb("fgt", [128, D], F32); bconst = Buf("fconst"); sconst = S.new_slot("fconst")
    cst = sb("fcst", [128, 2], F32)
    xt = [sb(f"fxt{i}", [128, D], F32) for i in range(2)]
    bxt = [Buf(f"fxt{i}") for i in range(2)]
    sxt = [S.new_slot(f"fxt{i}") for i in range(2)]
    mt = [sb(f"fmt{i}", [128, NPR, 8, 512], BF16) for i in range(2)]
    bmt = [Buf(f"fmt{i}") for i in range(2)]
    smt = [S.new_slot(f"fmt{i}") for i in range(2)]
    ot = [sb(f"fot{i}", [128, D], F32) for i in range(2)]
    bot = [Buf(f"fot{i}") for i in range(2)]
    sot = [S.new_slot(f"fot{i}") for i in range(2)]
    junk = sb("fjunk", [128, D], F32); bjunk = Buf("fjunk")
    stat = sb("fstat", [128, 4], F32); bstat = Buf("fstat")
    S.dma("sp", sconst, fgt[:, :], fg[0:1, :].partition_broadcast(128), writes=[bconst])
    S.op("pool", lambda e: e.memset(cst[:, 0:1], -0.5), writes=[bconst])
    wi = 0
    for li, wsrc in enumerate([p[1] for p in pairs]):
        for fc in range(8):
            s = wi % 2; wi += 1
            S.dma("sp", swst[s], wst[s][:, :], wsrc[fc * 128:(fc + 1) * 128, :], writes=[bwst[s]])
            S.op("dve", lambda e: e.tensor_copy(out=WO[li][:, fc, :], in_=wst[s][:, :]), reads=[bwst[s]], writes=[bWO])
    pbi = 0
    for t in range(SEQ // 128):
        s = t % 2
        X = xt[s]
        p0 = NMETA + 128 * t
        S.dma("sp", sxt[s], X[:, :], hin[p0:p0 + 128, :], writes=[bxt[s]])
        ms_, tj = (t // 4) % 2, t % 4
        if tj == 0:
            for li in range(NPR):
                S.dma("sp", smt[ms_], mt[ms_][:, li, :, :],
                      pairs[li][0].rearrange("(c p) t -> p c t", p=128)[:, :, p0:p0 + 512],
                      writes=[bmt[ms_]], same_batch=(li > 0))
        for half in range(2):
            pb, bpb = bank[pbi % 4], bbank[pbi % 4]
            pbi += 1
            k = 0
            for li in range(NPR):
                for fc in range(8):
                    S.op("pe", lambda e: e.matmul(pb[:, :], lhsT=mt[ms_][:, li, fc, 128 * tj:128 * tj + 128],
                                                  rhs=WO[li][:, fc, half * 512:(half + 1) * 512],
                                                  start=(k == 0), stop=(k == 8 * NPR - 1)), reads=[bmt[ms_], bWO], writes=[bpb])
                    k += 1
            S.op("dve", lambda e: e.tensor_tensor(out=X[:, half * 512:(half + 1) * 512], in0=pb[:, :],
                                                  in1=X[:, half * 512:(half + 1) * 512], op=ALU.add),
                 reads=[bpb, bxt[s]], writes=[bxt[s]])
        S.op("dve", lambda e: e.tensor_tensor(out=junk[:, :], in0=X[:, :], in1=X[:, :], op=ALU.mult), reads=[bxt[s]], writes=[bjunk])
        S.op("dve", lambda e: e.reduce_sum(out=stat[:, 0:1], in_=junk[:, :], axis=mybir.AxisListType.X), reads=[bjunk], writes=[bstat])
        S.op("dve", lambda e: e.tensor_scalar(out=stat[:, 1:2], in0=stat[:, 0:1], scalar1=1.0 / D, scalar2=EPS,
                                              op0=ALU.mult, op1=ALU.add), reads=[bstat], writes=[bstat])
        S.op("act", lambda e: e.activation(out=stat[:, 3:4], in_=stat[:, 1:2], func=AF.Ln), reads=[bstat], writes=[bstat])
        S.op("act", lambda e: e.activation(out=stat[:, 2:3], in_=stat[:, 3:4], func=AF.Exp, scale=-0.5),
             reads=[bstat], writes=[bstat])
        S.op("dve", lambda e: e.scalar_tensor_tensor(out=ot[s][:, :], in0=X[:, :], scalar=stat[:, 2:3], in1=fgt[:, :],
                                                     op0=ALU.mult, op1=ALU.mult),
             reads=[bxt[s], bstat, bconst, bot[s]], writes=[bot[s]])
        S.dma("pool", sot[s], out[t * 128:(t + 1) * 128, :], ot[s][:, :], reads=[bot[s]])


def build_final():
    NTOK = 2048
    nc = bass.Bass("TRN2", target_bir_lowering=False)
    xq = nc.dram_tensor("xq", [NTOK, D], F32, kind="ExternalInput").ap()
    m1 = nc.dram_tensor("m1", [D, NTOK], BF16, kind="ExternalInput").ap()
    m2 = nc.dram_tensor("m2", [D, NTOK], BF16, kind="ExternalInput").ap()
    wo0 = nc.dram_tensor("wo0", [D, D], F32, kind="ExternalInput").ap()
    wo1 = nc.dram_tensor("wo1", [D, D], F32, kind="ExternalInput").ap()
    fg = nc.dram_tensor("fg", [1, D], F32, kind="ExternalInput").ap()
    out = nc.dram_tensor("out", [NTOK, D], F32, kind="ExternalOutput").ap()
    with ExitStack() as es:
        S = Sched(nc, es)
        sb = lambda name, shape, dt: _sb(nc, es, name, shape, dt)
        WO = [sb(f"WO{i}", [128, 8, D], BF16) for i in range(2)]
        bWO = Buf("WO")
        wst = [sb(f"wst{i}", [128, D], F32) for i in range(2)]
        bwst = [Buf(f"wst{i}") for i in range(2)]
        swst = [S.new_slot(f"wst{i}") for i in range(2)]
        fgt = sb("fgt", [128, D], F32); bconst = Buf("const"); sconst = S.new_slot("const")
        cst = sb("cst", [128, 2], F32)
        xt = [sb(f"xt{i}", [128, D], F32) for i in range(2)]
        bxt = [Buf(f"xt{i}") for i in range(2)]
        sxt = [S.new_slot(f"xt{i}") for i in range(2)]
        mt = [sb(f"mt{i}", [128, 2, 8, 128], BF16) for i in range(2)]
        bmt = [Buf(f"mt{i}") for i in range(2)]
        smt = [S.new_slot(f"mt{i}") for i in range(2)]
        ot = [sb(f"ot{i}", [128, D], F32) for i in range(2)]
        bot = [Buf(f"ot{i}") for i in range(2)]
        sot = [S.new_slot(f"ot{i}") for i in range(2)]
        junk = sb("junk", [128, D], F32); bjunk = Buf("junk")
        stat = sb("stat", [128, 4], F32); bstat = Buf("stat")
        bank = [_ps(nc, es, f"bank{i}", [128, 512], F32) for i in range(4)]
        bbank = [Buf(f"bank{i}") for i in range(4)]

        S.dma("sp", sconst, fgt[:, :], fg[0:1, :].partition_broadcast(128), writes=[bconst])
        S.op("pool", lambda e: e.memset(cst[:, 0:1], -0.5), writes=[bconst])
        wi = 0
        for li, wsrc in enumerate((wo0, wo1)):
            for fc in range(8):
                s = wi % 2; wi += 1
                S.dma("sp", swst[s], wst[s][:, :], wsrc[fc * 128:(fc + 1) * 128, :], writes=[bwst[s]])
                S.op("dve", lambda e, s=s, fc=fc, li=li: e.tensor_copy(out=WO[li][:, fc, :], in_=wst[s][:, :]),
                     reads=[bwst[s]], writes=[bWO])
        pbi = 0
        for t in range(NTOK // 128):
            s = t % 2
            X = xt[s]
            S.dma("sp", sxt[s], X[:, :], xq[t * 128:(t + 1) * 128, :], writes=[bxt[s]])
            S.dma("sp", smt[s], mt[s][:, 0, :, :], m1.rearrange("(c p) t -> p c t", p=128)[:, :, t * 128:(t + 1) * 128],
                  writes=[bmt[s]])
            S.dma("sp", smt[s], mt[s][:, 1, :, :], m2.rearrange("(c p) t -> p c t", p=128)[:, :, t * 128:(t + 1) * 128],
                  writes=[bmt[s]], same_batch=True)
            for half in range(2):
                pb, bpb = bank[pbi % 4], bbank[pbi % 4]
                pbi += 1
                k = 0
                for li in range(2):
                    for fc in range(8):
                        S.op("pe", lambda e, li=li, fc=fc, k=k: e.matmul(pb[:, :], lhsT=mt[s][:, li, fc, :],
                                                                          rhs=WO[li][:, fc, half * 512:(half + 1) * 512],
                                                                          start=(k == 0), stop=(k == 15)),
                             reads=[bmt[s], bWO], writes=[bpb])
                        k += 1
                S.op("dve", lambda e: e.tensor_tensor(out=X[:, half * 512:(half + 1) * 512], in0=pb[:, :],
                                                      in1=X[:, half * 512:(half + 1) * 512], op=ALU.add),
                     reads=[bpb, bxt[s]], writes=[bxt[s]])
            S.op("dve", lambda e: e.tensor_tensor(out=junk[:, :], in0=X[:, :], in1=X[:, :], op=ALU.mult),
                 reads=[bxt[s]], writes=[bjunk])
            S.op("dve", lambda e: e.reduce_sum(out=stat[:, 0:1], in_=junk[:, :], axis=mybir.AxisListType.X),
                 reads=[bjunk], writes=[bstat])
            S.op("dve", lambda e: e.tensor_scalar(out=stat[:, 1:2], in0=stat[:, 0:1], scalar1=1.0 / D, scalar2=EPS,
                                                  op0=ALU.mult, op1=ALU.add), reads=[bstat], writes=[bstat])
            S.op("act", lambda e: e.activation(out=stat[:, 3:4], in_=stat[:, 1:2], func=AF.Ln), reads=[bstat], writes=[bstat])
            S.op("act", lambda e: e.activation(out=stat[:, 2:3], in_=stat[:, 3:4], func=AF.Exp, scale=-0.5),
                 reads=[bstat], writes=[bstat])
            S.op("dve", lambda e: e.scalar_tensor_tensor(out=ot[s][:, :], in0=X[:, :], scalar=stat[:, 2:3], in1=fgt[:, :],
                                                         op0=ALU.mult, op1=ALU.mult),
                 reads=[bxt[s], bstat, bconst, bot[s]], writes=[bot[s]])
            S.dma("pool", sot[s], out[t * 128:(t + 1) * 128, :], ot[s][:, :], reads=[bot[s]])
        S.finish("sp")
    return nc


def _core_cols(g):
    dq = [g * 128 + r for r in range(128)]
    dqs = [g * 128 + (r // 64) * 64 + ((r % 64) + 32) % 64 for r in range(128)]
    dk = [512 + x for x in dq]
    dks = [512 + x for x in dqs]
    dv = [1024 + g * 128 + r for r in range(128)]
    a, b = 2 * g, 2 * g + 1
    fva = [3072 + a * 64 + r for r in range(64)]
    fvb = [3072 + b * 64 + r for r in range(64)]
    dz = [1536 + g * 128 + r for r in range(128)]
    fz = [3584 + a * 64 + r for r in range(128)]
    fqa = [2048 + a * 64 + r for r in range(64)]
    fqb = [2048 + b * 64 + r for r in range(64)]
    fka = [2560 + a * 64 + r for r in range(64)]
    fkb = [2560 + b * 64 + r for r in range(64)]
    fl = [4096 + a, 4096 + b]
    cols = dq + dqs + dk + dks + dv + fva + fvb + dz + fz + fqa + fqb + fka + fkb + fl
    assert len(cols) == NCOL
    return np.array(cols)


def _wout_perm():
    rows = []
    for g in range(4):
        rows += list(range(g * 128, (g + 1) * 128))
        rows += list(range(512 + g * 128, 512 + (g + 1) * 128))
    return np.array(rows)


def _rope_tables():
    half = 32
    inv = (10000.0 ** (-np.arange(half, dtype=np.float32) / half)).astype(np.float32)
    pos = np.arange(L, dtype=np.float32)
    ang = (pos[:, None] * inv[None, :]).astype(np.float32)
    cos = np.cos(ang).astype(np.float32).T
    sin = np.sin(ang).astype(np.float32).T
    cosT = np.concatenate([cos, cos, cos, cos], axis=0)
    sinT = np.concatenate([-sin, sin, -sin, sin], axis=0)
    return np.ascontiguousarray(cosT), np.ascontiguousarray(sinT)


_PROGS = {}
SERIAL_CUT = 0
FUSED = True


def _prog(key, fn):
    if key not in _PROGS:
        _PROGS[key] = fn()
    return _PROGS[key]


def kernel(x, meta_tokens, norm_g, w_in, b_forget, lam_q1, lam_k1, lam_q2, lam_k2, subln_g, w_out, final_g):
    x = np.asarray(x, np.float32)
    f = lambda a: np.asarray(a, np.float32)
    meta_tokens, norm_g, w_in, b_forget = f(meta_tokens), f(norm_g), f(w_in), f(b_forget)
    lam_q1, lam_k1, lam_q2, lam_k2, subln_g, w_out, final_g = map(f, (lam_q1, lam_k1, lam_q2, lam_k2, subln_g, w_out, final_g))
    cores = list(range(8))
    cosT, sinT = _rope_tables()
    ident = np.eye(128, dtype=np.float32).astype(NPB)
    tri = np.triu(np.ones((128, 128), np.float32)).astype(NPB)
    perm = _wout_perm()
    hin = [np.ascontiguousarray(np.concatenate([meta_tokens, x[b]], axis=0)) for b in range(2)]

    def layer_maps(layer, mix_prev):
        maps = []
        for core in cores:
            b, g = core // 4, core % 4
            m = {
                "hin": hin[b],
                "win": np.ascontiguousarray(w_in[layer][:, _core_cols(g)]),
                "normg": np.ascontiguousarray(norm_g[layer].reshape(8, 128).T),
                "bfg": np.ascontiguousarray(b_forget[layer][2 * g:2 * g + 2].reshape(1, 2)),
                "lam4": np.concatenate([lam_q1[layer], lam_k1[layer], lam_q2[layer], lam_k2[layer]]).reshape(1, 256),
                "sublng": np.ascontiguousarray(subln_g[layer][g * 128:(g + 1) * 128].reshape(128, 1)),
                "cosT": cosT, "sinT": sinT, "ident": ident, "tri": tri,
            }
            if mix_prev is not None:
                m["mixp"] = mix_prev[b]
                m["woutp"] = np.ascontiguousarray(w_out[layer - 1][perm])
            maps.append(m)
        return maps

    if FUSED:
        nc = _prog("fused", lambda: build_layer(True, 0.0, fused=True))
        shared = {"cosT": cosT, "sinT": sinT, "ident": ident, "tri": tri,
                  "woutp0": np.ascontiguousarray(w_out[0][perm]), "woutp1": np.ascontiguousarray(w_out[1][perm]),
                  "fg": np.ascontiguousarray(final_g.reshape(1, D))}
        for layer in range(2):
            sfx = str(layer)
            shared["win" + sfx] = np.ascontiguousarray(
                np.concatenate([w_in[layer][:, _core_cols(g)] for g in range(4)], axis=0))
            shared["normg" + sfx] = np.ascontiguousarray(norm_g[layer].reshape(8, 128).T)
            shared["bfg" + sfx] = np.ascontiguousarray(b_forget[layer].reshape(4, 2))
            shared["lam4" + sfx] = np.concatenate([lam_q1[layer], lam_k1[layer], lam_q2[layer], lam_k2[layer]]).reshape(1, 256)
            shared["sublng" + sfx] = np.ascontiguousarray(subln_g[layer].reshape(512, 1))
            shared["wfl" + sfx] = np.ascontiguousarray(w_in[layer][:, 4096:4104])
            shared["bf8" + sfx] = np.ascontiguousarray(b_forget[layer].reshape(8, 1))
        maps = [dict(shared, hin=hin[core // 4]) for core in cores]
        res = run_bass_kernel_spmd(nc, maps, core_ids=cores)
        return np.stack([np.asarray(res.results[0]["out"]), np.asarray(res.results[4]["out"])], axis=0).astype(np.float32)

    mixes = []
    mix_prev = None
    for layer in range(2):
        li = 0.8 - 0.6 * math.exp(-0.3 * layer)
        nc = _prog(("layer", layer), lambda: build_layer(layer > 0, li))
        res = run_bass_kernel_spmd(nc, layer_maps(layer, mix_prev), core_ids=cores)
        outs = [np.asarray(r["mixo"]) for r in res.results]
        mix_prev = [np.ascontiguousarray(np.concatenate(outs[4 * b:4 * b + 4], axis=0)) for b in range(2)]
        mixes.append(mix_prev)

    ncf = _prog("final", build_final)
    maps = []
    for core in cores:
        b, q = core // 4, core % 4
        t0 = NMETA + 2048 * q
        maps.append({
            "xq": np.ascontiguousarray(x[b, 2048 * q:2048 * (q + 1)]),
            "m1": np.ascontiguousarray(mixes[0][b][:, t0:t0 + 2048]),
            "m2": np.ascontiguousarray(mixes[1][b][:, t0:t0 + 2048]),
            "wo0": np.ascontiguousarray(w_out[0][perm]),
            "wo1": np.ascontiguousarray(w_out[1][perm]),
            "fg": final_g.reshape(1, D),
        })
    res = run_bass_kernel_spmd(ncf, maps, core_ids=cores)
    out = np.empty((2, SEQ, D), np.float32)
    for core in cores:
        b, q = core // 4, core % 4
        out[b, 2048 * q:2048 * (q + 1)] = np.asarray(res.results[core]["out"])
    return out
```

```python
import math
from contextlib import ExitStack

import numpy as np
import ml_dtypes

import concourse.bass as bass
import concourse.mybir as mybir
from concourse.bass_utils import run_bass_kernel_spmd

F32 = mybir.dt.float32
BF16 = mybir.dt.bfloat16
AF = mybir.ActivationFunctionType
ALU = mybir.AluOpType

D = 1024
SEQ = 8192
NMETA = 16
L = SEQ + NMETA
NCH = 17
NT = 65
NCOL = 1282
EPS = 1e-6
NPB = np.dtype(ml_dtypes.bfloat16)

C_DQ, C_DQS, C_DK, C_DKS = 0, 128, 256, 384
C_V = 512
C_DZ, C_FZ = 768, 896
C_FQA, C_FQB, C_FKA, C_FKB = 1024, 1088, 1152, 1216
C_FLA, C_FLB = 1280, 1281


def tile_pos(t):
    return (0, 16) if t == 0 else (16 + 128 * (t - 1), 128)


def chunk_tiles(c):
    return [0] if c == 0 else list(range(4 * (c - 1) + 1, 4 * c + 1))


def chunk_pos(c):
    return (0, 16) if c == 0 else (16 + 512 * (c - 1), 512)


class Buf:
    def __init__(self, name):
        self.name = name
        self.w = None
        self.r = {}


class Sched:
    def __init__(self, nc, es):
        self.nc = nc
        self.es = es
        self.eng = {"pe": nc.tensor, "act": nc.scalar, "dve": nc.vector, "pool": nc.gpsimd, "sp": nc.sync}
        self.sem = {k: es.enter_context(nc.semaphore("s_" + k)) for k in ("pe", "act", "dve", "pool")}
        self.cnt = {k: 0 for k in self.sem}
        self.seen = {k: {} for k in self.eng}
        self.dsems = []
        self.nsem = 0

    def _deps(self, e, reads, writes):
        deps = {}

        def add(tok, raw=False):
            if tok is None:
                return
            sem, val, owner = tok
            if owner == e and not (raw and e != "pe"):
                return
            if deps.get(sem, 0) < val:
                deps[sem] = val

        for b in reads:
            add(b.w, raw=True)
        for b in writes:
            add(b.w)
            for t in b.r.values():
                add(t)
        for sem, val in deps.items():
            if self.seen[e].get(sem, 0) < val:
                self.eng[e].wait_ge(sem, val)
                self.seen[e][sem] = val

    def op(self, e, fn, reads=(), writes=()):
        self._deps(e, reads, writes)
        ins = fn(self.eng[e])
        self.cnt[e] += 1
        ins.then_inc(self.sem[e], 1)
        tok = (self.sem[e], self.cnt[e], e)
        for b in reads:
            b.r[e] = tok
        for b in writes:
            b.w = tok
            b.r = {}

    def new_slot(self, name):
        sem = self.es.enter_context(self.nc.semaphore("d_" + name))
        slot = {"sem": sem, "total": 0, "name": name}
        self.dsems.append(slot)
        return slot

    def dma(self, q, slot, out, in_, reads=(), writes=(), same_batch=False):
        self._deps(q, reads, writes)
        if not same_batch and slot["total"] > 0:
            if self.seen[q].get(slot["sem"], 0) < slot["total"]:
                self.eng[q].wait_ge(slot["sem"], slot["total"])
                self.seen[q][slot["sem"]] = slot["total"]
        ins = self.eng[q].dma_start(out=out, in_=in_)
        slot["total"] += 16
        ins.then_inc(slot["sem"], 16)
        tok = (slot["sem"], slot["total"], "dma:" + slot["name"])
        if not same_batch:
            slot["bw"], slot["br"] = [], []
        for b in writes:
            b.r = {}
        slot.setdefault("bw", []).extend(writes)
        slot.setdefault("br", []).extend(reads)
        for b in slot["br"]:
            b.r["dma:" + slot["name"]] = tok
        for b in slot["bw"]:
            b.w = tok

    def barrier(self):
        for e in self.eng:
            for k in self.sem:
                if k != e and self.cnt[k] > 0 and self.seen[e].get(self.sem[k], 0) < self.cnt[k]:
                    self.eng[e].wait_ge(self.sem[k], self.cnt[k])
                    self.seen[e][self.sem[k]] = self.cnt[k]
            for slot in self.dsems:
                if slot["total"] > 0 and self.seen[e].get(slot["sem"], 0) < slot["total"]:
                    self.eng[e].wait_ge(slot["sem"], slot["total"])
                    self.seen[e][slot["sem"]] = slot["total"]

    def finish(self, q="sp"):
        for slot in self.dsems:
            if slot["total"] > 0:
                self.eng[q].wait_ge(slot["sem"], slot["total"])


def _sb(nc, es, name, shape, dt):
    return es.enter_context(nc.sbuf_tensor(name, shape, dt))


def _ps(nc, es, name, shape, dt):
    return es.enter_context(nc.psum_tensor(name, shape, dt))


def build_layer(has_prev, lambda_init, nchunks=NCH, dbg=False, fused=False):
    nc = bass.Bass("TRN2", target_bir_lowering=False)
    hin = nc.dram_tensor("hin", [L, D], F32, kind="ExternalInput").ap()
    cosT = nc.dram_tensor("cosT", [128, L], F32, kind="ExternalInput").ap()
    sinT = nc.dram_tensor("sinT", [128, L], F32, kind="ExternalInput").ap()
    identD = nc.dram_tensor("ident", [128, 128], BF16, kind="ExternalInput").ap()
    triD = nc.dram_tensor("tri", [128, 128], BF16, kind="ExternalInput").ap()
    cfgs = []
    if fused:
        mixall = [nc.dram_tensor("mixall%d" % l, [D, L], BF16).ap() for l in range(2)]
        h1scr = nc.dram_tensor("h1scr", [L, D], F32).ap()
        utscr = nc.dram_tensor("utscr", [128, 8, L], BF16).ap()
        stgall = nc.dram_tensor("stgall", [8, 6, L], BF16).ap()
        woutp0 = nc.dram_tensor("woutp0", [D, D], F32, kind="ExternalInput").ap()
        woutp1 = nc.dram_tensor("woutp1", [D, D], F32, kind="ExternalInput").ap()
        fg = nc.dram_tensor("fg", [1, D], F32, kind="ExternalInput").ap()
        outD = nc.dram_tensor("out", [SEQ, D], F32, kind="ExternalOutput").ap()
        for l in range(2):
            winL = nc.dram_tensor("win%d" % l, [4 * D, NCOL], F32, kind="ExternalInput").ap()
            normgL = nc.dram_tensor("normg%d" % l, [128, 8], F32, kind="ExternalInput").ap()
            bfgL = nc.dram_tensor("bfg%d" % l, [4, 2], F32, kind="ExternalInput").ap()
            lam4L = nc.dram_tensor("lam4%d" % l, [1, 256], F32, kind="ExternalInput").ap()
            sublngL = nc.dram_tensor("sublng%d" % l, [4 * 128, 1], F32, kind="ExternalInput").ap()
            wflL = nc.dram_tensor("wfl%d" % l, [D, 8], F32, kind="ExternalInput").ap()
            bf8L = nc.dram_tensor("bf8%d" % l, [8, 1], F32, kind="ExternalInput").ap()
            for g in range(4):
                cfg = {"hin": hin, "win": winL[g * D:(g + 1) * D, :], "normg": normgL, "bfg": bfgL[g:g + 1, :],
                       "lam4": lam4L, "sublng": sublngL[g * 128:(g + 1) * 128, :],
                       "has_prev": l > 0, "li": 0.8 - 0.6 * math.exp(-0.3 * l),
                       "mixo": mixall[l][g * 256:(g + 1) * 256, :], "layer": l, "g": g}
                cfg["utscr"] = utscr
                cfg["stgall"] = stgall
                cfg["wfl"] = wflL
                cfg["bf8"] = bf8L
                cfg["fl_store"] = (g == 0)
                cfg["fl_load"] = (g > 0)
                cfg["ut_store"] = (g == 0)
                cfg["ut_load"] = (g > 0)
                if l > 0 and g == 0:
                    cfg["mixp"] = mixall[0]
                    cfg["woutp"] = woutp0
                    cfg["h1out"] = h1scr
                elif l > 0:
                    cfg["hin"] = h1scr
                    cfg["has_prev"] = False
                cfgs.append(cfg)
    else:
        cfg = {
            "hin": hin,
            "win": nc.dram_tensor("win", [D, NCOL], F32, kind="ExternalInput").ap(),
            "normg": nc.dram_tensor("normg", [128, 8], F32, kind="ExternalInput").ap(),
            "bfg": nc.dram_tensor("bfg", [1, 2], F32, kind="ExternalInput").ap(),
            "lam4": nc.dram_tensor("lam4", [1, 256], F32, kind="ExternalInput").ap(),
            "sublng": nc.dram_tensor("sublng", [128, 1], F32, kind="ExternalInput").ap(),
            "has_prev": has_prev, "li": lambda_init, "layer": 0, "g": 0,
        }
        if has_prev:
            cfg["mixp"] = nc.dram_tensor("mixp", [D, L], BF16, kind="ExternalInput").ap()
            cfg["woutp"] = nc.dram_tensor("woutp", [D, D], F32, kind="ExternalInput").ap()
        cfg["mixo"] = nc.dram_tensor("mixo", [256, L], BF16, kind="ExternalOutput").ap()
        cfgs.append(cfg)
    has_prev = has_prev or fused
    cur = {}

    with ExitStack() as es:
        S = Sched(nc, es)
        es2 = es.enter_context(ExitStack())
        sb = lambda name, shape, dt: _sb(nc, es2, name, shape, dt)

        W = sb("W", [128, 8, NCOL], BF16)
        bW = Buf("W")
        if has_prev:
            WO = sb("WO", [128, 8, D], BF16)
            bWO = Buf("WO")
        KdT = sb("KdT", [128, L], BF16)
        KA = sb("KA", [70, L], BF16)
        KB = sb("KB", [70, L], BF16)
        Vd = sb("Vd", [128, NT, 128], BF16)
        Vf = sb("Vf", [128, NT, 192], BF16)
        bKd = [Buf(f"Kd{c}") for c in range(NCH)]
        bKA = [Buf(f"KA{c}") for c in range(NCH)]
        bKB = [Buf(f"KB{c}") for c in range(NCH)]
        bVd = [Buf(f"Vd{c}") for c in range(NCH)]
        bVf = [Buf(f"Vf{c}") for c in range(NCH)]

        qsets = []
        for qi in range(2):
            qsets.append((sb(f"QdT{qi}", [128, 512], BF16), Buf(f"Qd{qi}"),
                          sb(f"QA{qi}", [70, 512], BF16), Buf(f"QA{qi}"),
                          sb(f"QB{qi}", [70, 512], BF16), Buf(f"QB{qi}"),
                          sb(f"ZdT{qi}", [128, 512], BF16), Buf(f"Zd{qi}"),
                          sb(f"ZfT{qi}", [128, 512], BF16), Buf(f"Zf{qi}")))

        xt = [sb(f"xt{i}", [128, D], F32) for i in range(2)]
        bxt = [Buf(f"xt{i}") for i in range(2)]
        sxt = [S.new_slot(f"xt{i}") for i in range(2)]
        sh1 = [S.new_slot(f"h1o{i}") for i in range(2)]
        sutl = S.new_slot("utl")
        suts = S.new_slot("uts")
        ub = [sb(f"ub{i}", [128, D], BF16) for i in range(2)]
        bub = [Buf(f"ub{i}") for i in range(2)]
        uT = sb("uT", [128, 8, 512], BF16); buT = Buf("uT")
        stat = sb("stat", [128, 8], F32); bstat = Buf("stat")
        if has_prev:
            mp = [sb(f"mp{i}", [128, 8, 512], BF16) for i in range(1)]
            bmp = [Buf(f"mp{i}") for i in range(1)]
            smp = [S.new_slot(f"mp{i}") for i in range(1)]
        rope = [sb(f"rope{i}", [128, 2, 512], F32) for i in range(1)]
        brope = [Buf(f"rope{i}") for i in range(1)]
        srope = [S.new_slot(f"rope{i}") for i in range(1)]

        ident = sb("identS", [128, 128], BF16)
        tri = sb("triS", [128, 128], BF16)
        onesb = sb("onesb", [128, 128], BF16)
        onesf = sb("onesf", [128, 128], F32)
        gT = sb("gT", [128, 8], F32)
        cst = sb("cst", [128, 16], F32)
        lamv = sb("lamv", [128, 256], F32)
        bconst = Buf("const")
        sconst = S.new_slot("const")
        bf2 = sb("bf2", [8, 1], F32)
        Wfl = sb("Wfl", [128, 8, 8], BF16); bWfl = Buf("Wfl")
        sstg = S.new_slot("stgo")

        rows = sb("rows", [8, 2, 512], F32); brows = Buf("rows")
        stg = sb("stg", [8, 6, 512], BF16); bstg = Buf("stg")
        onesrow = sb("onesrow", [8, 512], F32)
        carry = sb("carry", [8, 1], F32)
        saug = S.new_slot("aug")

        NP = 6
        Pb = [sb(f"P{i}", [128, 512], BF16) for i in range(NP)]
        bP = [Buf(f"P{i}") for i in range(NP)]
        ep = [sb(f"ep{i}", [128, 512], F32) for i in range(6)]
        bep = [Buf(f"ep{i}") for i in range(6)]
        t1, bt1, t2, bt2 = ep[4], bep[4], ep[5], bep[5]
        mixd = [sb(f"mixd{i}", [128, 512], BF16) for i in range(2)]
        bmixd = [Buf(f"mixd{i}") for i in range(2)]
        smixd = [S.new_slot(f"mixd{i}") for i in range(2)]
        mixf = [sb(f"mixf{i}", [128, 512], BF16) for i in range(2)]
        bmixf = [Buf(f"mixf{i}") for i in range(2)]
        smixf = [S.new_slot(f"mixf{i}") for i in range(2)]

        bank = [_ps(nc, es, f"bank{i}", [128, 512], F32) for i in range(8)]
        bbank = [Buf(f"bank{i}") for i in range(8)]

        acc = [sb(f"acc{i}", [128, 512], F32) for i in range(2)]
        bacc = [Buf(f"acc{i}") for i in range(2)]
        ones32 = sb("ones32", [128, 128], F32)

        def setup():
            first = not state.get("setup_done")
            state["setup_done"] = True
            S.dma("sp", sconst, gT[:, :], cur["normg"][:, :], writes=[bconst])
            if first:
                S.dma("sp", sconst, ident[:, :], identD[:, :], writes=[bconst], same_batch=True)
                S.dma("sp", sconst, tri[:, :], triD[:, :], writes=[bconst], same_batch=True)
            S.dma("sp", sconst, lamv[:, :], cur["lam4"][0:1, :].partition_broadcast(128), writes=[bconst], same_batch=True)
            S.dma("sp", sconst, cst[:, 4:5], cur["sublng"][:, :], writes=[bconst], same_batch=True)
            if cur.get("fl_store"):
                S.dma("sp", sconst, bf2[0:8, 0:1], cur["bf8"][:, :], writes=[bconst], same_batch=True)
                S.dma("sp", sconst, ep[3][:, 0:64].rearrange("p (c h) -> p c h", c=8),
                      cur["wfl"].rearrange("(c p) h -> p c h", p=128), writes=[bconst, bep[3]], same_batch=True)
            else:
                S.dma("sp", sconst, bf2[0:1, 0:1], cur["bfg"][0:1, 0:1], writes=[bconst], same_batch=True)
                S.dma("sp", sconst, bf2[1:2, 0:1], cur["bfg"][0:1, 1:2], writes=[bconst], same_batch=True)

            S.op("pool", lambda e: e.memset(carry[:, :], 0.0), writes=[brows])
            if first:
                S.op("pool", lambda e: e.memset(onesb[:, :], 1.0), writes=[bconst])
                S.op("pool", lambda e: e.memset(onesf[:, :], 1.0 / 128.0), writes=[bconst])
                S.op("pool", lambda e: e.memset(ones32[:, :], 1.0), writes=[bconst])
                S.op("pool", lambda e: e.memset(cst[:, 0:1], -0.5), writes=[bconst])
                S.op("pool", lambda e: e.memset(cst[:, 1:2], EPS), writes=[bconst])
                S.op("pool", lambda e: e.memset(onesrow[:, :], 1.0), writes=[bconst])
                S.op("pool", lambda e: e.memset(Vf[:, :, :], 1.0), writes=bVf)
                S.op("pool", lambda e: e.memset(KA[64:70, :], 1.0), writes=bKA)
                S.op("pool", lambda e: e.memset(KB[64:70, :], 1.0), writes=bKB)
                for qs_ in qsets:
                    S.op("pool", lambda e: e.memset(qs_[2][64:70, :], 1.0), writes=[qs_[3]])
                    S.op("pool", lambda e: e.memset(qs_[4][64:70, :], 1.0), writes=[qs_[5]])

            S.op("dve", lambda e: e.tensor_tensor(out=ep[2][:, 0:64], in0=lamv[:, 0:64], in1=lamv[:, 64:128], op=ALU.mult),
                 reads=[bconst], writes=[bep[2]])
            S.op("dve", lambda e: e.tensor_tensor(out=ep[2][:, 64:128], in0=lamv[:, 128:192], in1=lamv[:, 192:256],
                                                  op=ALU.mult), reads=[bconst], writes=[bep[2]])
            S.op("dve", lambda e: e.reduce_sum(out=cst[:, 5:6], in_=ep[2][:, 0:64], axis=mybir.AxisListType.X),
                 reads=[bep[2]], writes=[bconst])
            S.op("dve", lambda e: e.reduce_sum(out=cst[:, 6:7], in_=ep[2][:, 64:128], axis=mybir.AxisListType.X),
                 reads=[bep[2]], writes=[bconst])
            S.op("act", lambda e: e.activation(out=cst[:, 7:9], in_=cst[:, 5:7], func=AF.Exp), reads=[bconst], writes=[bconst])
            S.op("dve", lambda e: e.tensor_tensor(out=cst[:, 9:10], in0=cst[:, 8:9], in1=cst[:, 7:8], op=ALU.subtract),
                 reads=[bconst], writes=[bconst])
            S.op("dve", lambda e: e.tensor_scalar(out=cst[:, 3:4], in0=cst[:, 9:10], scalar1=-float(cur["li"]),
                                                  scalar2=None, op0=ALU.add), reads=[bconst], writes=[bconst])
            S.op("dve", lambda e: e.tensor_scalar(out=cst[:, 2:3], in0=cst[:, 4:5], scalar1=float(1.0 - cur["li"]),
                                                  scalar2=None, op0=ALU.mult), reads=[bconst], writes=[bconst])
            if cur.get("fl_store"):
                S.op("dve", lambda e: e.tensor_copy(out=Wfl[:, :, :], in_=ep[3][:, 0:64].rearrange("p (c h) -> p c h", c=8)),
                     reads=[bconst, bep[3]], writes=[bWfl])
            S.op("dve", lambda e: e.tensor_scalar(out=bf2[:, :], in0=bf2[:, :], scalar1=-1.0, scalar2=None, op0=ALU.mult),
                 reads=[bconst], writes=[bconst])

            for fc in range(8):
                S.dma("sp", sxt[0], xt[0][:, :], cur["win"][fc * 128:(fc + 1) * 128, 0:D], writes=[bxt[0]])
                S.dma("sp", sxt[1], xt[1][:, 0:NCOL - D], cur["win"][fc * 128:(fc + 1) * 128, D:NCOL], writes=[bxt[1]])
                S.op("dve", lambda e: e.tensor_copy(out=W[:, fc, 0:D], in_=xt[0][:, :]), reads=[bxt[0]], writes=[bW])
                S.op("dve", lambda e: e.tensor_scalar(out=W[:, fc, C_DZ:C_DZ + 256], in0=xt[0][:, C_DZ:C_DZ + 256],
                                                      scalar1=0.5, scalar2=None, op0=ALU.mult), reads=[bxt[0]], writes=[bW])
                S.op("pool", lambda e: e.tensor_copy(out=W[:, fc, D:NCOL], in_=xt[1][:, 0:NCOL - D]), reads=[bxt[1]],
                     writes=[bW])
            if cur["has_prev"]:
                for fc in range(8):
                    s = fc % 2
                    S.dma("sp", sxt[s], xt[s][:, :], cur["woutp"][fc * 128:(fc + 1) * 128, :], writes=[bxt[s]])
                    S.op("dve", lambda e: e.tensor_copy(out=WO[:, fc, :], in_=xt[s][:, :]), reads=[bxt[s]], writes=[bWO])

        state = {"xt": 0, "pb": 0, "sbank": 0, "p": 0, "mix": 0}

        def proj_bank():
            pool = (3, 5, 0, 1, 6, 7, 2, 4) if state.get("drain") else (3, 5)
            i = pool[state["pb"] % len(pool)]
            state["pb"] += 1
            return bank[i], bbank[i]

        def s_bank(pool=(0, 1)):
            i = pool[state["sbank"] % len(pool)]
            state["sbank"] += 1
            return bank[i], bbank[i]

        def project(c):
            QdT, bQd, QA, bQA, QB, bQB, ZdT, bZd, ZfT, bZf = qsets[c % 2]
            p0, w = chunk_pos(c)
            tiles = chunk_tiles(c)
            rs = 0
            S.dma("sp", srope[rs], rope[rs][:, 0, 0:w], cosT[:, p0:p0 + w], writes=[brope[rs]])
            S.dma("sp", srope[rs], rope[rs][:, 1, 0:w], sinT[:, p0:p0 + w], writes=[brope[rs]], same_batch=True)
            if cur["has_prev"]:
                ms = 0
                S.dma("sp", smp[ms], mp[ms][:, :, 0:w],
                      cur["mixp"].rearrange("(c p) t -> p c t", p=128)[:, :, p0:p0 + w], writes=[bmp[ms]])
            tinfo = []
            if cur.get("ut_load"):
                S.dma("sp", sutl, uT[:, :, 0:w], cur["utscr"][:, :, p0:p0 + w], writes=[buT])
                yield True
            else:
                for j, t in enumerate(tiles):
                    tp, n = tile_pos(t)
                    xs = state["xt"] % 2
                    state["xt"] += 1
                    tinfo.append((j, t, tp, n, xs))
            def load(k):
                j, t, tp, n, xs = tinfo[k]
                S.dma("sp", sxt[xs], xt[xs][0:n, :], cur["hin"][tp:tp + n, :], writes=[bxt[xs]])
            if tinfo:
                load(0)
                yield True
            for k, (j, t, tp, n, xs) in enumerate(tinfo):
                X, bX = xt[xs], bxt[xs]
                U, bU = ub[xs], bub[xs]
                if k + 1 < len(tinfo):
                    load(k + 1)
                if cur["has_prev"]:
                    for half in range(2):
                        pb, bpb = proj_bank()
                        for fc in range(8):
                            S.op("pe", lambda e: e.matmul(
                                pb[0:n, :], lhsT=mp[ms][:, fc, 128 * j:128 * j + n],
                                rhs=WO[:, fc, half * 512:(half + 1) * 512], start=(fc == 0), stop=(fc == 7)),
                                reads=[bmp[ms], bWO], writes=[bpb])
                        yield False
                        S.op("dve", lambda e: e.tensor_tensor(
                            out=X[0:n, half * 512:(half + 1) * 512], in0=pb[0:n, :],
                            in1=X[0:n, half * 512:(half + 1) * 512], op=ALU.add), reads=[bpb, bX], writes=[bX])
                        yield True
                if cur.get("h1out") is not None:
                    S.dma("pool", sh1[xs], cur["h1out"][tp:tp + n, :], X[0:n, :], reads=[bX])
                for hf in range(2):
                    S.op("dve", lambda e: e.tensor_tensor(out=ep[2 + hf][0:n, :], in0=X[0:n, 512 * hf:512 * hf + 512],
                                                          in1=X[0:n, 512 * hf:512 * hf + 512], op=ALU.mult),
                         reads=[bX], writes=[bep[2 + hf]])
                    S.op("dve", lambda e: e.reduce_sum(out=stat[0:n, 4 + hf:5 + hf], in_=ep[2 + hf][0:n, :],
                                                       axis=mybir.AxisListType.X), reads=[bep[2 + hf]], writes=[bstat])
                S.op("dve", lambda e: e.tensor_tensor(out=stat[0:n, 0:1], in0=stat[0:n, 4:5], in1=stat[0:n, 5:6], op=ALU.add),
                     reads=[bstat], writes=[bstat])
                S.op("dve", lambda e: e.tensor_scalar(out=stat[0:n, 1:2], in0=stat[0:n, 0:1], scalar1=1.0 / D,
                                                      scalar2=EPS, op0=ALU.mult, op1=ALU.add),
                     reads=[bstat], writes=[bstat])
                yield True
                S.op("act", lambda e: e.activation(out=stat[0:n, 3:4], in_=stat[0:n, 1:2], func=AF.Ln),
                     reads=[bstat], writes=[bstat])
                S.op("act", lambda e: e.activation(out=stat[0:n, 2:3], in_=stat[0:n, 3:4], func=AF.Exp, scale=-0.5),
                     reads=[bstat], writes=[bstat])
                S.op("act", lambda e: e.activation(out=U[0:n, :], in_=X[0:n, :], func=AF.Copy, scale=stat[0:n, 2:3]),
                     reads=[bX, bstat], writes=[bU])
                yield True
                tb, btb = proj_bank()
                tbv = tb[:, :].bitcast(BF16)
                for fc in range(8):
                    S.op("pe", lambda e: e.transpose(out=tbv[:, fc * 128:fc * 128 + n],
                                                     in_=U[0:n, fc * 128:(fc + 1) * 128], identity=ident[0:n, 0:n]),
                         reads=[bU, bconst], writes=[btb])
                yield False
                for fc in range(8):
                    S.op("dve", lambda e: e.tensor_scalar(out=uT[:, fc, 128 * j:128 * j + n],
                                                          in0=tbv[:, fc * 128:fc * 128 + n],
                                                          scalar1=gT[:, fc:fc + 1], scalar2=None, op0=ALU.mult),
                         reads=[btb, bconst], writes=[buT])
                yield True
            if cur.get("ut_store"):
                S.dma("pool", suts, cur["utscr"][:, :, p0:p0 + w], uT[:, :, 0:w], reads=[buT])

            def fm(col, M):
                pb, bpb = proj_bank()
                for fc in range(8):
                    S.op("pe", lambda e: e.matmul(pb[0:M, 0:w], lhsT=W[:, fc, col:col + M], rhs=uT[:, fc, 0:w],
                                                  start=(fc == 0), stop=(fc == 7)),
                         reads=[bW, buT], writes=[bpb])
                    if fc == 3 and w == 512:
                        yield False
                return pb, bpb

            for (cn, cs, dst, bdst, dcol) in ((C_DQ, C_DQS, QdT, bQd, 0), (C_DK, C_DKS, KdT, bKd[c], p0)):
                pa, bpa = yield from fm(cn, 128)
                yield False
                pbk, bpbk = yield from fm(cs, 128)
                yield False
                S.op("dve", lambda e: e.tensor_tensor(out=t1[:, 0:w], in0=pa[:, 0:w], in1=rope[rs][:, 0, 0:w],
                                                      op=ALU.mult), reads=[bpa, brope[rs]], writes=[bt1])
                S.op("dve", lambda e: e.tensor_tensor(out=t2[:, 0:w], in0=pbk[:, 0:w], in1=rope[rs][:, 1, 0:w],
                                                      op=ALU.mult), reads=[bpbk, brope[rs]], writes=[bt2])
                S.op("dve", lambda e: e.tensor_tensor(out=dst[:, dcol:dcol + w], in0=t1[:, 0:w], in1=t2[:, 0:w],
                                                      op=ALU.add), reads=[bt1, bt2], writes=[bdst])
                yield True
            for (col, dA, bdA, dB, bdB, dcol) in ((C_FQA, QA, bQA, QB, bQB, 0), (C_FKA, KA, bKA[c], KB, bKB[c], p0)):
                pa, bpa = yield from fm(col, 128)
                yield False
                S.op("dve", lambda e: e.tensor_copy(out=dA[0:64, dcol:dcol + w], in_=pa[0:64, 0:w]),
                     reads=[bpa], writes=[bdA])
                S.op("dve", lambda e: e.tensor_copy(out=dB[0:64, dcol:dcol + w], in_=pa[64:128, 0:w]),
                     reads=[bpa], writes=[bdB])
                yield True
            for (col, dst, bdst) in ((C_DZ, ZdT, bZd), (C_FZ, ZfT, bZf)):
                pa, bpa = yield from fm(col, 128)
                yield False
                S.op("act", lambda e: e.activation(out=t1[:, 0:w], in_=pa[:, 0:w], func=AF.Tanh),
                     reads=[bpa], writes=[bt1])
                yield False
                S.op("dve", lambda e: e.scalar_tensor_tensor(out=dst[:, 0:w], in0=t1[:, 0:w], scalar=1.0, in1=pa[:, 0:w],
                                                             op0=ALU.add, op1=ALU.mult),
                     reads=[bt1, bpa], writes=[bdst])
                yield True
            gofs = 2 * cur.get("g", 0)
            if cur.get("fl_load"):
                first = True
                for hh in range(2):
                    Qt, bQt = (QA, bQA) if hh == 0 else (QB, bQB)
                    Kt, bKt = (KA, bKA[c]) if hh == 0 else (KB, bKB[c])
                    for i in range(3):
                        S.dma("sp", saug, Qt[64 + i:65 + i, 0:w], cur["stgall"][gofs + hh:gofs + hh + 1, i, p0:p0 + w],
                              writes=[bQt], same_batch=not first)
                        first = False
                        S.dma("sp", saug, Kt[67 + i:68 + i, p0:p0 + w],
                              cur["stgall"][gofs + hh:gofs + hh + 1, 3 + i, p0:p0 + w], writes=[bKt], same_batch=True)
                yield True
            else:
                NHh = 8 if cur.get("fl_store") else 2
                if cur.get("fl_store"):
                    pa, bpa = proj_bank()
                    for fc in range(8):
                        S.op("pe", lambda e: e.matmul(pa[0:8, 0:w], lhsT=Wfl[:, fc, 0:8], rhs=uT[:, fc, 0:w],
                                                      start=(fc == 0), stop=(fc == 7)), reads=[bWfl, buT], writes=[bpa])
                else:
                    pa, bpa = yield from fm(C_FLA, 2)
                yield False
                ra, rb = rows[0:NHh, 0, 0:w], rows[0:NHh, 1, 0:w]
                S.op("act", lambda e: e.activation(out=ra, in_=pa[0:NHh, 0:w], func=AF.Exp, scale=-1.0,
                                                   bias=bf2[0:NHh, 0:1]), reads=[bpa, bconst], writes=[brows])
                S.op("act", lambda e: e.activation(out=ra, in_=ra, func=AF.Ln, bias=1.0, scale=1.0),
                     reads=[brows], writes=[brows])
                yield True
                S.op("dve", lambda e: e.tensor_tensor_scan(out=rb, data0=onesrow[0:NHh, 0:w], data1=ra,
                                                           initial=carry[0:NHh, 0:1], op0=ALU.mult, op1=ALU.subtract),
                     reads=[brows, bconst], writes=[brows])
                S.op("dve", lambda e: e.tensor_copy(out=carry[0:NHh, 0:1], in_=rows[0:NHh, 1, w - 1:w]),
                     reads=[brows], writes=[brows])
                S.op("dve", lambda e: e.tensor_scalar(out=ra, in0=rb, scalar1=8.0, scalar2=None, op0=ALU.mult),
                     reads=[brows], writes=[brows])
                S.op("dve", lambda e: e.tensor_copy(out=stg[0:NHh, 0, 0:w], in_=ra), reads=[brows], writes=[bstg])
                S.op("dve", lambda e: e.tensor_tensor(out=rb, in0=ra, in1=stg[0:NHh, 0, 0:w], op=ALU.subtract),
                     reads=[brows, bstg], writes=[brows])
                S.op("dve", lambda e: e.tensor_copy(out=stg[0:NHh, 1, 0:w], in_=rb), reads=[brows], writes=[bstg])
                S.op("dve", lambda e: e.tensor_tensor(out=ra, in0=rb, in1=stg[0:NHh, 1, 0:w], op=ALU.subtract),
                     reads=[brows, bstg], writes=[brows])
                S.op("dve", lambda e: e.tensor_copy(out=stg[0:NHh, 2, 0:w], in_=ra), reads=[brows], writes=[bstg])
                S.op("dve", lambda e: e.tensor_scalar(out=stg[0:NHh, 3:6, 0:w], in0=stg[0:NHh, 0:3, 0:w], scalar1=-1.0,
                                                      scalar2=None, op0=ALU.mult), reads=[bstg], writes=[bstg])
                yield True
                if cur.get("fl_store"):
                    S.dma("pool", sstg, cur["stgall"][:, :, p0:p0 + w], stg[0:8, :, 0:w], reads=[bstg])
                first = True
                for hh in range(2):
                    Qt, bQt = (QA, bQA) if hh == 0 else (QB, bQB)
                    Kt, bKt = (KA, bKA[c]) if hh == 0 else (KB, bKB[c])
                    for i in range(3):
                        S.dma("sp", saug, Qt[64 + i:65 + i, 0:w], stg[gofs + hh:gofs + hh + 1, i, 0:w], reads=[bstg],
                              writes=[bQt], same_batch=not first)
                        first = False
                        S.dma("sp", saug, Kt[67 + i:68 + i, p0:p0 + w], stg[gofs + hh:gofs + hh + 1, 3 + i, 0:w],
                              reads=[bstg], writes=[bKt], same_batch=True)
                yield True
            for j, t in enumerate(tiles):
                tp, n = tile_pos(t)
                pb, bpb = proj_bank()
                for fc in range(8):
                    S.op("pe", lambda e: e.matmul(pb[0:n, 0:256], lhsT=uT[:, fc, 128 * j:128 * j + n],
                                                  rhs=W[:, fc, C_V:C_V + 256], start=(fc == 0), stop=(fc == 7)),
                         reads=[bW, buT], writes=[bpb])
                yield False
                S.op("dve", lambda e: e.tensor_copy(out=Vd[0:n, t, :], in_=pb[0:n, 0:128]), reads=[bpb], writes=[bVd[c]])
                S.op("dve", lambda e: e.tensor_copy(out=Vf[0:n, t, 0:64], in_=pb[0:n, 128:192]), reads=[bpb],
                     writes=[bVf[c]])
                S.op("dve", lambda e: e.tensor_copy(out=Vf[0:n, t, 128:192], in_=pb[0:n, 192:256]), reads=[bpb],
                     writes=[bVf[c]])
                yield True

        def attention(c, nxt=None):
            QdT, bQd, QA, bQA, QB, bQB, ZdT, bZd, ZfT, bZf = qsets[c % 2]
            p0, w = chunk_pos(c)
            last_t = chunk_tiles(c)[-1]
            kts = list(range(0, last_t + 1))
            first_diag = chunk_tiles(c)[0]

            def kinfo(kt):
                kp, kn = tile_pos(kt)
                if kt >= first_diag:
                    i = kt - first_diag
                    return kp, kn, 128 * i, True
                return kp, kn, 0, False

            def kchunk(kt):
                return 0 if kt == 0 else (kt - 1) // 4 + 1

            O1, bO1 = bank[2], bbank[2]
            L1, bL1 = bank[3], bbank[3]
            O2, bO2 = bank[4], bbank[4]
            L2, bL2 = bank[5], bbank[5]
            OA, bOA = bank[6], bbank[6]
            OB, bOB = bank[7], bbank[7]

            steps = []
            for kt in kts:
                steps.append(("d1", kt))
                steps.append(("d2", kt))
            for kt in kts:
                steps.append(("fA", kt))
            for kt in kts:
                steps.append(("fB", kt))
            pend = {}

            def qk(si):
                u, kt = steps[si]
                kp, kn, qlo, diag = kinfo(kt)
                kc = kchunk(kt)
                if c == 0:
                    spool = (0, 1)
                elif u in ("d1", "d2"):
                    spool = (0, 1, 6, 7)
                elif u == "fA":
                    spool = (0, 1, 7)
                else:
                    spool = (0, 1, 2, 4)
                sbk, bsbk = s_bank(spool)
                pi = state["p"] % NP
                state["p"] += 1
                P, bPi = Pb[pi], bP[pi]
                if u == "d1":
                    lhsT, rhs, rd = KdT[0:64, kp:kp + kn], QdT[0:64, qlo:w], [bKd[kc], bQd]
                elif u == "d2":
                    lhsT, rhs, rd = KdT[64:128, kp:kp + kn], QdT[64:128, qlo:w], [bKd[kc], bQd]
                elif u == "fA":
                    lhsT, rhs, rd = KA[0:70, kp:kp + kn], QA[0:70, qlo:w], [bKA[kc], bQA]
                else:
                    lhsT, rhs, rd = KB[0:70, kp:kp + kn], QB[0:70, qlo:w], [bKB[kc], bQB]
                S.op("pe", lambda e: e.matmul(sbk[0:kn, qlo:w], lhsT=lhsT, rhs=rhs, start=True, stop=True),
                     reads=rd, writes=[bsbk])
                pend[si] = (P, bPi, sbk, bsbk)

            def qk_post(si):
                u, kt = steps[si]
                kp, kn, qlo, diag = kinfo(kt)
                P, bPi, sbk, bsbk = pend[si]
                S.op("act", lambda e: e.activation(out=P[0:kn, qlo:w], in_=sbk[0:kn, qlo:w], func=AF.Exp, scale=0.125),
                     reads=[bsbk], writes=[bPi])
                if diag:
                    dw = min(128, w - qlo)
                    S.op("pool", lambda e: e.tensor_tensor(out=P[0:kn, qlo:qlo + dw], in0=P[0:kn, qlo:qlo + dw],
                                                           in1=tri[0:kn, 0:dw], op=ALU.mult),
                         reads=[bPi, bconst], writes=[bPi])

            def pv(si):
                u, kt = steps[si]
                kp, kn, qlo, diag = kinfo(kt)
                kc = kchunk(kt)
                P, bPi = pend.pop(si)[0:2]
                st, sp_ = (kt == kts[0]), (kt == kts[-1])
                rhs = P[0:kn, qlo:w]
                if u in ("d1", "d2"):
                    O, bO, Lb, bLb = (O1, bO1, L1, bL1) if u == "d1" else (O2, bO2, L2, bL2)
                    S.op("pe", lambda e: e.matmul(O[:, qlo:w], lhsT=Vd[0:kn, kt, :], rhs=rhs, start=st, stop=sp_),
                         reads=[bVd[kc], bPi], writes=[bO])
                    ai = 0 if u == "d1" else 1
                    S.op("dve", lambda e: e.tensor_tensor(out=acc[ai][0:kn, qlo:w], in0=acc[ai][0:kn, qlo:w],
                                                          in1=P[0:kn, qlo:w], op=ALU.add),
                         reads=[bPi, bacc[ai]], writes=[bacc[ai]])
                elif u == "fA":
                    S.op("pe", lambda e: e.matmul(OA[:, qlo:w], lhsT=Vf[0:kn, kt, 0:128], rhs=rhs, start=st, stop=sp_),
                         reads=[bVf[kc], bPi], writes=[bOA])
                else:
                    S.op("pe", lambda e: e.matmul(OB[:, qlo:w], lhsT=Vf[0:kn, kt, 64:192], rhs=rhs, start=st, stop=sp_),
                         reads=[bVf[kc], bPi], writes=[bOB])

            for ai in range(2):
                S.op("pool", lambda e: e.memset(acc[ai][:, 0:w], 0.0), writes=[bacc[ai]])
            LA = 2 if c >= 1 else 1
            nd = 2 * len(kts)
            nfa = nd + len(kts)
            groups = []
            i = 0
            while i < len(steps):
                if steps[i][0] == "d1":
                    groups.append([i, i + 1])
                    i += 2
                else:
                    groups.append([i])
                    i += 1
            per = max(1, -(-70 // len(groups)))
            pst = {"clean": True}

            def pull(k):
                if nxt is None:
                    return
                for _ in range(k):
                    r = next(nxt, None)
                    if r is None:
                        pst["clean"] = True
                        return
                    pst["clean"] = r

            def make_clean():
                while not pst["clean"]:
                    pull(1)

            for gi in range(len(groups) + LA):
                if gi > 0:
                    pull(per)
                if gi < len(groups):
                    for si in groups[gi]:
                        qk(si)
                    for si in groups[gi]:
                        qk_post(si)
                if gi - LA >= 0:
                    for si in groups[gi - LA]:
                        pv(si)
                        done = si + 1
                        if done in (nd, nfa, len(steps)):
                            make_clean()
                        if done == nd:
                            epi_diff(c, p0, w, O1, bO1, L1, bL1, O2, bO2, L2, bL2)
                        elif done == nfa:
                            epi_fox_a(c, p0, w, OA, bOA)
                        elif done == len(steps):
                            epi_fox_b(c, p0, w, OB, bOB)

        def epi_diff(c, p0, w, O1, bO1, L1, bL1, O2, bO2, L2, bL2):
            QdT, bQd, QA, bQA, QB, bQB, ZdT, bZd, ZfT, bZf = qsets[c % 2]
            r1, r2, a, b, o, sq = (ep[i] for i in range(6))
            br1, br2, ba, bb, bo, bsq = (bep[i] for i in range(6))
            S.op("pe", lambda e: e.matmul(L1[:, 0:w], lhsT=ones32[:, :], rhs=acc[0][:, 0:w], start=True, stop=True),
                 reads=[bacc[0], bconst], writes=[bL1])
            S.op("pe", lambda e: e.matmul(L2[:, 0:w], lhsT=ones32[:, :], rhs=acc[1][:, 0:w], start=True, stop=True),
                 reads=[bacc[1], bconst], writes=[bL2])
            S.op("dve", lambda e: e.reciprocal(out=r1[:, 0:w], in_=L1[:, 0:w]), reads=[bL1], writes=[br1])
            S.op("dve", lambda e: e.reciprocal(out=r2[:, 0:w], in_=L2[:, 0:w]), reads=[bL2], writes=[br2])
            S.op("dve", lambda e: e.tensor_tensor(out=a[:, 0:w], in0=O1[:, 0:w], in1=r1[:, 0:w], op=ALU.mult),
                 reads=[bO1, br1], writes=[ba])
            S.op("dve", lambda e: e.tensor_tensor(out=b[:, 0:w], in0=O2[:, 0:w], in1=r2[:, 0:w], op=ALU.mult),
                 reads=[bO2, br2], writes=[bb])
            S.op("dve", lambda e: e.scalar_tensor_tensor(out=o[:, 0:w], in0=b[:, 0:w], scalar=cst[:, 3:4], in1=a[:, 0:w],
                                                         op0=ALU.mult, op1=ALU.add), reads=[ba, bb, bconst], writes=[bo])
            S.op("pool", lambda e: e.tensor_tensor(out=sq[:, 0:w], in0=o[:, 0:w], in1=o[:, 0:w], op=ALU.mult),
                 reads=[bo], writes=[bsq])
            mb, bmb = s_bank()
            S.op("pe", lambda e: e.matmul(mb[:, 0:w], lhsT=onesf[:, :], rhs=sq[:, 0:w], start=True, stop=True),
                 reads=[bsq, bconst], writes=[bmb])
            S.op("dve", lambda e: e.tensor_scalar(out=r1[:, 0:w], in0=mb[:, 0:w], scalar1=EPS, scalar2=None, op0=ALU.add),
                 reads=[bmb], writes=[br1])
            S.op("act", lambda e: e.activation(out=r1[:, 0:w], in_=r1[:, 0:w], func=AF.Ln), reads=[br1], writes=[br1])
            S.op("act", lambda e: e.activation(out=r2[:, 0:w], in_=r1[:, 0:w], func=AF.Exp, scale=-0.5),
                 reads=[br1], writes=[br2])
            S.op("dve", lambda e: e.tensor_tensor(out=a[:, 0:w], in0=o[:, 0:w], in1=r2[:, 0:w], op=ALU.mult),
                 reads=[bo, br2], writes=[ba])
            mi = state["mix"] % 2
            S.op("dve", lambda e: e.scalar_tensor_tensor(out=mixd[mi][:, 0:w], in0=a[:, 0:w], scalar=cst[:, 2:3],
                                                         in1=ZdT[:, 0:w], op0=ALU.mult, op1=ALU.mult),
                 reads=[ba, bconst, bZd, bmixd[mi]], writes=[bmixd[mi]])
            S.dma("pool", smixd[mi], cur["mixo"][0:128, p0:p0 + w], mixd[mi][:, 0:w], reads=[bmixd[mi]])

        def epi_fox_a(c, p0, w, OA, bOA):
            QdT, bQd, QA, bQA, QB, bQB, ZdT, bZd, ZfT, bZf = qsets[c % 2]
            mi = state["mix"] % 2
            rl, t = ep[0], ep[1]
            brl, bt = bep[0], bep[1]
            S.op("dve", lambda e: e.reciprocal(out=rl[0:64, 0:w], in_=OA[64:128, 0:w]), reads=[bOA], writes=[brl])
            S.op("dve", lambda e: e.tensor_tensor(out=t[0:64, 0:w], in0=OA[0:64, 0:w], in1=rl[0:64, 0:w], op=ALU.mult),
                 reads=[bOA, brl], writes=[bt])
            S.op("pool", lambda e: e.tensor_tensor(out=mixf[mi][0:64, 0:w], in0=t[0:64, 0:w], in1=ZfT[0:64, 0:w],
                                                   op=ALU.mult), reads=[bt, bZf, bmixf[mi]], writes=[bmixf[mi]])

        def epi_fox_b(c, p0, w, OB, bOB):
            QdT, bQd, QA, bQA, QB, bQB, ZdT, bZd, ZfT, bZf = qsets[c % 2]
            mi = state["mix"] % 2
            rl, t = ep[2], ep[3]
            brl, bt = bep[2], bep[3]
            S.op("dve", lambda e: e.reciprocal(out=rl[64:128, 0:w], in_=OB[0:64, 0:w]), reads=[bOB], writes=[brl])
            S.op("dve", lambda e: e.tensor_tensor(out=t[64:128, 0:w], in0=OB[64:128, 0:w], in1=rl[64:128, 0:w],
                                                  op=ALU.mult), reads=[bOB, brl], writes=[bt])
            S.op("pool", lambda e: e.tensor_tensor(out=mixf[mi][64:128, 0:w], in0=t[64:128, 0:w], in1=ZfT[64:128, 0:w],
                                                   op=ALU.mult), reads=[bt, bZf, bmixf[mi]], writes=[bmixf[mi]])
            S.dma("pool", smixf[mi], cur["mixo"][128:256, p0:p0 + w], mixf[mi][:, 0:w], reads=[bmixf[mi]])
            state["mix"] += 1

        for ci, cfg in enumerate(cfgs):
            if fused and ci in (1, 4, 5):
                S.barrier()
            cur.clear()
            cur.update(cfg)
            setup()
            state["drain"] = True
            for _ in project(0):
                pass
            state["drain"] = False
            for c in range(nchunks):
                nxt = project(c + 1) if c + 1 < nchunks else None
                serial = c < SERIAL_CUT or (c < 2 and cur.get("ut_load"))
                attention(c, None if serial else nxt)
                if nxt is not None:
                    state["drain"] = True
                    for _ in nxt:
                        pass
                    state["drain"] = False
        if fused:
            S.barrier()
            es2.close()
            emit_final(nc, es, S, bank, bbank, h1scr, [(mixall[1], woutp1)], fg, outD)
        S.finish("sp")
    return nc


def emit_final(nc, es, S, bank, bbank, hin, pairs, fg, out):
    NPR = len(pairs)
    sb = lambda name, shape, dt: _sb(nc, es, name, shape, dt)
    WO = [sb(f"fWO{i}", [128, 8, D], BF16) for i in range(NPR)]
    bWO = Buf("fWO")
    wst = [sb(f"fwst{i}", [128, D], F32) for i in range(2)]
    bwst = [Buf(f"fwst{i}") for i in range(2)]
    swst = [S.new_slot(f"fwst{i}") for i in range(2)]
    fgt = sb("fgt", [128, D], F32); bconst = Buf("fconst"); sconst = S.new_slot("fconst")
    cst = sb("fcst", [128, 2], F32)
    xt = [sb(f"fxt{i}", [128, D], F32) for i in range(2)]
    bxt = [Buf(f"fxt{i}") for i in range(2)]
    sxt = [S.new_slot(f"fxt{i}") for i in range(2)]
    mt = [sb(f"fmt{i}", [128, NPR, 8, 512], BF16) for i in range(2)]
    bmt = [Buf(f"fmt{i}") for i in range(2)]
    smt = [S.new_slot(f"fmt{i}") for i in range(2)]
    ot = [sb(f"fot{i}", [128, D], F32) for i in range(2)]
    bot = [Buf(f"fot{i}") for i in range(2)]
    sot = [S.new_slot(f"fot{i}") for i in range(2)]
    junk = sb("fjunk", [128, D], F32); bjunk = Buf("fjunk")
    stat = sb("fstat", [128, 4], F32); bstat = Buf("fstat")
    S.dma("sp", sconst, fgt[:, :], fg[0:1, :].partition_broadcast(128), writes=[bconst])
    S.op("pool", lambda e: e.memset(cst[:, 0:1], -0.5), writes=[bconst])
    wi = 0
    for li, wsrc in enumerate([p[1] for p in pairs]):
        for fc in range(8):
            s = wi % 2; wi += 1
            S.dma("sp", swst[s], wst[s][:, :], wsrc[fc * 128:(fc + 1) * 128, :], writes=[bwst[s]])
            S.op("dve", lambda e: e.tensor_copy(out=WO[li][:, fc, :], in_=wst[s][:, :]), reads=[bwst[s]], writes=[bWO])
    pbi = 0
    for t in range(SEQ // 128):
        s = t % 2
        X = xt[s]
        p0 = NMETA + 128 * t
        S.dma("sp", sxt[s], X[:, :], hin[p0:p0 + 128, :], writes=[bxt[s]])
        ms_, tj = (t // 4) % 2, t % 4
        if tj == 0:
            for li in range(NPR):
                S.dma("sp", smt[ms_], mt[ms_][:, li, :, :],
                      pairs[li][0].rearrange("(c p) t -> p c t", p=128)[:, :, p0:p0 + 512],
                      writes=[bmt[ms_]], same_batch=(li > 0))
        for half in range(2):
            pb, bpb = bank[pbi % 4], bbank[pbi % 4]
            pbi += 1
            k = 0
            for li in range(NPR):
                for fc in range(8):
                    S.op("pe", lambda e: e.matmul(pb[:, :], lhsT=mt[ms_][:, li, fc, 128 * tj:128 * tj + 128],
                                                  rhs=WO[li][:, fc, half * 512:(half + 1) * 512],
                                                  start=(k == 0), stop=(k == 8 * NPR - 1)), reads=[bmt[ms_], bWO], writes=[bpb])
                    k += 1
            S.op("dve", lambda e: e.tensor_tensor(out=X[:, half * 512:(half + 1) * 512], in0=pb[:, :],
                                                  in1=X[:, half * 512:(half + 1) * 512], op=ALU.add),
                 reads=[bpb, bxt[s]], writes=[bxt[s]])
        S.op("dve", lambda e: e.tensor_tensor(out=junk[:, :], in0=X[:, :], in1=X[:, :], op=ALU.mult), reads=[bxt[s]], writes=[bjunk])
        S.op("dve", lambda e: e.reduce_sum(out=stat[:, 0:1], in_=junk[:, :], axis=mybir.AxisListType.X), reads=[bjunk], writes=[bstat])
        S.op("dve", lambda e: e.tensor_scalar(out=stat[:, 1:2], in0=stat[:, 0:1], scalar1=1.0 / D, scalar2=EPS,
                                              op0=ALU.mult, op1=ALU.add), reads=[bstat], writes=[bstat])
        S.op("act", lambda e: e.activation(out=stat[:, 3:4], in_=stat[:, 1:2], func=AF.Ln), reads=[bstat], writes=[bstat])
        S.op("act", lambda e: e.activation(out=stat[:, 2:3], in_=stat[:, 3:4], func=AF.Exp, scale=-0.5),
             reads=[bstat], writes=[bstat])
        S.op("dve", lambda e: e.scalar_tensor_tensor(out=ot[s][:, :], in0=X[:, :], scalar=stat[:, 2:3], in1=fgt[:, :],
                                                     op0=ALU.mult, op1=ALU.mult),
             reads=[bxt[s], bstat, bconst, bot[s]], writes=[bot[s]])
        S.dma("pool", sot[s], out[t * 128:(t + 1) * 128, :], ot[s][:, :], reads=[bot[s]])


def build_final():
    NTOK = 2048
    nc = bass.Bass("TRN2", target_bir_lowering=False)
    xq = nc.dram_tensor("xq", [NTOK, D], F32, kind="ExternalInput").ap()
    m1 = nc.dram_tensor("m1", [D, NTOK], BF16, kind="ExternalInput").ap()
    m2 = nc.dram_tensor("m2", [D, NTOK], BF16, kind="ExternalInput").ap()
    wo0 = nc.dram_tensor("wo0", [D, D], F32, kind="ExternalInput").ap()
    wo1 = nc.dram_tensor("wo1", [D, D], F32, kind="ExternalInput").ap()
    fg = nc.dram_tensor("fg", [1, D], F32, kind="ExternalInput").ap()
    out = nc.dram_tensor("out", [NTOK, D], F32, kind="ExternalOutput").ap()
    with ExitStack() as es:
        S = Sched(nc, es)
        sb = lambda name, shape, dt: _sb(nc, es, name, shape, dt)
        WO = [sb(f"WO{i}", [128, 8, D], BF16) for i in range(2)]
        bWO = Buf("WO")
        wst = [sb(f"wst{i}", [128, D], F32) for i in range(2)]
        bwst = [Buf(f"wst{i}") for i in range(2)]
        swst = [S.new_slot(f"wst{i}") for i in range(2)]
        fgt = sb("fgt", [128, D], F32); bconst = Buf("const"); sconst = S.new_slot("const")
        cst = sb("cst", [128, 2], F32)
        xt = [sb(f"xt{i}", [128, D], F32) for i in range(2)]
        bxt = [Buf(f"xt{i}") for i in range(2)]
        sxt = [S.new_slot(f"xt{i}") for i in range(2)]
        mt = [sb(f"mt{i}", [128, 2, 8, 128], BF16) for i in range(2)]
        bmt = [Buf(f"mt{i}") for i in range(2)]
        smt = [S.new_slot(f"mt{i}") for i in range(2)]
        ot = [sb(f"ot{i}", [128, D], F32) for i in range(2)]
        bot = [Buf(f"ot{i}") for i in range(2)]
        sot = [S.new_slot(f"ot{i}") for i in range(2)]
        junk = sb("junk", [128, D], F32); bjunk = Buf("junk")
        stat = sb("stat", [128, 4], F32); bstat = Buf("stat")
        bank = [_ps(nc, es, f"bank{i}", [128, 512], F32) for i in range(4)]
        bbank = [Buf(f"bank{i}") for i in range(4)]

        S.dma("sp", sconst, fgt[:, :], fg[0:1, :].partition_broadcast(128), writes=[bconst])
        S.op("pool", lambda e: e.memset(cst[:, 0:1], -0.5), writes=[bconst])
        wi = 0
        for li, wsrc in enumerate((wo0, wo1)):
            for fc in range(8):
                s = wi % 2; wi += 1
                S.dma("sp", swst[s], wst[s][:, :], wsrc[fc * 128:(fc + 1) * 128, :], writes=[bwst[s]])
                S.op("dve", lambda e, s=s, fc=fc, li=li: e.tensor_copy(out=WO[li][:, fc, :], in_=wst[s][:, :]),
                     reads=[bwst[s]], writes=[bWO])
        pbi = 0
        for t in range(NTOK // 128):
            s = t % 2
            X = xt[s]
            S.dma("sp", sxt[s], X[:, :], xq[t * 128:(t + 1) * 128, :], writes=[bxt[s]])
            S.dma("sp", smt[s], mt[s][:, 0, :, :], m1.rearrange("(c p) t -> p c t", p=128)[:, :, t * 128:(t + 1) * 128],
                  writes=[bmt[s]])
            S.dma("sp", smt[s], mt[s][:, 1, :, :], m2.rearrange("(c p) t -> p c t", p=128)[:, :, t * 128:(t + 1) * 128],
                  writes=[bmt[s]], same_batch=True)
            for half in range(2):
                pb, bpb = bank[pbi % 4], bbank[pbi % 4]
                pbi += 1
                k = 0
                for li in range(2):
                    for fc in range(8):
                        S.op("pe", lambda e, li=li, fc=fc, k=k: e.matmul(pb[:, :], lhsT=mt[s][:, li, fc, :],
                                                                          rhs=WO[li][:, fc, half * 512:(half + 1) * 512],
                                                                          start=(k == 0), stop=(k == 15)),
                             reads=[bmt[s], bWO], writes=[bpb])
                        k += 1
                S.op("dve", lambda e: e.tensor_tensor(out=X[:, half * 512:(half + 1) * 512], in0=pb[:, :],
                                                      in1=X[:, half * 512:(half + 1) * 512], op=ALU.add),
                     reads=[bpb, bxt[s]], writes=[bxt[s]])
            S.op("dve", lambda e: e.tensor_tensor(out=junk[:, :], in0=X[:, :], in1=X[:, :], op=ALU.mult),
                 reads=[bxt[s]], writes=[bjunk])
            S.op("dve", lambda e: e.reduce_sum(out=stat[:, 0:1], in_=junk[:, :], axis=mybir.AxisListType.X),
                 reads=[bjunk], writes=[bstat])
            S.op("dve", lambda e: e.tensor_scalar(out=stat[:, 1:2], in0=stat[:, 0:1], scalar1=1.0 / D, scalar2=EPS,
                                                  op0=ALU.mult, op1=ALU.add), reads=[bstat], writes=[bstat])
            S.op("act", lambda e: e.activation(out=stat[:, 3:4], in_=stat[:, 1:2], func=AF.Ln), reads=[bstat], writes=[bstat])
            S.op("act", lambda e: e.activation(out=stat[:, 2:3], in_=stat[:, 3:4], func=AF.Exp, scale=-0.5),
                 reads=[bstat], writes=[bstat])
            S.op("dve", lambda e: e.scalar_tensor_tensor(out=ot[s][:, :], in0=X[:, :], scalar=stat[:, 2:3], in1=fgt[:, :],
                                                         op0=ALU.mult, op1=ALU.mult),
                 reads=[bxt[s], bstat, bconst, bot[s]], writes=[bot[s]])
            S.dma("pool", sot[s], out[t * 128:(t + 1) * 128, :], ot[s][:, :], reads=[bot[s]])
        S.finish("sp")
    return nc


def _core_cols(g):
    dq = [g * 128 + r for r in range(128)]
    dqs = [g * 128 + (r // 64) * 64 + ((r % 64) + 32) % 64 for r in range(128)]
    dk = [512 + x for x in dq]
    dks = [512 + x for x in dqs]
    dv = [1024 + g * 128 + r for r in range(128)]
    a, b = 2 * g, 2 * g + 1
    fva = [3072 + a * 64 + r for r in range(64)]
    fvb = [3072 + b * 64 + r for r in range(64)]
    dz = [1536 + g * 128 + r for r in range(128)]
    fz = [3584 + a * 64 + r for r in range(128)]
    fqa = [2048 + a * 64 + r for r in range(64)]
    fqb = [2048 + b * 64 + r for r in range(64)]
    fka = [2560 + a * 64 + r for r in range(64)]
    fkb = [2560 + b * 64 + r for r in range(64)]
    fl = [4096 + a, 4096 + b]
    cols = dq + dqs + dk + dks + dv + fva + fvb + dz + fz + fqa + fqb + fka + fkb + fl
    assert len(cols) == NCOL
    return np.array(cols)


def _wout_perm():
    rows = []
    for g in range(4):
        rows += list(range(g * 128, (g + 1) * 128))
        rows += list(range(512 + g * 128, 512 + (g + 1) * 128))
    return np.array(rows)


def _rope_tables():
    half = 32
    inv = (10000.0 ** (-np.arange(half, dtype=np.float32) / half)).astype(np.float32)
    pos = np.arange(L, dtype=np.float32)
    ang = (pos[:, None] * inv[None, :]).astype(np.float32)
    cos = np.cos(ang).astype(np.float32).T
    sin = np.sin(ang).astype(np.float32).T
    cosT = np.concatenate([cos, cos, cos, cos], axis=0)
    sinT = np.concatenate([-sin, sin, -sin, sin], axis=0)
    return np.ascontiguousarray(cosT), np.ascontiguousarray(sinT)


_PROGS = {}
SERIAL_CUT = 0
FUSED = True


def _prog(key, fn):
    if key not in _PROGS:
        _PROGS[key] = fn()
    return _PROGS[key]


def kernel(x, meta_tokens, norm_g, w_in, b_forget, lam_q1, lam_k1, lam_q2, lam_k2, subln_g, w_out, final_g):
    x = np.asarray(x, np.float32)
    f = lambda a: np.asarray(a, np.float32)
    meta_tokens, norm_g, w_in, b_forget = f(meta_tokens), f(norm_g), f(w_in), f(b_forget)
    lam_q1, lam_k1, lam_q2, lam_k2, subln_g, w_out, final_g = map(f, (lam_q1, lam_k1, lam_q2, lam_k2, subln_g, w_out, final_g))
    cores = list(range(8))
    cosT, sinT = _rope_tables()
    ident = np.eye(128, dtype=np.float32).astype(NPB)
    tri = np.triu(np.ones((128, 128), np.float32)).astype(NPB)
    perm = _wout_perm()
    hin = [np.ascontiguousarray(np.concatenate([meta_tokens, x[b]], axis=0)) for b in range(2)]

    def layer_maps(layer, mix_prev):
        maps = []
        for core in cores:
            b, g = core // 4, core % 4
            m = {
                "hin": hin[b],
                "win": np.ascontiguousarray(w_in[layer][:, _core_cols(g)]),
                "normg": np.ascontiguousarray(norm_g[layer].reshape(8, 128).T),
                "bfg": np.ascontiguousarray(b_forget[layer][2 * g:2 * g + 2].reshape(1, 2)),
                "lam4": np.concatenate([lam_q1[layer], lam_k1[layer], lam_q2[layer], lam_k2[layer]]).reshape(1, 256),
                "sublng": np.ascontiguousarray(subln_g[layer][g * 128:(g + 1) * 128].reshape(128, 1)),
                "cosT": cosT, "sinT": sinT, "ident": ident, "tri": tri,
            }
            if mix_prev is not None:
                m["mixp"] = mix_prev[b]
                m["woutp"] = np.ascontiguousarray(w_out[layer - 1][perm])
            maps.append(m)
        return maps

    if FUSED:
        nc = _prog("fused", lambda: build_layer(True, 0.0, fused=True))
        shared = {"cosT": cosT, "sinT": sinT, "ident": ident, "tri": tri,
                  "woutp0": np.ascontiguousarray(w_out[0][perm]), "woutp1": np.ascontiguousarray(w_out[1][perm]),
                  "fg": np.ascontiguousarray(final_g.reshape(1, D))}
        for layer in range(2):
            sfx = str(layer)
            shared["win" + sfx] = np.ascontiguousarray(
                np.concatenate([w_in[layer][:, _core_cols(g)] for g in range(4)], axis=0))
            shared["normg" + sfx] = np.ascontiguousarray(norm_g[layer].reshape(8, 128).T)
            shared["bfg" + sfx] = np.ascontiguousarray(b_forget[layer].reshape(4, 2))
            shared["lam4" + sfx] = np.concatenate([lam_q1[layer], lam_k1[layer], lam_q2[layer], lam_k2[layer]]).reshape(1, 256)
            shared["sublng" + sfx] = np.ascontiguousarray(subln_g[layer].reshape(512, 1))
            shared["wfl" + sfx] = np.ascontiguousarray(w_in[layer][:, 4096:4104])
            shared["bf8" + sfx] = np.ascontiguousarray(b_forget[layer].reshape(8, 1))
        maps = [dict(shared, hin=hin[core // 4]) for core in cores]
        res = run_bass_kernel_spmd(nc, maps, core_ids=cores)
        return np.stack([np.asarray(res.results[0]["out"]), np.asarray(res.results[4]["out"])], axis=0).astype(np.float32)

    mixes = []
    mix_prev = None
    for layer in range(2):
        li = 0.8 - 0.6 * math.exp(-0.3 * layer)
        nc = _prog(("layer", layer), lambda: build_layer(layer > 0, li))
        res = run_bass_kernel_spmd(nc, layer_maps(layer, mix_prev), core_ids=cores)
        outs = [np.asarray(r["mixo"]) for r in res.results]
        mix_prev = [np.ascontiguousarray(np.concatenate(outs[4 * b:4 * b + 4], axis=0)) for b in range(2)]
        mixes.append(mix_prev)

    ncf = _prog("final", build_final)
    maps = []
    for core in cores:
        b, q = core // 4, core % 4
        t0 = NMETA + 2048 * q
        maps.append({
            "xq": np.ascontiguousarray(x[b, 2048 * q:2048 * (q + 1)]),
            "m1": np.ascontiguousarray(mixes[0][b][:, t0:t0 + 2048]),
            "m2": np.ascontiguousarray(mixes[1][b][:, t0:t0 + 2048]),
            "wo0": np.ascontiguousarray(w_out[0][perm]),
            "wo1": np.ascontiguousarray(w_out[1][perm]),
            "fg": final_g.reshape(1, D),
        })
    res = run_bass_kernel_spmd(ncf, maps, core_ids=cores)
    out = np.empty((2, SEQ, D), np.float32)
    for core in cores:
        b, q = core // 4, core % 4
        out[b, 2048 * q:2048 * (q + 1)] = np.asarray(res.results[core]["out"])
    return out
```
